# Optimizing a Trainium2 kernel written in Bass

```python
import jax, jax.numpy as jnp
from jax import lax
import numpy as np

D_MODEL = 2048
BATCH = 4
SEQ = 8192
DEPTH = 1

MEM_LEN = 256
D_CONV = 2048
CONV_WIDTH = 31
ML_HEADS = 4
D_ML = 2048
ML_HEAD_DIM = D_ML // ML_HEADS
ML_CHUNK = 64
QK_CONV_WIDTH = 4
XA_HEADS = 4
D_XA = 2048
XA_HEAD_DIM = D_XA // XA_HEADS
N_BRANCH = 3
EPS = 1e-6

IN_GROUPS = (
    ("glu_a", D_CONV), ("glu_b", D_CONV), ("z_conv", D_CONV),
    ("qk_ml", 2 * D_ML), ("v_ml", D_ML), ("o_ml", D_ML), ("z_ml", D_ML),
    ("if_ml", 2 * ML_HEADS),
    ("q_xa", D_XA), ("z_xa", D_XA),
    ("gates", N_BRANCH * D_MODEL),
)
N_IN = 3 * D_CONV + 5 * D_ML + 2 * ML_HEADS + 2 * D_XA + N_BRANCH * D_MODEL

kernel_name = "hybrid_conformer_mlstm_memxattn_gated"


def rms_norm(x, g):
    xf = x.astype(jnp.float32)
    y = xf * lax.rsqrt(jnp.mean(xf * xf, axis=-1, keepdims=True) + EPS)
    return (y * g.astype(jnp.float32)).astype(x.dtype)


def layer_norm(x, g, b):
    xf = x.astype(jnp.float32)
    mu = jnp.mean(xf, axis=-1, keepdims=True)
    var = jnp.mean(jnp.square(xf - mu), axis=-1, keepdims=True)
    y = (xf - mu) * lax.rsqrt(var + EPS)
    return (y * g.astype(jnp.float32) + b.astype(jnp.float32)).astype(x.dtype)


def causal_depthwise_conv(x, w):
    width, ch = w.shape
    return lax.conv_general_dilated(
        x, w[:, None, :].astype(x.dtype), window_strides=(1,),
        padding=((width - 1, 0),), dimension_numbers=("NWC", "WIO", "NWC"),
        feature_group_count=ch)


def in_cols(h, w_in, name):
    start = 0
    for n, width in IN_GROUPS:
        if n == name:
            return h @ w_in[:, start:start + width]
        start += width
    raise ValueError(name)


def mlstm_chunkwise(q, k, v, log_i, log_f):
    B, H, S, Dh = q.shape
    L = ML_CHUNK
    nc = S // L

    def to_chunks(a):
        return jnp.moveaxis(a.reshape(B, H, nc, L, *a.shape[3:]), 2, 0)

    xs = (to_chunks(q), to_chunks(k), to_chunks(v), to_chunks(log_i), to_chunks(log_f))
    causal = jnp.tril(jnp.ones((L, L), dtype=bool))

    def step(carry, chunk):
        C, n, m = carry
        qj, kj, vj, li, lf = chunk
        b = jnp.cumsum(lf, axis=-1)
        d = b[..., :, None] - b[..., None, :] + li[..., None, :]
        d = jnp.where(causal, d, -jnp.inf)
        inter = b + m[..., None]
        m_row = jnp.maximum(inter, jnp.max(d, axis=-1))
        w_intra = jnp.exp(d - m_row[..., None])
        w_inter = jnp.exp(inter - m_row)
        s = jnp.einsum("bhid,bhjd->bhij", qj, kj) * w_intra
        num = (jnp.einsum("bhij,bhje->bhie", s, vj)
               + w_inter[..., None] * jnp.einsum("bhid,bhde->bhie", qj, C))
        den = jnp.sum(s, axis=-1) + w_inter * jnp.einsum("bhid,bhd->bhi", qj, n)
        h = num / jnp.maximum(jnp.abs(den), jnp.exp(-m_row))[..., None]
        b_last = b[..., -1]
        g = b_last[..., None] - b + li
        m_new = jnp.maximum(b_last + m, jnp.max(g, axis=-1))
        decay = jnp.exp(b_last + m - m_new)
        wk = jnp.exp(g - m_new[..., None])
        C = decay[..., None, None] * C + jnp.einsum("bhj,bhjd,bhje->bhde", wk, kj, vj)
        n = decay[..., None] * n + jnp.einsum("bhj,bhjd->bhd", wk, kj)
        return (C, n, m_new), h

    init = (jnp.zeros((B, H, Dh, Dh), jnp.float32),
            jnp.zeros((B, H, Dh), jnp.float32),
            jnp.zeros((B, H), jnp.float32))
    _, hs = lax.scan(step, init, xs)
    return jnp.moveaxis(hs, 0, 2).reshape(B, H, S, Dh)


def setup_inputs(seed: int = 0) -> dict:
    key = jax.random.key(seed)
    ks = jax.random.split(key, 24)
    f32 = jnp.float32

    def nrm(k, shape, scale):
        return jax.random.normal(k, shape, f32) * scale

    def gain(k, n):
        return 1.0 + 0.02 * jax.random.normal(k, (n,), f32)

    b_i = 0.1 * jax.random.normal(ks[3], (ML_HEADS,), f32)
    b_f = jnp.linspace(3.0, 6.0, ML_HEADS, dtype=f32) + 0.1 * jax.random.normal(ks[4], (ML_HEADS,), f32)
    return {
        "x": nrm(ks[0], (BATCH, SEQ, D_MODEL), 1.0),
        "mem": nrm(ks[1], (BATCH, MEM_LEN, D_MODEL), 1.0),
        "g_pre": gain(ks[2], D_MODEL),
        "w_in": nrm(ks[5], (D_MODEL, N_IN), D_MODEL ** -0.5),
        "b_if": jnp.concatenate([b_i, b_f]),
        "w_qk_conv": nrm(ks[6], (QK_CONV_WIDTH, 2 * D_ML), QK_CONV_WIDTH ** -0.5),
        "w_dw": nrm(ks[7], (CONV_WIDTH, D_CONV), CONV_WIDTH ** -0.5),
        "b_dw": nrm(ks[8], (D_CONV,), 0.02),
        "g_ln": gain(ks[9], D_CONV),
        "b_ln": nrm(ks[10], (D_CONV,), 0.02),
        "w_conv_out": nrm(ks[11], (D_CONV, D_MODEL), D_CONV ** -0.5),
        "g_ml_head": gain(ks[12], D_ML),
        "w_ml_out": nrm(ks[13], (D_ML, D_MODEL), D_ML ** -0.5),
        "g_mem": gain(ks[14], D_MODEL),
        "w_mem_kv": nrm(ks[15], (D_MODEL, 2 * D_XA), D_MODEL ** -0.5),
        "w_xa_out": nrm(ks[16], (D_XA, D_MODEL), D_XA ** -0.5),
        "w_out": nrm(ks[17], (D_MODEL, D_MODEL), D_MODEL ** -0.5),
        "g_post": gain(ks[18], D_MODEL),
    }


def reference(x, mem, g_pre, w_in, b_if, w_qk_conv, w_dw, b_dw, g_ln, b_ln,
              w_conv_out, g_ml_head, w_ml_out, g_mem, w_mem_kv, w_xa_out,
              w_out, g_post):
    B, S, _ = x.shape
    f32 = jnp.float32
    for _layer in range(DEPTH):
        h = rms_norm(x, g_pre)

        u = in_cols(h, w_in, "glu_a") * jax.nn.sigmoid(in_cols(h, w_in, "glu_b"))
        u = causal_depthwise_conv(u, w_dw) + b_dw.astype(u.dtype)
        u = jax.nn.silu(layer_norm(u, g_ln, b_ln))
        y_conv = (u * jax.nn.silu(in_cols(h, w_in, "z_conv"))) @ w_conv_out

        qk = jax.nn.silu(causal_depthwise_conv(in_cols(h, w_in, "qk_ml"), w_qk_conv))
        q_ml, k_ml = jnp.split(qk, 2, axis=-1)
        v_ml = in_cols(h, w_in, "v_ml")

        def heads(t):
            return t.reshape(B, S, ML_HEADS, ML_HEAD_DIM).transpose(0, 2, 1, 3).astype(f32)

        gif = in_cols(h, w_in, "if_ml").astype(f32) + b_if.astype(f32)
        log_i = gif[..., :ML_HEADS].transpose(0, 2, 1)
        log_f = jax.nn.log_sigmoid(gif[..., ML_HEADS:]).transpose(0, 2, 1)
        hm = mlstm_chunkwise(heads(q_ml), heads(k_ml) * (ML_HEAD_DIM ** -0.5),
                             heads(v_ml), log_i, log_f)
        hm = hm.transpose(0, 2, 1, 3).reshape(B, S, D_ML)
        hm = jax.nn.sigmoid(in_cols(h, w_in, "o_ml").astype(f32)) * hm
        hm = hm.reshape(B, S, ML_HEADS, ML_HEAD_DIM)
        hm = hm * lax.rsqrt(jnp.mean(hm * hm, axis=-1, keepdims=True) + EPS)
        hm = (hm * g_ml_head.astype(f32).reshape(ML_HEADS, ML_HEAD_DIM)).reshape(B, S, D_ML)
        hm = hm.astype(x.dtype)
        y_ml = (hm * jax.nn.silu(in_cols(h, w_in, "z_ml"))) @ w_ml_out

        kv = rms_norm(mem, g_mem) @ w_mem_kv
        k_m, v_m = jnp.split(kv, 2, axis=-1)
        k_m = k_m.reshape(B, -1, XA_HEADS, XA_HEAD_DIM)
        v_m = v_m.reshape(B, -1, XA_HEADS, XA_HEAD_DIM)
        q_x = in_cols(h, w_in, "q_xa").reshape(B, S, XA_HEADS, XA_HEAD_DIM)
        scores = jnp.einsum("bshd,bmhd->bhsm", q_x, k_m).astype(f32) * (XA_HEAD_DIM ** -0.5)
        p = jax.nn.softmax(scores, axis=-1).astype(x.dtype)
        o_x = jnp.einsum("bhsm,bmhd->bshd", p, v_m).reshape(B, S, D_XA)
        y_xa = (o_x * jax.nn.silu(in_cols(h, w_in, "z_xa"))) @ w_xa_out

        g_c, g_m, g_x = jnp.split(jax.nn.sigmoid(in_cols(h, w_in, "gates")), N_BRANCH, axis=-1)
        merged = g_c * y_conv + g_m * y_ml.astype(x.dtype) + g_x * y_xa
        x = x + rms_norm(merged @ w_out, g_post)
    return x
```

```python
import contextlib
import numpy as np
import concourse.bass as bass
import concourse.mybir as mybir
from concourse.alu_op_type import AluOpType as ALU
from concourse.bass_utils import run_bass_kernel_spmd

F32 = mybir.dt.float32
BF16 = mybir.dt.bfloat16
AF = mybir.ActivationFunctionType
P = 128
D = 2048
KC = 16
NT = 512
EPS = 1e-6
BIG = 1.0e4
KSCALE_LN = float(np.log(512.0 ** -0.5))
XSCALE = float(512.0 ** -0.5)

OFF = {'glu_a': 0, 'glu_b': 2048, 'z_conv': 4096, 'q': 6144, 'k': 8192, 'v': 10240, 'o': 12288,
       'z_ml': 14336, 'if': 16384, 'q_xa': 16392, 'z_xa': 18440, 'g_c': 20488, 'g_m': 22536, 'g_x': 24584}
GROUPS = ['k', 'v', 'mk', 'mv', 'glu_b', 'glu_a', 'q', 'o', 'z_conv', 'conv_out', 'g_c', 'z_ml', 'ml_out',
          'g_m', 'q_xa', 'z_xa', 'xa_out', 'g_x', 'w_out']
GSRC = {'mk': ('w_mem_kv', 0), 'mv': ('w_mem_kv', 2048), 'conv_out': ('w_conv_out', 0),
        'ml_out': ('w_ml_out', 0), 'xa_out': ('w_xa_out', 0), 'w_out': ('w_out', 0)}
for _g in GROUPS:
    if _g not in GSRC:
        GSRC[_g] = ('w_in', OFF[_g])
GIDX = {g: i for i, g in enumerate(GROUPS)}
NSLOT = 3
VG, VB, VGL, VBL, VGM, VGP, VGMEM = 0, 1, 2, 3, 4, 5, 6


class Buf:
    def __init__(self):
        self.w = {}
        self.r = {}

    def rdeps(self):
        return list(self.w.items())

    def wdeps(self):
        d = dict(self.r)
        for k, v in self.w.items():
            d[k] = max(d.get(k, 0), v)
        return list(d.items())

    def read(self, tok):
        if tok is not None:
            self.r[tok[0]] = max(self.r.get(tok[0], 0), tok[1])

    def wrote(self, tok):
        if tok is not None:
            self.w[tok[0]] = max(self.w.get(tok[0], 0), tok[1])


class Prog:
    ENG = ('pe', 'act', 'dve', 'pool', 'sp')

    def __init__(self, dry):
        self.dry = dry
        self.q = {e: [] for e in self.ENG}
        self.cnt = {e: 0 for e in self.ENG}
        self.dcnt = {}

    def op(self, e, fn, deps=(), sig=True):
        tok = None
        if sig:
            self.cnt[e] += 1
            tok = (e, self.cnt[e])
        if not self.dry:
            self.q[e].append((fn, tuple(d for d in deps if d is not None), ('eng',) if sig else ('none',)))
        return tok

    def dma(self, qe, sem, fn, deps=()):
        self.dcnt[sem] = self.dcnt.get(sem, 0) + 16
        tok = (sem, self.dcnt[sem])
        if not self.dry:
            self.q[qe].append((fn, tuple(d for d in deps if d is not None), ('dma', sem)))
        return tok


def build_program(n_pre, n_main, wseq=None, debug=False):
    dry = wseq is None
    T_pre, T_main = n_pre * NT, n_main * NT
    pg = Prog(dry)
    nc = bass.Bass("TRN2", target_bir_lowering=False)
    es = contextlib.ExitStack()

    def din(name, shape, dt=F32):
        return nc.dram_tensor(name, list(shape), dt, kind="ExternalInput").ap()

    xT = din("xT", [D, T_main])
    xTp = din("xTp", [D, T_pre])
    memT = din("memT", [D, 256])
    wsrc = {'w_in': din("w_in", [D, 26632]), 'w_mem_kv': din("w_mem_kv", [D, 4096]),
            'w_conv_out': din("w_conv_out", [D, D]), 'w_ml_out': din("w_ml_out", [D, D]),
            'w_xa_out': din("w_xa_out", [D, D]), 'w_out': din("w_out", [D, D])}
    d_cst = din("cst", [P, 384])
    d_csm = din("csm", [4, 516])
    d_vecs = din("vecs", [P, 7 * 16])
    d_wdw = din("wdw", [P, 16 * 31])
    d_wqk = din("wqk", [P, 32 * 4])
    d_bif = din("bif", [4, 2])
    d_pm = din("pm", [P, 2])
    d_wif = din("wif", [P, 16 * 8])
    outT = nc.dram_tensor("outT", [D, T_main], F32, kind="ExternalOutput").ap()
    WB = nc.dram_tensor("wbscr", [len(GROUPS) * 8, P, KC * 256], BF16, kind="Internal").ap()

    def sb(name, shape, dt):
        return es.enter_context(nc.sbuf_tensor("sb_" + name, list(shape), dt))

    def psb(name):
        return es.enter_context(nc.psum_tensor("ps_" + name, [P, 512], F32))

    cst = sb("cst", [P, 384], F32)
    csm = sb("csm", [4, 516], F32)
    ident_bf = sb("identb", [P, P], BF16)
    ones_bf = sb("onesb", [P, P], BF16)
    maskf = cst[:, 128:256]
    ones4 = sb("ones4", [4, 512], F32)
    vecs = sb("vecs", [P, 7, 16], F32)
    wdw = sb("wdw", [P, 16, 31], F32)
    wqk = sb("wqk", [P, 32, 4], F32)
    bif = sb("bif", [4, 2], F32)
    negbf = sb("negbf", [4, 1], F32)
    pm = sb("pm", [P, 2], F32)
    wif_f = sb("wif_f", [P, 16, 8], F32)
    wif = sb("wif", [P, 16, 8], BF16)
    hT = sb("hT", [P, 16, 512], BF16)
    wbuf = [sb(f"wbuf{i}", [P, 16, 256], BF16) for i in range(NSLOT)]
    R1 = sb("R1", [P, 16864], BF16)
    uext = R1[:, 0:8672].rearrange("p (c t) -> p c t", t=542)
    Pbuf = R1[:, 8672:16864].rearrange("p (c t) -> p c t", t=512)
    o_out = R1[:, 0:16384].bitcast(F32).rearrange("p (c t) -> p c t", t=512)
    R2 = sb("R2", [P, 10240], BF16)
    qT = R2[:, 0:2048].rearrange("p (c t) -> p c t", t=512)
    kT = R2[:, 2048:4096].rearrange("p (c t) -> p c t", t=512)
    ktok = R2[:, 4096:6144].rearrange("p (c t) -> p c t", t=512)
    vtok = R2[:, 6144:8192].rearrange("p (c t) -> p c t", t=512)
    oT = R2[:, 8192:10240].rearrange("p (c t) -> p c t", t=512)
    merged = R2[:, 0:8192].rearrange("p (c t) -> p c t", t=512)
    qx = oT
    Cm = sb("Cm", [P, 16, 512], F32)
    Cd = sb("Cd", [P, 4, 512], BF16)
    nst = sb("nst", [P, 16], F32)
    nd = sb("nd", [P, 4], F32)
    nrep = sb("nrep", [P, 4, 128], BF16)
    kmT = sb("kmT", [P, 16, 256], BF16)
    vm = sb("vm", [P, 2, 2048], BF16)
    uhalo = sb("uhalo", [P, 16, 30], BF16)
    qkhalo = sb("qkhalo", [P, 32, 4], BF16)
    xs = [sb(f"xs{i}", [P, 512], F32) for i in range(2)]
    sqb = [sb(f"sqb{i}", [P, 512], BF16) for i in range(2)]
    rstdx = sb("mf", [P, 512], F32)
    Fbc = rstdx
    rdx = rstdx
    sg = [sb(f"sg{i}", [P, 512], F32) for i in range(2)]
    tmpf = [sb(f"tmp{i}", [P, 512], F32) for i in range(3)]
    acc = tmpf
    t1 = tmpf
    qpre = [sb(f"qpre{i}", [P, 516], BF16) for i in range(2)]
    lnr = sb("lnr", [P, 512], F32)
    lnb = sb("lnb", [P, 512], F32)
    zs = [sb(f"zs{i}", [P, 512], BF16) for i in range(2)]
    yb = [sb(f"yb{i}", [P, 512], BF16) for i in range(2)]
    Sm = [sb(f"Sm{i}", [P, 128], BF16) for i in range(2)]
    dmt = sb("dmt", [P, 128], F32)
    rden = sb("rden", [P, 128], F32)
    prod = sb("prod", [P, 4, 128], F32)
    ho = sb("ho", [P, 4, 128], F32)
    hsq = sb("hsq", [P, 4, 128], BF16)
    rsh = sb("rsh", [P, 128], F32)
    vw = [sb(f"vw{i}", [P, 512], BF16) for i in range(2)]
    wkf = sb("wkf", [P, 16], F32)
    wkb = sb("wkb", [P, 16], BF16)
    decbc = sb("decbc", [P, 16], F32)
    g_li = sb("g_li", [4, 512], F32)
    g_e = sb("g_e", [4, 512], F32)
    g_L = sb("g_L", [4, 512], F32)
    g_G = sb("g_G", [4, 512], F32)
    g_wk = g_e
    g_F = g_G
    g_sm = sb("g_sm", [4, 32], F32)
    pT = sb("pT", [P, 2, 512], BF16)
    memx = sb("memx", [P, 16, 256], F32) if False else None

    banks = {n: psb(n) for n in ['m0', 'm1', 'm2', 'sa', 'sb', 'sm', 'hh', 'dc']}
    bbuf = {n: Buf() for n in banks}

    semnames = ['pe', 'act', 'dve', 'pool', 'cst', 'xs0', 'xs1', 'st0', 'st1', 'st2'] + \
               [f'w{i}' for i in range(NSLOT)] + [f'cg{i}' for i in range(len(GROUPS))]
    sems = {n: es.enter_context(nc.semaphore(n)) for n in semnames}

    def mm(out_ap, pairs, deps=(), sig=True):
        n = len(pairs)
        tok = None
        for i, (l, r) in enumerate(pairs):
            tok = pg.op('pe', (lambda e, l=l, r=r, i=i: e.matmul(out_ap, l, r, start=(i == 0), stop=(i == n - 1))),
                        deps if i == 0 else (), sig=(sig and i == n - 1))
        return tok

    def act(out, in_, func, deps=(), bias=None, scale=None):
        kw = {}
        if bias is not None:
            kw['bias'] = bias
        if scale is not None:
            kw['scale'] = scale
        return pg.op('act', lambda e: e.activation(out=out, in_=in_, func=func, **kw), deps)

    def tt(eng, out, a, b, op, deps=()):
        return pg.op(eng, lambda e: e.tensor_tensor(out=out, in0=a, in1=b, op=op), deps)

    def ts(eng, out, a, s1, s2, op0, op1, deps=()):
        if op1 is None:
            return pg.op(eng, lambda e: e.tensor_scalar(out=out, in0=a, scalar1=s1, scalar2=None, op0=op0), deps)
        return pg.op(eng, lambda e: e.tensor_scalar(out=out, in0=a, scalar1=s1, scalar2=s2, op0=op0, op1=op1), deps)

    def stt(out, in0, scalar, in1, op0, op1, deps=()):
        return pg.op('dve', lambda e: e.scalar_tensor_tensor(out=out, in0=in0, scalar=scalar, in1=in1, op0=op0, op1=op1), deps)

    def cp(eng, out, in_, deps=()):
        return pg.op(eng, lambda e: e.tensor_copy(out=out, in_=in_), deps)

    def recip(out, in_, deps=()):
        return pg.op('dve', lambda e: e.reciprocal(out=out, in_=in_), deps)

    def scan(out, d0, d1, init, op0, op1, deps=()):
        return pg.op('dve', lambda e: e.tensor_tensor_scan(out=out, data0=d0, data1=d1, initial=init, op0=op0, op1=op1), deps)

    def memset(eng, ap, val, deps=()):
        return pg.op(eng, lambda e: e.memset(ap, val), deps)

    mm_names = ['m0', 'm1', 'm2']
    mm_i = [0]

    def take_mm():
        n = mm_names[mm_i[0] % 3]
        mm_i[0] += 1
        return n

    wrec = []
    wstate = {'next_load': 0, 'next_use': 0}
    slotbuf = [Buf() for _ in range(NSLOT)]
    released = {}
    castbuf = {g: Buf() for g in GROUPS}

    def WBUF(i):
        return wbuf[i % NSLOT]

    def w_try_issue():
        if dry:
            return
        while wstate['next_load'] < len(wseq):
            i = wstate['next_load']
            if i >= NSLOT and not released.get(i - NSLOT, False):
                return
            s = i % NSLOT
            g, j = wseq[i]
            src = WB[GIDX[g] * 8 + j].rearrange("p (k n) -> p k n", n=256)
            deps = slotbuf[s].wdeps() + castbuf[g].rdeps()
            assert castbuf[g].w, ("cast not issued before load", g)
            tok = pg.dma('sp', f'w{s}', (lambda e, s=s, src=src: e.dma_start(out=wbuf[s][:], in_=src)), deps)
            slotbuf[s].wrote(tok)
            wstate['next_load'] = i + 1

    def w_get(g, j):
        i = wstate['next_use']
        wstate['next_use'] = i + 1
        if dry:
            wrec.append((g, j))
            return i, []
        assert wseq[i] == (g, j), (i, wseq[i], g, j)
        w_try_issue()
        assert wstate['next_load'] > i, ("weight block not loadable (slot starvation)", i, g, j)
        return i, slotbuf[i % NSLOT].rdeps()

    def w_release(i, tok):
        if dry:
            return
        slotbuf[i % NSLOT].read(tok)
        released[i] = True
        w_try_issue()

    B_cst = Buf()
    loads = [(cst, d_cst), (csm, d_csm), (vecs, d_vecs.rearrange("p (a b) -> p a b", b=16)),
             (wdw, d_wdw.rearrange("p (a b) -> p a b", b=31)), (wqk, d_wqk.rearrange("p (a b) -> p a b", b=4)),
             (bif, d_bif), (pm, d_pm), (wif_f, d_wif.rearrange("p (a b) -> p a b", b=8))]
    for dst, src in loads:
        tok = pg.dma('sp', 'cst', (lambda e, dst=dst, src=src: e.dma_start(out=dst[:], in_=src)))
    B_cst.wrote(tok)
    cdep = B_cst.rdeps()
    tok = cp('dve', ident_bf[:], cst[:, 0:128], cdep)
    tok = cp('dve', ones_bf[:], cst[:, 256:384], cdep)
    tok = cp('dve', wif[:], wif_f[:], cdep)
    tok = ts('dve', negbf[:], bif[:, 1:2], -1.0, None, ALU.mult, None, cdep)
    tok = memset('dve', ones4[:], 1.0)
    tok = memset('dve', g_sm[:], 0.0)
    tok = memset('dve', uhalo[:], 0.0)
    tok = memset('dve', qkhalo[:], 0.0)
    tok = memset('dve', nst[:], 0.0)
    tok = memset('dve', Cm[:], 0.0)
    B_setup = Buf()
    B_setup.wrote(tok)
    sdep = B_setup.rdeps()
    sel = lambda h: csm[:, h * 128:(h + 1) * 128]
    i4 = csm[:, 512:516]

    def cast_group(g):
        sname, c0 = GSRC[g]
        src_t = wsrc[sname]
        for j in range(8):
            src = src_t[:, c0 + 256 * j: c0 + 256 * (j + 1)].rearrange("(k p) n -> p k n", p=P)
            dst = WB[GIDX[g] * 8 + j].rearrange("p (k n) -> p k n", n=256)
            tok = pg.dma('pool', f'cg{GIDX[g]}', (lambda e, dst=dst, src=src: e.dma_start(out=dst, in_=src)))
        castbuf[g].wrote(tok)

    cast_next = [0]

    def cast_some(n):
        for _ in range(n):
            if cast_next[0] < len(GROUPS):
                cast_group(GROUPS[cast_next[0]])
                cast_next[0] += 1

    B = {n: Buf() for n in ['hT', 'uext', 'Pbuf', 'qT', 'kT', 'ktok', 'vtok', 'oT', 'merged', 'Cm', 'Cd', 'nst', 'nd',
                            'nrep', 'kmT', 'vm', 'uhalo', 'qkhalo', 'rstdx', 'lnm', 'lnr', 'lnb', 'Fbc', 'dmt', 'rden',
                            'prod', 'ho', 'hsq', 'rsh', 'wkf', 'wkb', 'decbc', 'g_li', 'g_e', 'g_L', 'g_G', 'g_wk',
                            'g_F', 'g_sm', 'qx', 'pT', 'rdx', 'o_out']}
    for n, k in [('xs', 2), ('sqb', 2), ('sg', 2), ('qpre', 2), ('zs', 2), ('yb', 2),
                 ('Sm', 2), ('vw', 2)]:
        for i in range(k):
            B[f'{n}{i}'] = Buf()
    B['Fbc'] = B['rstdx']
    B['rdx'] = B['rstdx']
    B['g_wk'] = B['g_e']
    B['g_F'] = B['g_G']
    B['qx'] = B['oT']
    for i in range(3):
        B[f'tmp{i}'] = Buf()
    rot = {}

    def nxt(name, k):
        i = rot.get(name, 0)
        rot[name] = i + 1
        return i % k

    def x_loader(src_dram, t0, width=NT):
        state = {'issued': 0}

        def issue(kc_list):
            i = state['issued']
            if i >= len(kc_list):
                return
            s = nxt('xs', 2)
            kc = kc_list[i]
            src = src_dram[kc * P:(kc + 1) * P, t0:t0 + width]
            bname = f'xs{s}'
            tok = pg.dma('sp', bname, (lambda e, s=s, src=src: e.dma_start(out=xs[s][:, 0:width], in_=src)), B[bname].wdeps())
            B[bname].wrote(tok)
            state.setdefault('slots', []).append(s)
            state['issued'] = i + 1
        return state, issue

    def rms_tile(src_dram, t0, gvec, dst, dstB, width=NT):
        kcs = list(range(16)) + list(range(16))
        st, issue = x_loader(src_dram, t0, width)
        for _ in range(2):
            issue(kcs)
        for i in range(16):
            s = st['slots'][i]
            q = nxt('sqb', 2)
            tok = act(sqb[q][:, 0:width], xs[s][:, 0:width], AF.Square, B[f'xs{s}'].rdeps() + B[f'sqb{q}'].wdeps())
            B[f'xs{s}'].read(tok)
            B[f'sqb{q}'].wrote(tok)
            deps = B[f'sqb{q}'].rdeps() + (bbuf['sa'].wdeps() if i == 0 else [])
            tok = pg.op('pe', (lambda e, q=q, i=i: e.matmul(banks['sa'][:, 0:width], ones_bf[:], sqb[q][:, 0:width],
                                                             start=(i == 0), stop=(i == 15))), deps)
            B[f'sqb{q}'].read(tok)
            issue(kcs)
        bbuf['sa'].wrote(tok)
        tok = act(rstdx[:, 0:width], banks['sa'][:, 0:width], AF.Sqrt, bbuf['sa'].rdeps() + B['rstdx'].wdeps(),
                  bias=EPS, scale=1.0 / D)
        bbuf['sa'].read(tok)
        tok = recip(rstdx[:, 0:width], rstdx[:, 0:width], [tok])
        B['rstdx'].wrote(tok)
        for i in range(16):
            s = st['slots'][16 + i]
            deps = B[f'xs{s}'].rdeps() + B['rstdx'].rdeps() + (dstB.wdeps() if i == 0 else [])
            tok = stt(dst[:, i, 0:width], xs[s][:, 0:width], vecs[:, gvec, i:i + 1], rstdx[:, 0:width], ALU.mult, ALU.mult, deps)
            B[f'xs{s}'].read(tok)
            issue(kcs)
        B['rstdx'].read(tok)
        dstB.wrote(tok)

    def gates_tile(prefix):
        gi = banks['sm'][0:4, 0:512]
        gf = banks['sb'][0:4, 0:512]
        tok = mm(gi, [(wif[:, kc, 0:4], hT[:, kc, :]) for kc in range(16)], B['hT'].rdeps() + bbuf['sm'].wdeps() + sdep)
        bbuf['sm'].wrote(tok)
        tok = mm(gf, [(wif[:, kc, 4:8], hT[:, kc, :]) for kc in range(16)], bbuf['sb'].wdeps())
        bbuf['sb'].wrote(tok)
        B['hT'].read(tok)
        tok = act(g_li[:], gi, AF.Identity, bbuf['sm'].rdeps() + B['g_li'].wdeps(), bias=bif[:, 0:1])
        bbuf['sm'].read(tok)
        B['g_li'].wrote(tok)
        tok = act(g_e[:], gf, AF.Exp, bbuf['sb'].rdeps() + B['g_e'].wdeps(), bias=negbf[:, 0:1], scale=-1.0)
        bbuf['sb'].read(tok)
        tok = act(g_e[:], g_e[:], AF.Ln, [tok], bias=1.0)
        B['g_e'].wrote(tok)
        if prefix:
            tok = ts('dve', g_li[:], g_li[:], pm[0:4, 0:1], pm[0:4, 1:2], ALU.mult, ALU.add, B['g_li'].rdeps())
            B['g_li'].wrote(tok)
            tok = ts('dve', g_e[:], g_e[:], pm[0:4, 0:1], None, ALU.mult, None, B['g_e'].rdeps())
            B['g_e'].wrote(tok)
        tok = cp('dve', g_sm[:, 12:13], g_sm[:, 1:2], B['g_sm'].wdeps())
        tok = scan(g_L[:], ones4[:], g_e[:], g_sm[:, 0:1], ALU.mult, ALU.add, B['g_e'].rdeps() + B['g_L'].wdeps())
        B['g_e'].read(tok)
        tok = cp('dve', g_sm[:, 0:1], g_L[:, 511:512])
        tok = tt('dve', g_li[:], g_li[:], g_L[:], ALU.add, B['g_li'].wdeps())
        tok = scan(g_G[:], g_li[:], g_li[:], g_sm[:, 1:2], ALU.max, ALU.max, B['g_G'].wdeps())
        tok = cp('dve', g_sm[:, 1:2], g_G[:, 511:512])
        gl = g_G[:].rearrange("p (c t) -> p c t", t=128)[:, :, 127]
        tok = ts('dve', g_sm[:, 4:8], gl, -1.0, None, ALU.mult, None)
        tok = ts('dve', g_sm[:, 8:12], g_sm[:, 4:8], KSCALE_LN, None, ALU.add, None)
        tok = cp('dve', g_sm[:, 13:16], gl[:, 0:3])
        tok = tt('dve', g_sm[:, 16:20], g_sm[:, 12:16], g_sm[:, 4:8], ALU.add)
        B['g_sm'].wrote(tok)
        B['g_li'].wrote(tok)
        B['g_L'].wrote(tok)
        B['g_G'].wrote(tok)
        dv = [tok]
        tok = act(g_sm[:, 20:24], g_sm[:, 16:20], AF.Exp, dv)
        for c in range(4):
            cs = slice(c * 128, (c + 1) * 128)
            tok = act(g_wk[:, cs], g_li[:, cs], AF.Exp, dv + (B['g_wk'].wdeps() if c == 0 else []), bias=g_sm[:, 8 + c:9 + c])
            tok = act(g_F[:, cs], g_L[:, cs], AF.Exp, dv + (B['g_F'].wdeps() if c == 0 else []), bias=g_sm[:, 4 + c:5 + c])
        B['g_wk'].wrote(tok)
        B['g_F'].wrote(tok)
        B['g_sm'].read(tok)
        B['g_sm'].wrote(tok)
        B['g_li'].read(tok)
        B['g_L'].read(tok)

    def gates_pe():
        smb = banks['sm']
        deps = B['g_wk'].rdeps() + bbuf['sm'].wdeps()
        for c in range(4):
            tok = mm(smb[:, 384 + 4 * c:388 + 4 * c], [(g_wk[:, c * 128:(c + 1) * 128], i4)], deps if c == 0 else ())
        for h in range(4):
            tok = mm(smb[:, 400 + 4 * h:404 + 4 * h], [(sel(h), g_sm[:, 20:24])], B['g_sm'].rdeps() if h == 0 else ())
        bbuf['sm'].wrote(tok)
        B['g_wk'].read(tok)
        B['g_sm'].read(tok)
        d = bbuf['sm'].rdeps()
        tok = cp('dve', wkf[:], smb[:, 384:400], d + B['wkf'].wdeps())
        B['wkf'].wrote(tok)
        tok = cp('dve', wkb[:], smb[:, 384:400], d + B['wkb'].wdeps())
        B['wkb'].wrote(tok)
        tok = cp('dve', decbc[:], smb[:, 400:416], d + B['decbc'].wdeps())
        B['decbc'].wrote(tok)
        bbuf['sm'].read(tok)

    def proj_chunk(slot, cc, rhs_fn, first_deps):
        bn = take_mm()
        tok = mm(banks[bn][:], [(WBUF(slot)[:, kc, cc * 128:(cc + 1) * 128], rhs_fn(kc)) for kc in range(16)],
                 first_deps + bbuf[bn].wdeps())
        bbuf[bn].wrote(tok)
        return bn, tok

    hT_rhs = lambda kc: hT[:, kc, :]

    def phase_a(do_conv, mask_halo):
        for c in range(16):
            tok = cp('pool', uext[:, c, 0:30], uhalo[:, c, :], (B['uext'].wdeps() + B['uhalo'].rdeps() + B['o_out'].wdeps()) if c == 0 else ())
        B['uext'].wrote(tok)
        B['uhalo'].read(tok)
        for j in range(8):
            sb_, db = w_get('glu_b', j)
            sa_, da = w_get('glu_a', j)
            for cc in range(2):
                c = 2 * j + cc
                bn, tok = proj_chunk(sb_, cc, hT_rhs, db + B['hT'].rdeps())
                if cc == 1:
                    w_release(sb_, tok)
                s = nxt('sg', 2)
                tok2 = act(sg[s][:], banks[bn][:], AF.Sigmoid, [tok] + B[f'sg{s}'].wdeps())
                bbuf[bn].read(tok2)
                B[f'sg{s}'].wrote(tok2)
                bn2, tok = proj_chunk(sa_, cc, hT_rhs, da)
                B['hT'].read(tok)
                if cc == 1:
                    w_release(sa_, tok)
                tok3 = tt('dve', uext[:, c, 30:542], banks[bn2][:], sg[s][:], ALU.mult, [tok, tok2] + B['uext'].wdeps())
                bbuf[bn2].read(tok3)
                B[f'sg{s}'].read(tok3)
                B['uext'].wrote(tok3)
        for c in range(16):
            if mask_halo:
                tok = ts('pool', uhalo[:, c, :], uext[:, c, 512:542], pm[:, 0:1], None, ALU.mult, None,
                         B['uext'].rdeps() + B['uhalo'].wdeps())
            else:
                tok = cp('pool', uhalo[:, c, :], uext[:, c, 512:542], B['uext'].rdeps() + B['uhalo'].wdeps())
        B['uhalo'].wrote(tok)
        B['uext'].read(tok)
        if not do_conv:
            return
        for c in range(16):
            a = nxt('tmp', 3)
            an = f'tmp{a}'
            tok = ts('dve', acc[a][:], uext[:, c, 0:512], wdw[:, c, 0:1], None, ALU.mult, None,
                     B['uext'].rdeps() + B[an].wdeps())
            for j in range(1, 31):
                tok = stt(acc[a][:], uext[:, c, j:j + 512], wdw[:, c, j:j + 1], acc[a][:], ALU.mult, ALU.add)
            B[an].wrote(tok)
            tok = act(uext[:, c, 30:542], acc[a][:], AF.Identity, B[an].rdeps() + B['uhalo'].rdeps() + B['uext'].wdeps(),
                      bias=vecs[:, VB, c:c + 1])
            B[an].read(tok)
            B['uext'].wrote(tok)

    def phase_a_fin():
        ub = lambda c: uext[:, c, 30:542]
        for c in range(16):
            q = nxt('sqb', 2)
            tok = act(sqb[q][:], ub(c), AF.Square, B['uext'].rdeps() + B[f'sqb{q}'].wdeps())
            B[f'sqb{q}'].wrote(tok)
            tok1 = pg.op('pe', (lambda e, c=c: e.matmul(banks['sa'][:], ones_bf[:], ub(c), start=(c == 0), stop=(c == 15))),
                         B['uext'].rdeps() + (bbuf['sa'].wdeps() if c == 0 else []))
            tok2 = pg.op('pe', (lambda e, c=c, q=q: e.matmul(banks['sb'][:], ones_bf[:], sqb[q][:], start=(c == 0), stop=(c == 15))),
                         [tok] + (bbuf['sb'].wdeps() if c == 0 else []))
            B[f'sqb{q}'].read(tok2)
        bbuf['sa'].wrote(tok1)
        bbuf['sb'].wrote(tok2)
        B['uext'].read(tok1)
        m_ = nxt('tmp', 3)
        lnm = tmpf[m_]
        d0 = B[f'tmp{m_}'].wdeps() + B['lnr'].wdeps() + B['lnb'].wdeps()
        tok = ts('dve', lnm[:], banks['sa'][:], 1.0 / D, None, ALU.mult, None, bbuf['sa'].rdeps() + d0)
        bbuf['sa'].read(tok)
        tok = tt('dve', lnb[:], lnm[:], lnm[:], ALU.mult)
        tok = stt(lnr[:], banks['sb'][:], 1.0 / D, lnb[:], ALU.mult, ALU.subtract, bbuf['sb'].rdeps())
        bbuf['sb'].read(tok)
        tok = act(lnr[:], lnr[:], AF.Sqrt, [tok], bias=EPS)
        tok = recip(lnr[:], lnr[:], [tok])
        tok = stt(lnb[:], lnm[:], -1.0, lnr[:], ALU.mult, ALU.mult)
        B[f'tmp{m_}'].wrote(tok)
        B[f'tmp{m_}'].read(tok)
        B['lnr'].wrote(tok)
        B['lnb'].wrote(tok)
        lnd = [tok]
        for j in range(8):
            sz, dz = w_get('z_conv', j)
            for cc in range(2):
                c = 2 * j + cc
                bn, tok = proj_chunk(sz, cc, hT_rhs, dz + B['hT'].rdeps())
                B['hT'].read(tok)
                if cc == 1:
                    w_release(sz, tok)
                z = nxt('zs', 2)
                tokz = act(zs[z][:], banks[bn][:], AF.Silu, [tok] + B[f'zs{z}'].wdeps())
                bbuf[bn].read(tokz)
                B[f'zs{z}'].wrote(tokz)
                t = nxt('tmp', 3)
                tok = tt('dve', t1[t][:], ub(c), lnr[:], ALU.mult, lnd + B['uext'].rdeps() + B[f'tmp{t}'].wdeps())
                tok = tt('dve', t1[t][:], t1[t][:], lnb[:], ALU.add)
                B[f'tmp{t}'].wrote(tok)
                y = nxt('yb', 2)
                tok = act(yb[y][:], t1[t][:], AF.Silu, [tok] + B[f'yb{y}'].wdeps(), bias=vecs[:, VBL, c:c + 1],
                          scale=vecs[:, VGL, c:c + 1])
                B[f'tmp{t}'].read(tok)
                B[f'yb{y}'].wrote(tok)
                tok = tt('pool', ub(c), yb[y][:], zs[z][:], ALU.mult, [tok, tokz] + B['uext'].wdeps())
                B[f'yb{y}'].read(tok)
                B[f'zs{z}'].read(tok)
                B['uext'].wrote(tok)
        B['lnr'].read(tok)
        B['lnb'].read(tok)
        out_and_gate('conv_out', 'g_c', lambda kc: uext[:, kc, 30:542], 'uext', first=True)

    def out_and_gate(wname, gname, rhs_fn, srcbuf, first):
        for j in range(8):
            so, do = w_get(wname, j)
            sg_, dg = w_get(gname, j)
            for cc in range(2):
                c = 2 * j + cc
                bn, tok = proj_chunk(sg_, cc, hT_rhs, dg + B['hT'].rdeps())
                B['hT'].read(tok)
                if cc == 1:
                    w_release(sg_, tok)
                s = nxt('sg', 2)
                tok2 = act(sg[s][:], banks[bn][:], AF.Sigmoid, [tok] + B[f'sg{s}'].wdeps())
                bbuf[bn].read(tok2)
                B[f'sg{s}'].wrote(tok2)
                bn2, tok = proj_chunk(so, cc, rhs_fn, do + B[srcbuf].rdeps())
                B[srcbuf].read(tok)
                if cc == 1:
                    w_release(so, tok)
                if first:
                    tok3 = tt('dve', merged[:, c, :], banks[bn2][:], sg[s][:], ALU.mult, [tok, tok2] + B['merged'].wdeps()
                              + B['qT'].wdeps() + B['kT'].wdeps() + B['ktok'].wdeps() + B['vtok'].wdeps())
                    bbuf[bn2].read(tok3)
                    B[f'sg{s}'].read(tok3)
                    B['merged'].wrote(tok3)
                else:
                    t = nxt('tmp', 3)
                    tok3 = tt('dve', t1[t][:], banks[bn2][:], sg[s][:], ALU.mult, [tok, tok2] + B[f'tmp{t}'].wdeps())
                    bbuf[bn2].read(tok3)
                    B[f'sg{s}'].read(tok3)
                    B[f'tmp{t}'].wrote(tok3)
                    tok4 = tt('pool', merged[:, c, :], merged[:, c, :], t1[t][:], ALU.add, [tok3] + B['merged'].wdeps())
                    B[f'tmp{t}'].read(tok4)
                    B['merged'].wrote(tok4)

    def qk_chunk(slot, cc, ch, dst, d_idx, dslot, mask_halo, only_halo):
        bn, tok = proj_chunk(slot, cc, hT_rhs, dslot + B['hT'].rdeps())
        B['hT'].read(tok)
        pq = nxt('qpre', 2)
        pn = f'qpre{pq}'
        tok1 = act(qpre[pq][:, 4:516], banks[bn][:], AF.Copy, [tok] + B[pn].wdeps())
        bbuf[bn].read(tok1)
        tokh = cp('pool', qpre[pq][:, 0:4], qkhalo[:, ch, :], B[pn].wdeps() + B['qkhalo'].rdeps())
        if mask_halo:
            tok2 = ts('pool', qkhalo[:, ch, :], qpre[pq][:, 512:516], pm[:, 0:1], None, ALU.mult, None, [tok1])
        else:
            tok2 = cp('pool', qkhalo[:, ch, :], qpre[pq][:, 512:516], [tok1])
        B['qkhalo'].wrote(tok2)
        B[pn].wrote(tok2)
        B[pn].wrote(tok1)
        if only_halo:
            B[pn].read(tok2)
            return tok
        a = nxt('tmp', 3)
        an = f'tmp{a}'
        tok3 = ts('dve', acc[a][:], qpre[pq][:, 1:513], wqk[:, ch, 0:1], None, ALU.mult, None, B[pn].rdeps() + B[an].wdeps())
        for j in range(1, 4):
            tok3 = stt(acc[a][:], qpre[pq][:, 1 + j:513 + j], wqk[:, ch, j:j + 1], acc[a][:], ALU.mult, ALU.add)
        B[pn].read(tok3)
        B[an].wrote(tok3)
        tok4 = act(dst[:, d_idx, :], acc[a][:], AF.Silu, [tok3])
        B[an].read(tok4)
        return tok, tok4

    def phase_b(state_only, last_pre):
        gates_pe()
        for h in range(4):
            if (not state_only) or last_pre:
                d0 = B['qT'].wdeps() + B['merged'].wdeps()
                for jj in range(2):
                    sq_, dq = w_get('q', 2 * h + jj)
                    for cc in range(2):
                        r = qk_chunk(sq_, cc, 4 * h + 2 * jj + cc, qT, 2 * jj + cc, dq + d0, last_pre, state_only)
                        if state_only:
                            tokp = r
                        else:
                            tokp, tokw = r
                            B['qT'].wrote(tokw)
                    w_release(sq_, tokp)
            d0 = B['kT'].wdeps() + B['merged'].wdeps()
            for jj in range(2):
                sk_, dk = w_get('k', 2 * h + jj)
                for cc in range(2):
                    tokp, tokw = qk_chunk(sk_, cc, 16 + 4 * h + 2 * jj + cc, kT, 2 * jj + cc, dk + d0, last_pre, False)
                    B['kT'].wrote(tokw)
                w_release(sk_, tokp)
            sv0, dv0 = w_get('v', 2 * h)
            sv1, dv1 = w_get('v', 2 * h + 1)
            for tc in range(4):
                bn = take_mm()
                tsl = slice(tc * 128, (tc + 1) * 128)
                tok = mm(banks[bn][:, 0:256], [(hT[:, kc, tsl], WBUF(sv0)[:, kc, :]) for kc in range(16)],
                         dv0 + dv1 + B['hT'].rdeps() + bbuf[bn].wdeps(), sig=False)
                tok = mm(banks[bn][:, 256:512], [(hT[:, kc, tsl], WBUF(sv1)[:, kc, :]) for kc in range(16)])
                bbuf[bn].wrote(tok)
                B['hT'].read(tok)
                tok2 = act(vtok[:, tc, :], banks[bn][:], AF.Copy, [tok] + B['vtok'].wdeps() + B['merged'].wdeps())
                bbuf[bn].read(tok2)
                B['vtok'].wrote(tok2)
            w_release(sv0, tok)
            w_release(sv1, tok)
            if not state_only:
                d0 = B['oT'].wdeps()
                for jj in range(2):
                    so_, do_ = w_get('o', 2 * h + jj)
                    for cc in range(2):
                        bn, tok = proj_chunk(so_, cc, hT_rhs, do_ + B['hT'].rdeps())
                        B['hT'].read(tok)
                        tok2 = act(oT[:, 2 * jj + cc, :], banks[bn][:], AF.Sigmoid, [tok] + d0)
                        bbuf[bn].read(tok2)
                        B['oT'].wrote(tok2)
                    w_release(so_, tok)
            for tc in range(4):
                trb = banks['dc'][:].bitcast(BF16)
                deps = B['kT'].rdeps() + bbuf['dc'].wdeps()
                for d in range(4):
                    tok = pg.op('pe', (lambda e, d=d, tc=tc, trb=trb: e.transpose(trb[:, d * 128:(d + 1) * 128],
                                                                                 kT[:, d, tc * 128:(tc + 1) * 128], ident_bf[:])),
                                deps if d == 0 else (), sig=(d == 3))
                bbuf['dc'].wrote(tok)
                B['kT'].read(tok)
                tok2 = cp('dve', ktok[:, tc, :], trb[:, 0:512], [tok] + B['ktok'].wdeps() + B['merged'].wdeps())
                bbuf['dc'].read(tok2)
                B['ktok'].wrote(tok2)
            if not state_only:
                tok = mm(banks['sb'][:], [(sel(h), g_F[:])], B['g_F'].rdeps() + bbuf['sb'].wdeps())
                bbuf['sb'].wrote(tok)
                B['g_F'].read(tok)
                tok2 = cp('dve', Fbc[:], banks['sb'][:], [tok] + B['Fbc'].wdeps())
                bbuf['sb'].read(tok2)
                B['Fbc'].wrote(tok2)
            for c in range(4):
                cs = slice(c * 128, (c + 1) * 128)
                col = c * 4 + h
                dcol = h * 4 + c
                smb = banks['sm']
                if not state_only:
                    tok = mm(smb[:, 0:128], [(kT[:, d, cs], qT[:, d, cs]) for d in range(4)],
                             B['kT'].rdeps() + B['qT'].rdeps() + bbuf['sm'].wdeps())
                    bbuf['sm'].wrote(tok)
                    B['kT'].read(tok)
                    s = nxt('Sm', 2)
                    tok2 = stt(Sm[s][:], smb[:, 0:128], wkf[:, col:col + 1], maskf, ALU.mult, ALU.mult,
                               [tok] + B['wkf'].rdeps() + B[f'Sm{s}'].wdeps() + cdep)
                    bbuf['sm'].read(tok2)
                    B['wkf'].read(tok2)
                    B[f'Sm{s}'].wrote(tok2)
                    tok3 = act(Cd[:], Cm[:, 4 * h:4 * h + 4, :], AF.Copy, B['Cm'].rdeps() + B['Cd'].wdeps() + B['decbc'].rdeps(),
                               scale=decbc[:, dcol:dcol + 1])
                    B['Cm'].read(tok3)
                    B['Cd'].wrote(tok3)
                    tok4 = ts('dve', nd[:], nst[:, 4 * h:4 * h + 4], decbc[:, dcol:dcol + 1], None, ALU.mult, None,
                              B['nst'].rdeps() + B['nd'].wdeps() + B['decbc'].rdeps())
                    B['nst'].read(tok4)
                    B['nd'].wrote(tok4)
                    for d in range(4):
                        tok5 = ts('pool', nrep[:, d, :], ones_bf[:], nd[:, d:d + 1], None, ALU.mult, None,
                                  [tok4] + B['nrep'].wdeps() + cdep)
                    B['nd'].read(tok5)
                    B['nrep'].wrote(tok5)
                    hb = banks['hh']
                    for ec in range(4):
                        es_ = slice(ec * 128, (ec + 1) * 128)
                        pairs = [(vtok[:, c, es_], Sm[s][:])] + [(Cd[:, d, es_], qT[:, d, cs]) for d in range(4)]
                        tok = mm(hb[:, es_], pairs, ([tok2, tok3] + B['vtok'].rdeps() + bbuf['hh'].wdeps()) if ec == 0 else (),
                                 sig=(ec == 3))
                    bbuf['hh'].wrote(tok)
                    B['Cd'].read(tok)
                    B['vtok'].read(tok)
                    pairs = [(ones_bf[:], Sm[s][:])] + [(nrep[:, d, :], qT[:, d, cs]) for d in range(4)]
                    tokd = mm(smb[:, 128:256], pairs, [tok5])
                    bbuf['sm'].wrote(tokd)
                    B['nrep'].read(tokd)
                    B[f'Sm{s}'].read(tokd)
                    B['qT'].read(tokd)
                    tok6 = act(dmt[:], smb[:, 128:256], AF.Abs, [tokd] + B['rden'].wdeps())
                    bbuf['sm'].read(tok6)
                    tok6 = tt('dve', dmt[:], dmt[:], Fbc[:, cs], ALU.max, [tok6] + B['Fbc'].rdeps())
                    B['Fbc'].read(tok6)
                    tok6 = recip(rden[:], dmt[:], [tok6])
                    B['rden'].wrote(tok6)
                    for ec in range(4):
                        tok7 = tt('pool', prod[:, ec, :], oT[:, ec, cs], rden[:], ALU.mult,
                                  [tok6] + B['oT'].rdeps() + B['prod'].wdeps())
                    B['rden'].read(tok7)
                    B['oT'].read(tok7)
                    B['prod'].wrote(tok7)
                    tok8 = tt('dve', ho[:].rearrange("p a b -> p (a b)"), hb[:], prod[:].rearrange("p a b -> p (a b)"), ALU.mult,
                              [tok, tok7] + B['ho'].wdeps())
                    bbuf['hh'].read(tok8)
                    B['prod'].read(tok8)
                    B['ho'].wrote(tok8)
                    tok9 = act(hsq[:].rearrange("p a b -> p (a b)"), ho[:].rearrange("p a b -> p (a b)"), AF.Square,
                               [tok8] + B['hsq'].wdeps())
                    B['hsq'].wrote(tok9)
                    tokn = mm(smb[:, 256:384], [(ones_bf[:], hsq[:, ec, :]) for ec in range(4)], [tok9])
                    bbuf['sm'].wrote(tokn)
                    B['hsq'].read(tokn)
                    tok10 = act(rsh[:], smb[:, 256:384], AF.Sqrt, [tokn] + B['rsh'].wdeps(), bias=EPS, scale=1.0 / 512.0)
                    bbuf['sm'].read(tok10)
                    tok10 = recip(rsh[:], rsh[:], [tok10])
                    B['rsh'].wrote(tok10)
                    for ec in range(4):
                        tok11 = stt(Pbuf[:, 4 * h + ec, cs], ho[:, ec, :], vecs[:, VGM, 4 * h + ec:4 * h + ec + 1], rsh[:],
                                    ALU.mult, ALU.mult, (B['Pbuf'].wdeps() + B['o_out'].wdeps()) if (ec == 0) else ())
                    B['ho'].read(tok11)
                    B['rsh'].read(tok11)
                    B['Pbuf'].wrote(tok11)
                v_ = nxt('vw', 2)
                tokv = act(vw[v_][:], vtok[:, c, :], AF.Copy, B['vtok'].rdeps() + B['wkf'].rdeps() + B[f'vw{v_}'].wdeps(),
                           scale=wkf[:, col:col + 1])
                B['vtok'].read(tokv)
                B['wkf'].read(tokv)
                B[f'vw{v_}'].wrote(tokv)
                for d in range(4):
                    bn = 'dc' if d % 2 == 0 else 'sa'
                    tok = mm(banks[bn][:], [(ktok[:, c, d * 128:(d + 1) * 128], vw[v_][:])],
                             [tokv] + B['ktok'].rdeps() + bbuf[bn].wdeps())
                    bbuf[bn].wrote(tok)
                    tok2 = stt(Cm[:, 4 * h + d, :], Cm[:, 4 * h + d, :], decbc[:, dcol:dcol + 1], banks[bn][:], ALU.mult, ALU.add,
                               [tok] + B['Cm'].wdeps() + B['decbc'].rdeps())
                    bbuf[bn].read(tok2)
                    B['Cm'].wrote(tok2)
                B[f'vw{v_}'].read(tok)
                for d in range(4):
                    tok = mm(smb[:, 416 + d:417 + d], [(ktok[:, c, d * 128:(d + 1) * 128], wkb[:, col:col + 1])],
                             (B['wkb'].rdeps() + bbuf['sm'].wdeps()) if d == 0 else (), sig=(d == 3))
                bbuf['sm'].wrote(tok)
                B['ktok'].read(tok)
                B['wkb'].read(tok)
                tok2 = stt(nst[:, 4 * h:4 * h + 4], nst[:, 4 * h:4 * h + 4], decbc[:, dcol:dcol + 1], smb[:, 416:420],
                           ALU.mult, ALU.add, [tok] + B['nst'].wdeps())
                bbuf['sm'].read(tok2)
                B['decbc'].read(tok2)
                B['nst'].wrote(tok2)

    def phase_b_fin():
        for j in range(8):
            sz, dz = w_get('z_ml', j)
            for cc in range(2):
                c = 2 * j + cc
                bn, tok = proj_chunk(sz, cc, hT_rhs, dz + B['hT'].rdeps())
                B['hT'].read(tok)
                if cc == 1:
                    w_release(sz, tok)
                z = nxt('zs', 2)
                tokz = act(zs[z][:], banks[bn][:], AF.Silu, [tok] + B[f'zs{z}'].wdeps())
                bbuf[bn].read(tokz)
                B[f'zs{z}'].wrote(tokz)
                tok = tt('pool', Pbuf[:, c, :], Pbuf[:, c, :], zs[z][:], ALU.mult, [tokz] + B['Pbuf'].wdeps())
                B[f'zs{z}'].read(tok)
                B['Pbuf'].wrote(tok)
        out_and_gate('ml_out', 'g_m', lambda kc: Pbuf[:, kc, :], 'Pbuf', first=False)

    def phase_c():
        for h in range(4):
            d0 = B['qx'].wdeps()
            for jj in range(2):
                sq_, dq = w_get('q_xa', 2 * h + jj)
                for cc in range(2):
                    bn, tok = proj_chunk(sq_, cc, hT_rhs, dq + B['hT'].rdeps())
                    B['hT'].read(tok)
                    tok2 = act(qx[:, 2 * jj + cc, :], banks[bn][:], AF.Copy, [tok] + d0)
                    bbuf[bn].read(tok2)
                    B['qx'].wrote(tok2)
                w_release(sq_, tok)
            d0 = B['pT'].wdeps()
            for mc in range(2):
                bn = take_mm()
                tok = mm(banks[bn][:], [(kmT[:, 4 * h + d, mc * 128:(mc + 1) * 128], qx[:, d, :]) for d in range(4)],
                         B['qx'].rdeps() + B['kmT'].rdeps() + bbuf[bn].wdeps())
                bbuf[bn].wrote(tok)
                tok2 = act(pT[:, mc, :], banks[bn][:], AF.Exp, [tok] + d0, scale=XSCALE)
                bbuf[bn].read(tok2)
                B['pT'].wrote(tok2)
            B['qx'].read(tok)
            tok = mm(banks['sa'][:], [(ones_bf[:], pT[:, mc, :]) for mc in range(2)], B['pT'].rdeps() + bbuf['sa'].wdeps())
            bbuf['sa'].wrote(tok)
            tok2 = recip(rdx[:], banks['sa'][:], [tok] + B['rdx'].wdeps())
            bbuf['sa'].read(tok2)
            B['rdx'].wrote(tok2)
            for jj in range(2):
                sz, dz = w_get('z_xa', 2 * h + jj)
                for cc in range(2):
                    ec = 2 * jj + cc
                    c = 4 * h + ec
                    bn = take_mm()
                    tok = mm(banks[bn][:], [(vm[:, mc, c * 128:(c + 1) * 128], pT[:, mc, :]) for mc in range(2)],
                             B['pT'].rdeps() + B['vm'].rdeps() + bbuf[bn].wdeps())
                    bbuf[bn].wrote(tok)
                    B['pT'].read(tok)
                    bn2, tokz = proj_chunk(sz, cc, hT_rhs, dz + B['hT'].rdeps())
                    B['hT'].read(tokz)
                    z = nxt('zs', 2)
                    tokz2 = act(zs[z][:], banks[bn2][:], AF.Silu, [tokz] + B[f'zs{z}'].wdeps())
                    bbuf[bn2].read(tokz2)
                    B[f'zs{z}'].wrote(tokz2)
                    t = nxt('tmp', 3)
                    tok3 = tt('dve', t1[t][:], banks[bn][:], rdx[:], ALU.mult, [tok] + B['rdx'].rdeps() + B[f'tmp{t}'].wdeps())
                    bbuf[bn].read(tok3)
                    B['rdx'].read(tok3)
                    B[f'tmp{t}'].wrote(tok3)
                    tok4 = tt('pool', Pbuf[:, c, :], t1[t][:], zs[z][:], ALU.mult, [tok3, tokz2] + B['Pbuf'].wdeps())
                    B[f'tmp{t}'].read(tok4)
                    B[f'zs{z}'].read(tok4)
                    B['Pbuf'].wrote(tok4)
                w_release(sz, tokz)
        out_and_gate('xa_out', 'g_x', lambda kc: Pbuf[:, kc, :], 'Pbuf', first=False)

    def phase_o(t0):
        for j in range(8):
            so, do = w_get('w_out', j)
            for cc in range(2):
                c = 2 * j + cc
                bn, tok = proj_chunk(so, cc, lambda kc: merged[:, kc, :], do + B['merged'].rdeps())
                B['merged'].read(tok)
                if cc == 1:
                    w_release(so, tok)
                tok2 = act(o_out[:, c, :], banks[bn][:], AF.Copy, [tok] + B['o_out'].wdeps() + B['uext'].wdeps() + B['Pbuf'].wdeps())
                B['o_out'].wrote(tok2)
                q = nxt('sqb', 2)
                tok3 = act(sqb[q][:], banks[bn][:], AF.Square, [tok] + B[f'sqb{q}'].wdeps())
                bbuf[bn].read(tok3)
                B[f'sqb{q}'].wrote(tok3)
                tok4 = pg.op('pe', (lambda e, c=c, q=q: e.matmul(banks['sa'][:], ones_bf[:], sqb[q][:], start=(c == 0), stop=(c == 15))),
                             [tok3] + (bbuf['sa'].wdeps() if c == 0 else []))
                B[f'sqb{q}'].read(tok4)
        bbuf['sa'].wrote(tok4)
        tok = act(rstdx[:], banks['sa'][:], AF.Sqrt, bbuf['sa'].rdeps() + B['rstdx'].wdeps(), bias=EPS, scale=1.0 / D)
        bbuf['sa'].read(tok)
        tok = recip(rstdx[:], rstdx[:], [tok])
        B['rstdx'].wrote(tok)
        kcs = list(range(16))
        st, issue = x_loader(xT, t0)
        for _ in range(2):
            issue(kcs)
        for c in range(16):
            s = st['slots'][c]
            t = nxt('tmp', 3)
            tok = stt(t1[t][:], o_out[:, c, :], vecs[:, VGP, c:c + 1], rstdx[:], ALU.mult, ALU.mult,
                      B['o_out'].rdeps() + B['rstdx'].rdeps() + B[f'tmp{t}'].wdeps())
            B['o_out'].read(tok)
            B[f'tmp{t}'].wrote(tok)
            a = nxt('tmp', 3)
            an = f'tmp{a}'
            tok2 = tt('pool', acc[a][:], t1[t][:], xs[s][:], ALU.add, [tok] + B[f'xs{s}'].rdeps() + B[an].wdeps())
            B[f'tmp{t}'].read(tok2)
            B[f'xs{s}'].read(tok2)
            B[an].wrote(tok2)
            dst = outT[c * P:(c + 1) * P, t0:t0 + NT]
            tok3 = pg.dma('pool', f'st{a}', (lambda e, a=a, dst=dst: e.dma_start(out=dst, in_=acc[a][:])), [tok2])
            B[an].read(tok3)
            issue(kcs)
        B['rstdx'].read(tok)

    def mem_kv():
        rms_tile(memT, 0, VGMEM, Pbuf, B['Pbuf'], width=256)
        mrhs = lambda kc: Pbuf[:, kc, 0:256]
        for j in range(8):
            s_, d_ = w_get('mk', j)
            for cc in range(2):
                c = 2 * j + cc
                bn = take_mm()
                tok = mm(banks[bn][:, 0:256], [(WBUF(s_)[:, kc, cc * 128:(cc + 1) * 128], mrhs(kc)) for kc in range(16)],
                         d_ + B['Pbuf'].rdeps() + bbuf[bn].wdeps())
                bbuf[bn].wrote(tok)
                tok2 = act(kmT[:, c, :], banks[bn][:, 0:256], AF.Copy, [tok])
                bbuf[bn].read(tok2)
                B['kmT'].wrote(tok2)
            w_release(s_, tok)
        for j in range(8):
            s_, d_ = w_get('mv', j)
            for mc in range(2):
                bn = take_mm()
                tok = mm(banks[bn][:, 0:256], [(Pbuf[:, kc, mc * 128:(mc + 1) * 128], WBUF(s_)[:, kc, :]) for kc in range(16)],
                         d_ + B['Pbuf'].rdeps() + bbuf[bn].wdeps())
                bbuf[bn].wrote(tok)
                tok2 = act(vm[:, mc, j * 256:(j + 1) * 256], banks[bn][:, 0:256], AF.Copy, [tok])
                bbuf[bn].read(tok2)
                B['vm'].wrote(tok2)
            w_release(s_, tok)
            B['Pbuf'].read(tok)

    cast_some(4)
    for t in range(n_pre):
        last = (t == n_pre - 1)
        rms_tile(xTp, t * NT, VG, hT, B['hT'])
        gates_tile(True)
        cast_some(4 if last else 2)
        if last:
            phase_a(do_conv=False, mask_halo=True)
        phase_b(state_only=True, last_pre=last)
    cast_some(len(GROUPS))
    mem_kv()
    for t in range(n_main):
        rms_tile(xT, t * NT, VG, hT, B['hT'])
        gates_tile(False)
        phase_a(do_conv=True, mask_halo=False)
        phase_b(state_only=False, last_pre=False)
        phase_a_fin()
        phase_b_fin()
        phase_c()
        phase_o(t * NT)

    if dry:
        es.close()
        return None, wrec

    if debug:
        alld = [(e_, pg.cnt[e_]) for e_ in ('pe', 'act', 'dve', 'pool') if pg.cnt[e_] > 0]
        dbg = {'Cm': (Cm, [P, 16 * 512], F32), 'nst': (nst, [P, 16], F32), 'decbc': (decbc, [P, 16], F32),
               'wkf': (wkf, [P, 16], F32), 'g_sm': (g_sm, [4, 32], F32), 'g_A': (g_li, [4, 512], F32),
               'g_L': (g_L, [4, 512], F32), 'g_F': (g_G, [4, 512], F32), 'g_wk': (g_e, [4, 512], F32),
               'R2': (R2, [P, 10240], BF16), 'hT': (hT, [P, 16 * 512], BF16), 'kmT': (kmT, [P, 16 * 256], BF16),
               'vm': (vm, [P, 2 * 2048], BF16)}
        for nm, (t_, shp, dt_) in dbg.items():
            d_ = nc.dram_tensor("dbg_" + nm, shp, dt_, kind="ExternalOutput").ap()
            src_ = t_[:]
            if len(src_.shape) == 3:
                src_ = src_.rearrange("p a b -> p (a b)")
            pg.dma('sp', 'cst', (lambda e, d_=d_, src_=src_: e.dma_start(out=d_, in_=src_)), alld)
        pg.op('pool', lambda e: e.nop(), [('cst', pg.dcnt['cst'])], sig=False)

    fin = [(f'st{a}', pg.dcnt.get(f'st{a}', 0)) for a in range(3) if pg.dcnt.get(f'st{a}', 0) > 0]
    pg.op('pool', lambda e: e.nop(), fin, sig=False)

    with nc.Block() as block:
        def run(ename, eng):
            waited = {}
            own = 0
            serial = ename in ('act', 'dve', 'pool')
            for (fn, deps, sig) in pg.q[ename]:
                deps = list(deps)
                if serial and own > 0:
                    deps.append((ename, own))
                for (key, val) in deps:
                    if key == ename and not serial and sig[0] != 'dma':
                        continue
                    if waited.get(key, 0) >= val:
                        continue
                    eng.wait_ge(sems[key], val)
                    waited[key] = val
                ins = fn(eng)
                if sig[0] == 'eng':
                    ins.then_inc(sems[ename], 1)
                    own += 1
                elif sig[0] == 'dma':
                    ins.then_inc(sems[sig[1]], 16)

        @block.tensor
        def _(e):
            run('pe', e)

        @block.scalar
        def _(e):
            run('act', e)

        @block.vector
        def _(e):
            run('dve', e)

        @block.gpsimd
        def _(e):
            run('pool', e)

        @block.sync
        def _(e):
            run('sp', e)
    es.close()
    return nc, None


def make_program(n_pre, n_main, debug=False):
    _, wseq = build_program(n_pre, n_main, None)
    nc, _ = build_program(n_pre, n_main, wseq, debug=debug)
    return nc


def host_consts():
    cst = np.zeros((P, 384), np.float32)
    cst[:, 0:128] = np.eye(P, dtype=np.float32)
    jj, ii = np.meshgrid(np.arange(P), np.arange(P), indexing="ij")
    cst[:, 128:256] = (jj <= ii).astype(np.float32)
    cst[:, 256:384] = 1.0
    csm = np.zeros((4, 516), np.float32)
    for h in range(4):
        csm[h, h * 128:(h + 1) * 128] = 1.0
        csm[h, 512 + h] = 1.0
    return cst, csm


def chan(v):
    return np.ascontiguousarray(np.asarray(v, np.float32).reshape(16, P).T)


def make_in_maps(inputs, n_pre_tok, n_main_tok, cores):
    f = lambda a: np.ascontiguousarray(np.asarray(a, np.float32))
    x = inputs["x"]
    cst, csm = host_consts()
    vecs = np.stack([chan(inputs["g_pre"]), chan(inputs["b_dw"]), chan(inputs["g_ln"]), chan(inputs["b_ln"]),
                     chan(inputs["g_ml_head"]), chan(inputs["g_post"]), chan(inputs["g_mem"])], axis=1)
    wdw = np.ascontiguousarray(np.asarray(inputs["w_dw"], np.float32).T.reshape(16, P, 31).transpose(1, 0, 2))
    wqk = np.ascontiguousarray(np.asarray(inputs["w_qk_conv"], np.float32).T.reshape(32, P, 4).transpose(1, 0, 2))
    b_if = np.asarray(inputs["b_if"], np.float32)
    bif = np.ascontiguousarray(np.stack([b_if[0:4], b_if[4:8]], axis=1))
    w_in = f(inputs["w_in"])
    wif = np.ascontiguousarray(w_in[:, OFF['if']:OFF['if'] + 8].reshape(16, P, 8).transpose(1, 0, 2))
    shared = {"w_in": w_in, "w_mem_kv": f(inputs["w_mem_kv"]), "w_conv_out": f(inputs["w_conv_out"]),
              "w_ml_out": f(inputs["w_ml_out"]), "w_xa_out": f(inputs["w_xa_out"]), "w_out": f(inputs["w_out"]),
              "cst": cst, "csm": csm, "vecs": vecs.reshape(P, -1), "wdw": wdw.reshape(P, -1), "wqk": wqk.reshape(P, -1),
              "bif": bif, "wif": wif.reshape(P, -1)}
    maps = []
    for (b, s0, second) in cores:
        m = dict(shared)
        m["xT"] = np.ascontiguousarray(x[b, s0:s0 + n_main_tok].T)
        if second:
            m["xTp"] = np.ascontiguousarray(x[b, s0 - n_pre_tok:s0].T)
            mval = 1.0
        else:
            m["xTp"] = m["xT"][:, :n_pre_tok] if n_pre_tok <= n_main_tok else np.ascontiguousarray(x[b, 0:n_pre_tok].T)
            mval = 0.0
        m["memT"] = np.ascontiguousarray(np.asarray(inputs["mem"][b], np.float32).T)
        pmv = np.zeros((P, 2), np.float32)
        pmv[:, 0] = mval
        pmv[:, 1] = (mval - 1.0) * BIG
        m["pm"] = pmv
        maps.append(m)
    return maps


_NC_CACHE = {}


def kernel(**inputs):
    x = np.asarray(inputs["x"])
    Bb, S, _ = x.shape
    half = S // 2
    n_tiles = half // NT
    key = (n_tiles, n_tiles)
    if key not in _NC_CACHE:
        _NC_CACHE[key] = make_program(n_tiles, n_tiles)
    nc = _NC_CACHE[key]
    cores = []
    for b in range(Bb):
        cores.append((b, 0, False))
        cores.append((b, half, True))
    in_maps = make_in_maps(inputs, half, half, cores)
    res = run_bass_kernel_spmd(nc, in_maps, core_ids=list(range(len(cores))))
    out = np.empty((Bb, S, D), np.float32)
    for i, (b, s0, _) in enumerate(cores):
        out[b, s0:s0 + half] = res.results[i]["outT"].T
    return out
```

```python
import contextlib
import numpy as np
import concourse.bass as bass
import concourse.mybir as mybir
from concourse.alu_op_type import AluOpType as ALU
from concourse.bass_utils import run_bass_kernel_spmd

F32 = mybir.dt.float32
BF16 = mybir.dt.bfloat16
AF = mybir.ActivationFunctionType
P = 128
D = 2048
KC = 16
NT = 512
EPS = 1e-6
BIG = 1.0e4
KSCALE_LN = float(np.log(512.0 ** -0.5))
XSCALE = float(512.0 ** -0.5)

OFF = {'glu_a': 0, 'glu_b': 2048, 'z_conv': 4096, 'q': 6144, 'k': 8192, 'v': 10240, 'o': 12288,
       'z_ml': 14336, 'if': 16384, 'q_xa': 16392, 'z_xa': 18440, 'g_c': 20488, 'g_m': 22536, 'g_x': 24584}
GROUPS = ['k', 'v', 'mk', 'mv', 'glu_b', 'glu_a', 'q', 'o', 'z_conv', 'conv_out', 'g_c', 'z_ml', 'ml_out',
          'g_m', 'q_xa', 'z_xa', 'xa_out', 'g_x', 'w_out']
GSRC = {'mk': ('w_mem_kv', 0), 'mv': ('w_mem_kv', 2048), 'conv_out': ('w_conv_out', 0),
        'ml_out': ('w_ml_out', 0), 'xa_out': ('w_xa_out', 0), 'w_out': ('w_out', 0)}
for _g in GROUPS:
    if _g not in GSRC:
        GSRC[_g] = ('w_in', OFF[_g])
GIDX = {g: i for i, g in enumerate(GROUPS)}
NSLOT = 3
VG, VB, VGL, VBL, VGM, VGP, VGMEM = 0, 1, 2, 3, 4, 5, 6


class Buf:
    def __init__(self):
        self.w = {}
        self.r = {}

    def rdeps(self):
        return list(self.w.items())

    def wdeps(self):
        d = dict(self.r)
        for k, v in self.w.items():
            d[k] = max(d.get(k, 0), v)
        return list(d.items())

    def read(self, tok):
        if tok is not None:
            self.r[tok[0]] = max(self.r.get(tok[0], 0), tok[1])

    def wrote(self, tok):
        if tok is not None:
            self.w[tok[0]] = max(self.w.get(tok[0], 0), tok[1])


class Prog:
    ENG = ('pe', 'act', 'dve', 'pool', 'sp')

    def __init__(self, dry):
        self.dry = dry
        self.q = {e: [] for e in self.ENG}
        self.cnt = {e: 0 for e in self.ENG}
        self.dcnt = {}

    def op(self, e, fn, deps=(), sig=True, nosync=False):
        tok = None
        if sig:
            self.cnt[e] += 1
            tok = (e, self.cnt[e])
        if not self.dry:
            self.q[e].append((fn, tuple(d for d in deps if d is not None), ('eng', nosync) if sig else ('none', nosync)))
        return tok

    def dma(self, qe, sem, fn, deps=()):
        self.dcnt[sem] = self.dcnt.get(sem, 0) + 16
        tok = (sem, self.dcnt[sem])
        if not self.dry:
            self.q[qe].append((fn, tuple(d for d in deps if d is not None), ('dma', sem)))
        return tok


def build_program(n_pre, n_main, wseq=None, debug=False):
    dry = wseq is None
    T_pre, T_main = n_pre * NT, n_main * NT
    pg = Prog(dry)
    nc = bass.Bass("TRN2", target_bir_lowering=False)
    es = contextlib.ExitStack()

    def din(name, shape, dt=F32):
        return nc.dram_tensor(name, list(shape), dt, kind="ExternalInput").ap()

    xT = din("xT", [D, T_main])
    xTp = din("xTp", [D, T_pre])
    memT = din("memT", [D, 256])
    wsrc = {'w_in': din("w_in", [D, 26632]), 'w_mem_kv': din("w_mem_kv", [D, 4096]),
            'w_conv_out': din("w_conv_out", [D, D]), 'w_ml_out': din("w_ml_out", [D, D]),
            'w_xa_out': din("w_xa_out", [D, D]), 'w_out': din("w_out", [D, D])}
    d_cst = din("cst", [P, 384])
    d_csm = din("csm", [4, 516])
    d_vecs = din("vecs", [P, 7 * 16])
    d_wdw = din("wdw", [P, 16 * 31])
    d_wqk = din("wqk", [P, 32 * 4])
    d_bif = din("bif", [4, 2])
    d_pm = din("pm", [P, 2])
    d_wif = din("wif", [P, 16 * 8])
    outT = nc.dram_tensor("outT", [D, T_main], F32, kind="ExternalOutput").ap()
    WB = nc.dram_tensor("wbscr", [len(GROUPS) * 8, P, KC * 256], BF16, kind="Internal").ap()

    def sb(name, shape, dt):
        return es.enter_context(nc.sbuf_tensor("sb_" + name, list(shape), dt))

    def psb(name):
        return es.enter_context(nc.psum_tensor("ps_" + name, [P, 512], F32))

    cst = sb("cst", [P, 384], F32)
    csm = sb("csm", [4, 516], F32)
    ident_bf = sb("identb", [P, P], BF16)
    ones_bf = sb("onesb", [P, P], BF16)
    maskf = cst[:, 128:256]
    ones4 = sb("ones4", [4, 512], F32)
    vecs = sb("vecs", [P, 7, 16], F32)
    wdw = sb("wdw", [P, 16, 31], F32)
    wqk = sb("wqk", [P, 32, 4], F32)
    bif = sb("bif", [4, 2], F32)
    negbf = sb("negbf", [4, 1], F32)
    pm = sb("pm", [P, 2], F32)
    wif_f = sb("wif_f", [P, 16, 8], F32)
    wif = sb("wif", [P, 16, 8], BF16)
    hT = sb("hT", [P, 16, 512], BF16)
    wbuf = [sb(f"wbuf{i}", [P, 16, 256], BF16) for i in range(NSLOT)]
    R1 = sb("R1", [P, 16864], BF16)
    uext = R1[:, 0:8672].rearrange("p (c t) -> p c t", t=542)
    Pbuf = R1[:, 8672:16864].rearrange("p (c t) -> p c t", t=512)
    o_out = R1[:, 0:16384].bitcast(F32).rearrange("p (c t) -> p c t", t=512)
    R2 = sb("R2", [P, 10240], BF16)
    qT = R2[:, 0:2048].rearrange("p (c t) -> p c t", t=512)
    kT = R2[:, 2048:4096].rearrange("p (c t) -> p c t", t=512)
    ktok = R2[:, 4096:6144].rearrange("p (c t) -> p c t", t=512)
    vtok = R2[:, 6144:8192].rearrange("p (c t) -> p c t", t=512)
    oT = R2[:, 8192:10240].rearrange("p (c t) -> p c t", t=512)
    merged = R2[:, 0:8192].rearrange("p (c t) -> p c t", t=512)
    qx = oT
    Cm = sb("Cm", [P, 16, 512], F32)
    Cd = sb("Cd", [P, 4, 512], BF16)
    nst = sb("nst", [P, 16], F32)
    nd = sb("nd", [P, 4], F32)
    nrep = sb("nrep", [P, 4, 128], BF16)
    kmT = sb("kmT", [P, 16, 256], BF16)
    vm = sb("vm", [P, 2, 2048], BF16)
    uhalo = sb("uhalo", [P, 16, 30], BF16)
    qkhalo = sb("qkhalo", [P, 32, 4], BF16)
    xs = [sb(f"xs{i}", [P, 512], F32) for i in range(2)]
    sqb = [sb(f"sqb{i}", [P, 512], BF16) for i in range(2)]
    rstdx = sb("mf", [P, 512], F32)
    Fbc = rstdx
    rdx = rstdx
    sg = [sb(f"sg{i}", [P, 512], F32) for i in range(2)]
    tmpf = [sb(f"tmp{i}", [P, 512], F32) for i in range(3)]
    cacc = [sb(f"cacc{i}", [P, 512], F32) for i in range(1)]
    acc = tmpf
    t1 = tmpf
    qpre = [sb(f"qpre{i}", [P, 516], BF16) for i in range(2)]
    lnr = sb("lnr", [P, 512], F32)
    lnb = sb("lnb", [P, 512], F32)
    zs = [sb(f"zs{i}", [P, 512], BF16) for i in range(2)]
    yb = [sb(f"yb{i}", [P, 512], BF16) for i in range(2)]
    Sm = [sb(f"Sm{i}", [P, 128], BF16) for i in range(2)]
    dmt = sb("dmt", [P, 128], F32)
    rden = sb("rden", [P, 128], F32)
    prod = sb("prod", [P, 4, 128], F32)
    ho = sb("ho", [P, 4, 128], F32)
    hsq = sb("hsq", [P, 4, 128], BF16)
    rsh = sb("rsh", [P, 128], F32)
    vw = [sb(f"vw{i}", [P, 512], BF16) for i in range(2)]
    wkf = sb("wkf", [P, 16], F32)
    wkb = sb("wkb", [P, 16], BF16)
    decbc = sb("decbc", [P, 16], F32)
    g_li = sb("g_li", [4, 512], F32)
    g_e = sb("g_e", [4, 512], F32)
    g_L = sb("g_L", [4, 512], F32)
    g_G = sb("g_G", [4, 512], F32)
    g_wk = g_e
    g_F = g_G
    g_sm = sb("g_sm", [4, 32], F32)
    pT = sb("pT", [P, 2, 512], BF16)
    memx = sb("memx", [P, 16, 256], F32) if False else None

    banks = {n: psb(n) for n in ['m0', 'm1', 'm2', 'sa', 'sb', 'sm', 'hh', 'dc']}
    bbuf = {n: Buf() for n in banks}
    for _r in ('sm_s', 'sm_d', 'sm_n', 'sm_m'):
        bbuf[_r] = bbuf['sm']

    semnames = ['pe', 'act', 'dve', 'pool', 'cst', 'xs0', 'xs1', 'st0', 'st1', 'st2'] + \
               [f'w{i}' for i in range(NSLOT)] + [f'cg{i}' for i in range(len(GROUPS))]
    sems = {n: es.enter_context(nc.semaphore(n)) for n in semnames}

    def mm(out_ap, pairs, deps=(), sig=True):
        n = len(pairs)
        tok = None
        for i, (l, r) in enumerate(pairs):
            tok = pg.op('pe', (lambda e, l=l, r=r, i=i: e.matmul(out_ap, l, r, start=(i == 0), stop=(i == n - 1))),
                        deps if i == 0 else (), sig=(sig and i == n - 1))
        return tok

    def act(out, in_, func, deps=(), bias=None, scale=None):
        kw = {}
        if bias is not None:
            kw['bias'] = bias
        if scale is not None:
            kw['scale'] = scale
        return pg.op('act', lambda e: e.activation(out=out, in_=in_, func=func, **kw), deps)

    def tt(eng, out, a, b, op, deps=()):
        return pg.op(eng, lambda e: e.tensor_tensor(out=out, in0=a, in1=b, op=op), deps)

    def ts(eng, out, a, s1, s2, op0, op1, deps=()):
        if op1 is None:
            return pg.op(eng, lambda e: e.tensor_scalar(out=out, in0=a, scalar1=s1, scalar2=None, op0=op0), deps)
        return pg.op(eng, lambda e: e.tensor_scalar(out=out, in0=a, scalar1=s1, scalar2=s2, op0=op0, op1=op1), deps)

    def stt(out, in0, scalar, in1, op0, op1, deps=(), nosync=False):
        return pg.op('dve', lambda e: e.scalar_tensor_tensor(out=out, in0=in0, scalar=scalar, in1=in1, op0=op0, op1=op1), deps,
                     nosync=nosync)

    def cp(eng, out, in_, deps=()):
        return pg.op(eng, lambda e: e.tensor_copy(out=out, in_=in_), deps)

    def recip(out, in_, deps=()):
        return pg.op('dve', lambda e: e.reciprocal(out=out, in_=in_), deps)

    def scan(out, d0, d1, init, op0, op1, deps=()):
        return pg.op('dve', lambda e: e.tensor_tensor_scan(out=out, data0=d0, data1=d1, initial=init, op0=op0, op1=op1), deps)

    def memset(eng, ap, val, deps=()):
        return pg.op(eng, lambda e: e.memset(ap, val), deps)

    mm_names = ['m0', 'm1', 'm2']
    mm_i = [0]

    def take_mm():
        n = mm_names[mm_i[0] % 3]
        mm_i[0] += 1
        return n

    wrec = []
    wstate = {'next_load': 0, 'next_use': 0}
    slotbuf = [Buf() for _ in range(NSLOT)]
    released = {}
    castbuf = {g: Buf() for g in GROUPS}

    def WBUF(i):
        return wbuf[i % NSLOT]

    def w_try_issue():
        if dry:
            return
        while wstate['next_load'] < len(wseq):
            i = wstate['next_load']
            if i >= NSLOT and not released.get(i - NSLOT, False):
                return
            s = i % NSLOT
            g, j = wseq[i]
            src = WB[GIDX[g] * 8 + j].rearrange("p (k n) -> p k n", n=256)
            deps = slotbuf[s].wdeps() + castbuf[g].rdeps()
            assert castbuf[g].w, ("cast not issued before load", g)
            tok = pg.dma('sp', f'w{s}', (lambda e, s=s, src=src: e.dma_start(out=wbuf[s][:], in_=src)), deps)
            slotbuf[s].wrote(tok)
            wstate['next_load'] = i + 1

    def w_get(g, j):
        i = wstate['next_use']
        wstate['next_use'] = i + 1
        if dry:
            wrec.append((g, j))
            return i, []
        assert wseq[i] == (g, j), (i, wseq[i], g, j)
        w_try_issue()
        assert wstate['next_load'] > i, ("weight block not loadable (slot starvation)", i, g, j)
        return i, slotbuf[i % NSLOT].rdeps()

    def w_release(i, tok):
        if dry:
            return
        slotbuf[i % NSLOT].read(tok)
        released[i] = True
        w_try_issue()

    B_cst = Buf()
    loads = [(cst, d_cst), (csm, d_csm), (vecs, d_vecs.rearrange("p (a b) -> p a b", b=16)),
             (wdw, d_wdw.rearrange("p (a b) -> p a b", b=31)), (wqk, d_wqk.rearrange("p (a b) -> p a b", b=4)),
             (bif, d_bif), (pm, d_pm), (wif_f, d_wif.rearrange("p (a b) -> p a b", b=8))]
    for dst, src in loads:
        tok = pg.dma('sp', 'cst', (lambda e, dst=dst, src=src: e.dma_start(out=dst[:], in_=src)))
    B_cst.wrote(tok)
    cdep = B_cst.rdeps()
    tok = cp('dve', ident_bf[:], cst[:, 0:128], cdep)
    tok = cp('dve', ones_bf[:], cst[:, 256:384], cdep)
    tok = cp('dve', wif[:], wif_f[:], cdep)
    tok = ts('dve', negbf[:], bif[:, 1:2], -1.0, None, ALU.mult, None, cdep)
    tok = memset('dve', ones4[:], 1.0)
    tok = memset('dve', g_sm[:], 0.0)
    tok = memset('dve', uhalo[:], 0.0)
    tok = memset('dve', qkhalo[:], 0.0)
    tok = memset('dve', nst[:], 0.0)
    tok = memset('dve', Cm[:], 0.0)
    B_setup = Buf()
    B_setup.wrote(tok)
    sdep = B_setup.rdeps()
    sel = lambda h: csm[:, h * 128:(h + 1) * 128]
    i4 = csm[:, 512:516]

    def cast_group(g):
        sname, c0 = GSRC[g]
        src_t = wsrc[sname]
        for j in range(8):
            src = src_t[:, c0 + 256 * j: c0 + 256 * (j + 1)].rearrange("(k p) n -> p k n", p=P)
            dst = WB[GIDX[g] * 8 + j].rearrange("p (k n) -> p k n", n=256)
            tok = pg.dma('pool', f'cg{GIDX[g]}', (lambda e, dst=dst, src=src: e.dma_start(out=dst, in_=src)))
        castbuf[g].wrote(tok)

    cast_next = [0]

    def cast_some(n):
        for _ in range(n):
            if cast_next[0] < len(GROUPS):
                cast_group(GROUPS[cast_next[0]])
                cast_next[0] += 1

    B = {n: Buf() for n in ['hT', 'uext', 'Pbuf', 'qT', 'kT', 'ktok', 'vtok', 'oT', 'merged', 'Cm', 'Cd', 'nst', 'nd',
                            'nrep', 'kmT', 'vm', 'uhalo', 'qkhalo', 'rstdx', 'lnm', 'lnr', 'lnb', 'Fbc', 'dmt', 'rden',
                            'prod', 'ho', 'hsq', 'rsh', 'wkf', 'wkb', 'decbc', 'g_li', 'g_e', 'g_L', 'g_G', 'g_wk',
                            'g_F', 'g_sm', 'qx', 'pT', 'rdx', 'o_out']}
    for n, k in [('xs', 2), ('sqb', 2), ('sg', 2), ('qpre', 2), ('zs', 2), ('yb', 2),
                 ('Sm', 2), ('vw', 2)]:
        for i in range(k):
            B[f'{n}{i}'] = Buf()
    B['Fbc'] = B['rstdx']
    B['rdx'] = B['rstdx']
    B['g_wk'] = B['g_e']
    B['g_F'] = B['g_G']
    B['qx'] = B['oT']
    for i in range(3):
        B[f'tmp{i}'] = Buf()
    for i in range(2):
        B[f'cacc{i}'] = Buf()
    rot = {}

    def nxt(name, k):
        i = rot.get(name, 0)
        rot[name] = i + 1
        return i % k

    def x_loader(src_dram, t0, width=NT):
        state = {'issued': 0}

        def issue(kc_list):
            i = state['issued']
            if i >= len(kc_list):
                return
            s = nxt('xs', 2)
            kc = kc_list[i]
            src = src_dram[kc * P:(kc + 1) * P, t0:t0 + width]
            bname = f'xs{s}'
            tok = pg.dma('sp', bname, (lambda e, s=s, src=src: e.dma_start(out=xs[s][:, 0:width], in_=src)), B[bname].wdeps())
            B[bname].wrote(tok)
            state.setdefault('slots', []).append(s)
            state['issued'] = i + 1
        return state, issue

    def rms_tile(src_dram, t0, gvec, dst, dstB, width=NT):
        kcs = list(range(16)) + list(range(16))
        st, issue = x_loader(src_dram, t0, width)
        for _ in range(2):
            issue(kcs)
        for i in range(16):
            s = st['slots'][i]
            q = nxt('sqb', 2)
            tok = act(sqb[q][:, 0:width], xs[s][:, 0:width], AF.Square, B[f'xs{s}'].rdeps() + B[f'sqb{q}'].wdeps())
            B[f'xs{s}'].read(tok)
            B[f'sqb{q}'].wrote(tok)
            deps = B[f'sqb{q}'].rdeps() + (bbuf['sa'].wdeps() if i == 0 else [])
            tok = pg.op('pe', (lambda e, q=q, i=i: e.matmul(banks['sa'][:, 0:width], ones_bf[:], sqb[q][:, 0:width],
                                                             start=(i == 0), stop=(i == 15))), deps)
            B[f'sqb{q}'].read(tok)
            issue(kcs)
        bbuf['sa'].wrote(tok)
        tok = act(rstdx[:, 0:width], banks['sa'][:, 0:width], AF.Sqrt, bbuf['sa'].rdeps() + B['rstdx'].wdeps(),
                  bias=EPS, scale=1.0 / D)
        bbuf['sa'].read(tok)
        tok = recip(rstdx[:, 0:width], rstdx[:, 0:width], [tok])
        B['rstdx'].wrote(tok)
        for i in range(16):
            s = st['slots'][16 + i]
            deps = B[f'xs{s}'].rdeps() + B['rstdx'].rdeps() + (dstB.wdeps() if i == 0 else [])
            tok = stt(dst[:, i, 0:width], xs[s][:, 0:width], vecs[:, gvec, i:i + 1], rstdx[:, 0:width], ALU.mult, ALU.mult, deps)
            B[f'xs{s}'].read(tok)
            issue(kcs)
        B['rstdx'].read(tok)
        dstB.wrote(tok)

    def gates_tile(prefix):
        gi = banks['hh'][0:4, 0:512]
        gf = banks['sb'][0:4, 0:512]
        tok = mm(gi, [(wif[:, kc, 0:4], hT[:, kc, :]) for kc in range(16)], B['hT'].rdeps() + bbuf['hh'].wdeps() + sdep)
        bbuf['hh'].wrote(tok)
        tok = mm(gf, [(wif[:, kc, 4:8], hT[:, kc, :]) for kc in range(16)], bbuf['sb'].wdeps())
        bbuf['sb'].wrote(tok)
        B['hT'].read(tok)
        tok = act(g_li[:], gi, AF.Identity, bbuf['hh'].rdeps() + B['g_li'].wdeps(), bias=bif[:, 0:1])
        bbuf['hh'].read(tok)
        B['g_li'].wrote(tok)
        tok = act(g_e[:], gf, AF.Exp, bbuf['sb'].rdeps() + B['g_e'].wdeps(), bias=negbf[:, 0:1], scale=-1.0)
        bbuf['sb'].read(tok)
        tok = act(g_e[:], g_e[:], AF.Ln, [tok], bias=1.0)
        B['g_e'].wrote(tok)
        if prefix:
            tok = ts('dve', g_li[:], g_li[:], pm[0:4, 0:1], pm[0:4, 1:2], ALU.mult, ALU.add, B['g_li'].rdeps())
            B['g_li'].wrote(tok)
            tok = ts('dve', g_e[:], g_e[:], pm[0:4, 0:1], None, ALU.mult, None, B['g_e'].rdeps())
            B['g_e'].wrote(tok)
        tok = cp('dve', g_sm[:, 12:13], g_sm[:, 1:2], B['g_sm'].wdeps())
        tok = scan(g_L[:], ones4[:], g_e[:], g_sm[:, 0:1], ALU.mult, ALU.add, B['g_e'].rdeps() + B['g_L'].wdeps())
        B['g_e'].read(tok)
        tok = cp('dve', g_sm[:, 0:1], g_L[:, 511:512])
        tok = tt('dve', g_li[:], g_li[:], g_L[:], ALU.add, B['g_li'].wdeps())
        tok = scan(g_G[:], g_li[:], g_li[:], g_sm[:, 1:2], ALU.max, ALU.max, B['g_G'].wdeps())
        tok = cp('dve', g_sm[:, 1:2], g_G[:, 511:512])
        gl = g_G[:].rearrange("p (c t) -> p c t", t=128)[:, :, 127]
        tok = ts('dve', g_sm[:, 4:8], gl, -1.0, None, ALU.mult, None)
        tok = ts('dve', g_sm[:, 8:12], g_sm[:, 4:8], KSCALE_LN, None, ALU.add, None)
        tok = cp('dve', g_sm[:, 13:16], gl[:, 0:3])
        tok = tt('dve', g_sm[:, 16:20], g_sm[:, 12:16], g_sm[:, 4:8], ALU.add)
        B['g_sm'].wrote(tok)
        B['g_li'].wrote(tok)
        B['g_L'].wrote(tok)
        B['g_G'].wrote(tok)
        dv = [tok]
        tok = act(g_sm[:, 20:24], g_sm[:, 16:20], AF.Exp, dv)
        for c in range(4):
            cs = slice(c * 128, (c + 1) * 128)
            tok = act(g_wk[:, cs], g_li[:, cs], AF.Exp, dv + (B['g_wk'].wdeps() if c == 0 else []), bias=g_sm[:, 8 + c:9 + c])
            tok = act(g_F[:, cs], g_L[:, cs], AF.Exp, dv + (B['g_F'].wdeps() if c == 0 else []), bias=g_sm[:, 4 + c:5 + c])
        B['g_wk'].wrote(tok)
        B['g_F'].wrote(tok)
        B['g_sm'].read(tok)
        B['g_sm'].wrote(tok)
        B['g_li'].read(tok)
        B['g_L'].read(tok)

    def gates_pe():
        smb = banks['sm']
        deps = B['g_wk'].rdeps() + bbuf['sm_m'].wdeps()
        for c in range(4):
            tok = mm(smb[:, 384 + 4 * c:388 + 4 * c], [(g_wk[:, c * 128:(c + 1) * 128], i4)], deps if c == 0 else ())
        for h in range(4):
            tok = mm(smb[:, 400 + 4 * h:404 + 4 * h], [(sel(h), g_sm[:, 20:24])], B['g_sm'].rdeps() if h == 0 else ())
        bbuf['sm_m'].wrote(tok)
        B['g_wk'].read(tok)
        B['g_sm'].read(tok)
        d = bbuf['sm_m'].rdeps()
        tok = cp('dve', wkf[:], smb[:, 384:400], d + B['wkf'].wdeps())
        B['wkf'].wrote(tok)
        tok = cp('dve', wkb[:], smb[:, 384:400], d + B['wkb'].wdeps())
        B['wkb'].wrote(tok)
        tok = cp('dve', decbc[:], smb[:, 400:416], d + B['decbc'].wdeps())
        B['decbc'].wrote(tok)
        bbuf['sm_m'].read(tok)

    def proj_chunk(slot, cc, rhs_fn, first_deps):
        bn = take_mm()
        tok = mm(banks[bn][:], [(WBUF(slot)[:, kc, cc * 128:(cc + 1) * 128], rhs_fn(kc)) for kc in range(16)],
                 first_deps + bbuf[bn].wdeps())
        bbuf[bn].wrote(tok)
        return bn, tok

    hT_rhs = lambda kc: hT[:, kc, :]

    def phase_a(do_conv, mask_halo):
        for c in range(16):
            tok = cp('pool', uext[:, c, 0:30], uhalo[:, c, :], (B['uext'].wdeps() + B['uhalo'].rdeps() + B['o_out'].wdeps()) if c == 0 else ())
        B['uext'].wrote(tok)
        B['uhalo'].read(tok)
        for j in range(8):
            sb_, db = w_get('glu_b', j)
            sa_, da = w_get('glu_a', j)
            for cc in range(2):
                c = 2 * j + cc
                bn, tok = proj_chunk(sb_, cc, hT_rhs, db + B['hT'].rdeps())
                if cc == 1:
                    w_release(sb_, tok)
                s = nxt('sg', 2)
                tok2 = act(sg[s][:], banks[bn][:], AF.Sigmoid, [tok] + B[f'sg{s}'].wdeps())
                bbuf[bn].read(tok2)
                B[f'sg{s}'].wrote(tok2)
                bn2, tok = proj_chunk(sa_, cc, hT_rhs, da)
                B['hT'].read(tok)
                if cc == 1:
                    w_release(sa_, tok)
                tok3 = tt('dve', uext[:, c, 30:542], banks[bn2][:], sg[s][:], ALU.mult, [tok, tok2] + B['uext'].wdeps())
                bbuf[bn2].read(tok3)
                B[f'sg{s}'].read(tok3)
                B['uext'].wrote(tok3)
        for c in range(16):
            if mask_halo:
                tok = ts('pool', uhalo[:, c, :], uext[:, c, 512:542], pm[:, 0:1], None, ALU.mult, None,
                         B['uext'].rdeps() + B['uhalo'].wdeps())
            else:
                tok = cp('pool', uhalo[:, c, :], uext[:, c, 512:542], B['uext'].rdeps() + B['uhalo'].wdeps())
        B['uhalo'].wrote(tok)
        B['uext'].read(tok)
        if not do_conv:
            return
        filler['ops'] = list(conv_ops())
        filler['pos'] = 0

    filler = {'ops': [], 'pos': 0}

    def conv_ops():
        for c in range(16):
            st_ = {}

            def first(c=c, st_=st_):
                a = nxt("cacc", 1)
                st_['a'] = a
                an = f'cacc{a}'
                ts('dve', cacc[a][:], uext[:, c, 0:512], wdw[:, c, 0:1], None, ALU.mult, None,
                   B['uext'].rdeps() + B[an].wdeps())
            yield first
            for j in range(1, 31):
                def tap(c=c, j=j, st_=st_):
                    a = st_['a']
                    tok = stt(cacc[a][:], uext[:, c, j:j + 512], wdw[:, c, j:j + 1], cacc[a][:], ALU.mult, ALU.add, nosync=True)
                    if j == 30:
                        an = f'cacc{a}'
                        B[an].wrote(tok)
                        tok2 = act(uext[:, c, 30:542], cacc[a][:], AF.Identity,
                                   B[an].rdeps() + B['uhalo'].rdeps() + B['uext'].wdeps(), bias=vecs[:, VB, c:c + 1])
                        B[an].read(tok2)
                        B['uext'].wrote(tok2)
                yield tap

    def fill(n):
        while n > 0 and filler['pos'] < len(filler['ops']):
            filler['ops'][filler['pos']]()
            filler['pos'] += 1
            n -= 1

    def phase_a_fin():
        ub = lambda c: uext[:, c, 30:542]
        for c in range(16):
            q = nxt('sqb', 2)
            tok = act(sqb[q][:], ub(c), AF.Square, B['uext'].rdeps() + B[f'sqb{q}'].wdeps())
            B[f'sqb{q}'].wrote(tok)
            tok1 = pg.op('pe', (lambda e, c=c: e.matmul(banks['sa'][:], ones_bf[:], ub(c), start=(c == 0), stop=(c == 15))),
                         B['uext'].rdeps() + (bbuf['sa'].wdeps() if c == 0 else []))
            tok2 = pg.op('pe', (lambda e, c=c, q=q: e.matmul(banks['sb'][:], ones_bf[:], sqb[q][:], start=(c == 0), stop=(c == 15))),
                         [tok] + (bbuf['sb'].wdeps() if c == 0 else []))
            B[f'sqb{q}'].read(tok2)
        bbuf['sa'].wrote(tok1)
        bbuf['sb'].wrote(tok2)
        B['uext'].read(tok1)
        m_ = nxt('tmp', 3)
        lnm = tmpf[m_]
        d0 = B[f'tmp{m_}'].wdeps() + B['lnr'].wdeps() + B['lnb'].wdeps()
        tok = ts('dve', lnm[:], banks['sa'][:], 1.0 / D, None, ALU.mult, None, bbuf['sa'].rdeps() + d0)
        bbuf['sa'].read(tok)
        tok = tt('dve', lnb[:], lnm[:], lnm[:], ALU.mult)
        tok = stt(lnr[:], banks['sb'][:], 1.0 / D, lnb[:], ALU.mult, ALU.subtract, bbuf['sb'].rdeps())
        bbuf['sb'].read(tok)
        tok = act(lnr[:], lnr[:], AF.Sqrt, [tok], bias=EPS)
        tok = recip(lnr[:], lnr[:], [tok])
        tok = stt(lnb[:], lnm[:], -1.0, lnr[:], ALU.mult, ALU.mult)
        B[f'tmp{m_}'].wrote(tok)
        B[f'tmp{m_}'].read(tok)
        B['lnr'].wrote(tok)
        B['lnb'].wrote(tok)
        lnd = [tok]
        for j in range(8):
            sz, dz = w_get('z_conv', j)
            for cc in range(2):
                c = 2 * j + cc
                bn, tok = proj_chunk(sz, cc, hT_rhs, dz + B['hT'].rdeps())
                B['hT'].read(tok)
                if cc == 1:
                    w_release(sz, tok)
                z = nxt('zs', 2)
                tokz = act(zs[z][:], banks[bn][:], AF.Silu, [tok] + B[f'zs{z}'].wdeps())
                bbuf[bn].read(tokz)
                B[f'zs{z}'].wrote(tokz)
                t = nxt('tmp', 3)
                tok = tt('dve', t1[t][:], ub(c), lnr[:], ALU.mult, lnd + B['uext'].rdeps() + B[f'tmp{t}'].wdeps())
                tok = tt('dve', t1[t][:], t1[t][:], lnb[:], ALU.add)
                B[f'tmp{t}'].wrote(tok)
                y = nxt('yb', 2)
                tok = act(yb[y][:], t1[t][:], AF.Silu, [tok] + B[f'yb{y}'].wdeps(), bias=vecs[:, VBL, c:c + 1],
                          scale=vecs[:, VGL, c:c + 1])
                B[f'tmp{t}'].read(tok)
                B[f'yb{y}'].wrote(tok)
                tok = tt('pool', ub(c), yb[y][:], zs[z][:], ALU.mult, [tok, tokz] + B['uext'].wdeps())
                B[f'yb{y}'].read(tok)
                B[f'zs{z}'].read(tok)
                B['uext'].wrote(tok)
        B['lnr'].read(tok)
        B['lnb'].read(tok)
        out_and_gate('conv_out', 'g_c', lambda kc: uext[:, kc, 30:542], 'uext', first=False)

    def out_and_gate(wname, gname, rhs_fn, srcbuf, first, nfill=0):
        for j in range(8):
            so, do = w_get(wname, j)
            sg_, dg = w_get(gname, j)
            for cc in range(2):
                c = 2 * j + cc
                bn, tok = proj_chunk(sg_, cc, hT_rhs, dg + B['hT'].rdeps())
                B['hT'].read(tok)
                if cc == 1:
                    w_release(sg_, tok)
                s = nxt('sg', 2)
                tok2 = act(sg[s][:], banks[bn][:], AF.Sigmoid, [tok] + B[f'sg{s}'].wdeps())
                bbuf[bn].read(tok2)
                B[f'sg{s}'].wrote(tok2)
                bn2, tok = proj_chunk(so, cc, rhs_fn, do + B[srcbuf].rdeps())
                B[srcbuf].read(tok)
                if cc == 1:
                    w_release(so, tok)
                if first:
                    tok3 = tt('dve', merged[:, c, :], banks[bn2][:], sg[s][:], ALU.mult, [tok, tok2] + B['merged'].wdeps()
                              + B['qT'].wdeps() + B['kT'].wdeps() + B['ktok'].wdeps() + B['vtok'].wdeps())
                    bbuf[bn2].read(tok3)
                    B[f'sg{s}'].read(tok3)
                    B['merged'].wrote(tok3)
                else:
                    t = nxt('tmp', 3)
                    tok3 = tt('dve', t1[t][:], banks[bn2][:], sg[s][:], ALU.mult, [tok, tok2] + B[f'tmp{t}'].wdeps())
                    bbuf[bn2].read(tok3)
                    B[f'sg{s}'].read(tok3)
                    B[f'tmp{t}'].wrote(tok3)
                    tok4 = tt('pool', merged[:, c, :], merged[:, c, :], t1[t][:], ALU.add, [tok3] + B['merged'].wdeps())
                    B[f'tmp{t}'].read(tok4)
                    B['merged'].wrote(tok4)
                fill(nfill)

    def qk_chunk(slot, cc, ch, dst, d_idx, dslot, mask_halo, only_halo):
        bn, tok = proj_chunk(slot, cc, hT_rhs, dslot + B['hT'].rdeps())
        B['hT'].read(tok)
        pq = nxt('qpre', 2)
        pn = f'qpre{pq}'
        tok1 = act(qpre[pq][:, 4:516], banks[bn][:], AF.Copy, [tok] + B[pn].wdeps())
        bbuf[bn].read(tok1)
        tokh = cp('pool', qpre[pq][:, 0:4], qkhalo[:, ch, :], B[pn].wdeps() + B['qkhalo'].rdeps())
        if mask_halo:
            tok2 = ts('pool', qkhalo[:, ch, :], qpre[pq][:, 512:516], pm[:, 0:1], None, ALU.mult, None, [tok1])
        else:
            tok2 = cp('pool', qkhalo[:, ch, :], qpre[pq][:, 512:516], [tok1])
        B['qkhalo'].wrote(tok2)
        B[pn].wrote(tok2)
        B[pn].wrote(tok1)
        if only_halo:
            B[pn].read(tok2)
            return tok
        a = nxt('tmp', 3)
        an = f'tmp{a}'
        tok3 = ts('dve', acc[a][:], qpre[pq][:, 1:513], wqk[:, ch, 0:1], None, ALU.mult, None, B[pn].rdeps() + B[an].wdeps())
        for j in range(1, 4):
            tok3 = stt(acc[a][:], qpre[pq][:, 1 + j:513 + j], wqk[:, ch, j:j + 1], acc[a][:], ALU.mult, ALU.add, nosync=True)
        B[pn].read(tok3)
        B[an].wrote(tok3)
        tok4 = act(dst[:, d_idx, :], acc[a][:], AF.Silu, [tok3])
        B[an].read(tok4)
        fill(2)
        return tok, tok4

    def phase_b(state_only, last_pre):
        gates_pe()
        for h in range(4):
            if (not state_only) or last_pre:
                d0 = B['qT'].wdeps() + B['merged'].wdeps()
                for jj in range(2):
                    sq_, dq = w_get('q', 2 * h + jj)
                    for cc in range(2):
                        r = qk_chunk(sq_, cc, 4 * h + 2 * jj + cc, qT, 2 * jj + cc, dq + d0, last_pre, state_only)
                        if state_only:
                            tokp = r
                        else:
                            tokp, tokw = r
                            B['qT'].wrote(tokw)
                    w_release(sq_, tokp)
            d0 = B['kT'].wdeps() + B['merged'].wdeps()
            for jj in range(2):
                sk_, dk = w_get('k', 2 * h + jj)
                for cc in range(2):
                    tokp, tokw = qk_chunk(sk_, cc, 16 + 4 * h + 2 * jj + cc, kT, 2 * jj + cc, dk + d0, last_pre, False)
                    B['kT'].wrote(tokw)
                w_release(sk_, tokp)
            sv0, dv0 = w_get('v', 2 * h)
            sv1, dv1 = w_get('v', 2 * h + 1)
            for tc in range(4):
                bn = take_mm()
                tsl = slice(tc * 128, (tc + 1) * 128)
                tok = mm(banks[bn][:, 0:256], [(hT[:, kc, tsl], WBUF(sv0)[:, kc, :]) for kc in range(16)],
                         dv0 + dv1 + B['hT'].rdeps() + bbuf[bn].wdeps(), sig=False)
                tok = mm(banks[bn][:, 256:512], [(hT[:, kc, tsl], WBUF(sv1)[:, kc, :]) for kc in range(16)])
                bbuf[bn].wrote(tok)
                B['hT'].read(tok)
                tok2 = act(vtok[:, tc, :], banks[bn][:], AF.Copy, [tok] + B['vtok'].wdeps() + B['merged'].wdeps())
                bbuf[bn].read(tok2)
                B['vtok'].wrote(tok2)
                fill(4)
            w_release(sv0, tok)
            w_release(sv1, tok)
            if not state_only:
                d0 = B['oT'].wdeps()
                for jj in range(2):
                    so_, do_ = w_get('o', 2 * h + jj)
                    for cc in range(2):
                        bn, tok = proj_chunk(so_, cc, hT_rhs, do_ + B['hT'].rdeps())
                        B['hT'].read(tok)
                        tok2 = act(oT[:, 2 * jj + cc, :], banks[bn][:], AF.Sigmoid, [tok] + d0)
                        bbuf[bn].read(tok2)
                        B['oT'].wrote(tok2)
                        fill(2)
                    w_release(so_, tok)
            for tc in range(4):
                trb = banks['dc'][:].bitcast(BF16)
                deps = B['kT'].rdeps() + bbuf['dc'].wdeps()
                for d in range(4):
                    tok = pg.op('pe', (lambda e, d=d, tc=tc, trb=trb: e.transpose(trb[:, d * 128:(d + 1) * 128],
                                                                                 kT[:, d, tc * 128:(tc + 1) * 128], ident_bf[:])),
                                deps if d == 0 else (), sig=(d == 3))
                bbuf['dc'].wrote(tok)
                B['kT'].read(tok)
                tok2 = cp('dve', ktok[:, tc, :], trb[:, 0:512], [tok] + B['ktok'].wdeps() + B['merged'].wdeps())
                bbuf['dc'].read(tok2)
                B['ktok'].wrote(tok2)
            if not state_only:
                tok = mm(banks['sb'][:], [(sel(h), g_F[:])], B['g_F'].rdeps() + bbuf['sb'].wdeps())
                bbuf['sb'].wrote(tok)
                B['g_F'].read(tok)
                tok2 = cp('dve', Fbc[:], banks['sb'][:], [tok] + B['Fbc'].wdeps())
                bbuf['sb'].read(tok2)
                B['Fbc'].wrote(tok2)
            smb = banks['sm']
            hb = banks['hh']
            pend = [None]

            def tail(cprev):
                csp = slice(cprev * 128, (cprev + 1) * 128)
                tokn = mm(smb[:, 256:384], [(ones_bf[:], hsq[:, ec, :]) for ec in range(4)],
                          B['hsq'].rdeps() + bbuf['sm'].wdeps())
                bbuf['sm'].wrote(tokn)
                B['hsq'].read(tokn)
                tok10 = act(rsh[:], smb[:, 256:384], AF.Sqrt, [tokn] + B['rsh'].wdeps(), bias=EPS, scale=1.0 / 512.0)
                bbuf['sm'].read(tok10)
                tok10 = recip(rsh[:], rsh[:], [tok10])
                B['rsh'].wrote(tok10)
                for ec in range(4):
                    tok11 = stt(Pbuf[:, 4 * h + ec, csp], ho[:, ec, :], vecs[:, VGM, 4 * h + ec:4 * h + ec + 1], rsh[:],
                                ALU.mult, ALU.mult, (B['Pbuf'].wdeps() + B['o_out'].wdeps() + B['ho'].rdeps()) if (ec == 0) else ())
                B['ho'].read(tok11)
                B['rsh'].read(tok11)
                B['Pbuf'].wrote(tok11)

            for c in range(4):
                cs = slice(c * 128, (c + 1) * 128)
                col = c * 4 + h
                dcol = h * 4 + c
                v_ = nxt('vw', 2)
                tokv = act(vw[v_][:], vtok[:, c, :], AF.Copy, B['vtok'].rdeps() + B['wkf'].rdeps() + B[f'vw{v_}'].wdeps(),
                           scale=wkf[:, col:col + 1])
                B['vtok'].read(tokv)
                B['wkf'].read(tokv)
                B[f'vw{v_}'].wrote(tokv)
                if not state_only:
                    tok3 = act(Cd[:], Cm[:, 4 * h:4 * h + 4, :], AF.Copy, B['Cm'].rdeps() + B['Cd'].wdeps() + B['decbc'].rdeps(),
                               scale=decbc[:, dcol:dcol + 1])
                    B['Cm'].read(tok3)
                    B['Cd'].wrote(tok3)
                    tok4 = ts('dve', nd[:], nst[:, 4 * h:4 * h + 4], decbc[:, dcol:dcol + 1], None, ALU.mult, None,
                              B['nst'].rdeps() + B['nd'].wdeps() + B['decbc'].rdeps())
                    B['nst'].read(tok4)
                    B['nd'].wrote(tok4)
                    for d in range(4):
                        tok5 = ts('pool', nrep[:, d, :], ones_bf[:], nd[:, d:d + 1], None, ALU.mult, None,
                                  [tok4] + B['nrep'].wdeps() + cdep)
                    B['nd'].read(tok5)
                    B['nrep'].wrote(tok5)
                    tok = mm(smb[:, 0:128], [(kT[:, d, cs], qT[:, d, cs]) for d in range(4)],
                             B['kT'].rdeps() + B['qT'].rdeps() + bbuf['sm'].wdeps())
                    bbuf['sm'].wrote(tok)
                    B['kT'].read(tok)
                    s = nxt('Sm', 2)
                    tok2 = stt(Sm[s][:], smb[:, 0:128], wkf[:, col:col + 1], maskf, ALU.mult, ALU.mult,
                               [tok] + B['wkf'].rdeps() + B[f'Sm{s}'].wdeps() + cdep)
                    bbuf['sm'].read(tok2)
                    B['wkf'].read(tok2)
                    B[f'Sm{s}'].wrote(tok2)
                for d in range(4):
                    bn = 'dc' if d % 2 == 0 else 'sa'
                    tok = mm(banks[bn][:], [(ktok[:, c, d * 128:(d + 1) * 128], vw[v_][:])],
                             [tokv] + B['ktok'].rdeps() + bbuf[bn].wdeps())
                    bbuf[bn].wrote(tok)
                    tokc = stt(Cm[:, 4 * h + d, :], Cm[:, 4 * h + d, :], decbc[:, dcol:dcol + 1], banks[bn][:], ALU.mult, ALU.add,
                               [tok] + B['Cm'].wdeps() + B['decbc'].rdeps())
                    bbuf[bn].read(tokc)
                    B['Cm'].wrote(tokc)
                B[f'vw{v_}'].read(tok)
                for d in range(4):
                    tok = mm(smb[:, 416 + d:417 + d], [(ktok[:, c, d * 128:(d + 1) * 128], wkb[:, col:col + 1])],
                             (B['wkb'].rdeps() + bbuf['sm'].wdeps()) if d == 0 else (), sig=(d == 3))
                bbuf['sm'].wrote(tok)
                B['ktok'].read(tok)
                B['wkb'].read(tok)
                tokn2 = stt(nst[:, 4 * h:4 * h + 4], nst[:, 4 * h:4 * h + 4], decbc[:, dcol:dcol + 1], smb[:, 416:420],
                            ALU.mult, ALU.add, [tok] + B['nst'].wdeps())
                bbuf['sm'].read(tokn2)
                B['decbc'].read(tokn2)
                B['nst'].wrote(tokn2)
                if state_only:
                    continue
                for ec in range(4):
                    es_ = slice(ec * 128, (ec + 1) * 128)
                    pairs = [(vtok[:, c, es_], Sm[s][:])] + [(Cd[:, d, es_], qT[:, d, cs]) for d in range(4)]
                    tokh = mm(hb[:, es_], pairs, ([tok2, tok3] + B['vtok'].rdeps() + bbuf['hh'].wdeps()) if ec == 0 else (),
                              sig=(ec == 3))
                bbuf['hh'].wrote(tokh)
                B['Cd'].read(tokh)
                B['vtok'].read(tokh)
                pairs = [(ones_bf[:], Sm[s][:])] + [(nrep[:, d, :], qT[:, d, cs]) for d in range(4)]
                tokd = mm(smb[:, 128:256], pairs, [tok5] + bbuf['sm'].wdeps())
                bbuf['sm'].wrote(tokd)
                B['nrep'].read(tokd)
                B[f'Sm{s}'].read(tokd)
                B['qT'].read(tokd)
                if pend[0] is not None:
                    tail(pend[0])
                tok6 = act(dmt[:], smb[:, 128:256], AF.Abs, [tokd] + B['rden'].wdeps())
                bbuf['sm'].read(tok6)
                tok6 = tt('dve', dmt[:], dmt[:], Fbc[:, cs], ALU.max, [tok6] + B['Fbc'].rdeps())
                B['Fbc'].read(tok6)
                tok6 = recip(rden[:], dmt[:], [tok6])
                B['rden'].wrote(tok6)
                for ec in range(4):
                    tok7 = tt('pool', prod[:, ec, :], oT[:, ec, cs], rden[:], ALU.mult,
                              [tok6] + B['oT'].rdeps() + B['prod'].wdeps())
                B['rden'].read(tok7)
                B['oT'].read(tok7)
                B['prod'].wrote(tok7)
                tok8 = tt('dve', ho[:].rearrange("p a b -> p (a b)"), hb[:], prod[:].rearrange("p a b -> p (a b)"), ALU.mult,
                          [tokh, tok7] + B['ho'].wdeps())
                bbuf['hh'].read(tok8)
                B['prod'].read(tok8)
                B['ho'].wrote(tok8)
                tok9 = act(hsq[:].rearrange("p a b -> p (a b)"), ho[:].rearrange("p a b -> p (a b)"), AF.Square,
                           [tok8] + B['hsq'].wdeps())
                B['hsq'].wrote(tok9)
                B['ho'].read(tok9)
                pend[0] = c
            if pend[0] is not None:
                tail(pend[0])

    def phase_b_fin():
        for j in range(8):
            sz, dz = w_get('z_ml', j)
            for cc in range(2):
                c = 2 * j + cc
                bn, tok = proj_chunk(sz, cc, hT_rhs, dz + B['hT'].rdeps())
                B['hT'].read(tok)
                if cc == 1:
                    w_release(sz, tok)
                z = nxt('zs', 2)
                tokz = act(zs[z][:], banks[bn][:], AF.Silu, [tok] + B[f'zs{z}'].wdeps())
                bbuf[bn].read(tokz)
                B[f'zs{z}'].wrote(tokz)
                tok = tt('pool', Pbuf[:, c, :], Pbuf[:, c, :], zs[z][:], ALU.mult, [tokz] + B['Pbuf'].wdeps())
                B[f'zs{z}'].read(tok)
                B['Pbuf'].wrote(tok)
                fill(5)
        out_and_gate('ml_out', 'g_m', lambda kc: Pbuf[:, kc, :], 'Pbuf', first=True, nfill=5)

    def phase_c():
        for h in range(4):
            d0 = B['qx'].wdeps()
            for jj in range(2):
                sq_, dq = w_get('q_xa', 2 * h + jj)
                for cc in range(2):
                    bn, tok = proj_chunk(sq_, cc, hT_rhs, dq + B['hT'].rdeps())
                    B['hT'].read(tok)
                    tok2 = act(qx[:, 2 * jj + cc, :], banks[bn][:], AF.Copy, [tok] + d0)
                    bbuf[bn].read(tok2)
                    B['qx'].wrote(tok2)
                w_release(sq_, tok)
            d0 = B['pT'].wdeps()
            for mc in range(2):
                bn = take_mm()
                tok = mm(banks[bn][:], [(kmT[:, 4 * h + d, mc * 128:(mc + 1) * 128], qx[:, d, :]) for d in range(4)],
                         B['qx'].rdeps() + B['kmT'].rdeps() + bbuf[bn].wdeps())
                bbuf[bn].wrote(tok)
                tok2 = act(pT[:, mc, :], banks[bn][:], AF.Exp, [tok] + d0, scale=XSCALE)
                bbuf[bn].read(tok2)
                B['pT'].wrote(tok2)
            B['qx'].read(tok)
            tok = mm(banks['sa'][:], [(ones_bf[:], pT[:, mc, :]) for mc in range(2)], B['pT'].rdeps() + bbuf['sa'].wdeps())
            bbuf['sa'].wrote(tok)
            tok2 = recip(rdx[:], banks['sa'][:], [tok] + B['rdx'].wdeps())
            bbuf['sa'].read(tok2)
            B['rdx'].wrote(tok2)
            for jj in range(2):
                sz, dz = w_get('z_xa', 2 * h + jj)
                for cc in range(2):
                    ec = 2 * jj + cc
                    c = 4 * h + ec
                    bn = take_mm()
                    tok = mm(banks[bn][:], [(vm[:, mc, c * 128:(c + 1) * 128], pT[:, mc, :]) for mc in range(2)],
                             B['pT'].rdeps() + B['vm'].rdeps() + bbuf[bn].wdeps())
                    bbuf[bn].wrote(tok)
                    B['pT'].read(tok)
                    bn2, tokz = proj_chunk(sz, cc, hT_rhs, dz + B['hT'].rdeps())
                    B['hT'].read(tokz)
                    z = nxt('zs', 2)
                    tokz2 = act(zs[z][:], banks[bn2][:], AF.Silu, [tokz] + B[f'zs{z}'].wdeps())
                    bbuf[bn2].read(tokz2)
                    B[f'zs{z}'].wrote(tokz2)
                    t = nxt('tmp', 3)
                    tok3 = tt('dve', t1[t][:], banks[bn][:], rdx[:], ALU.mult, [tok] + B['rdx'].rdeps() + B[f'tmp{t}'].wdeps())
                    bbuf[bn].read(tok3)
                    B['rdx'].read(tok3)
                    B[f'tmp{t}'].wrote(tok3)
                    tok4 = tt('pool', Pbuf[:, c, :], t1[t][:], zs[z][:], ALU.mult, [tok3, tokz2] + B['Pbuf'].wdeps())
                    B[f'tmp{t}'].read(tok4)
                    B[f'zs{z}'].read(tok4)
                    B['Pbuf'].wrote(tok4)
                    fill(5)
                w_release(sz, tokz)
        out_and_gate('xa_out', 'g_x', lambda kc: Pbuf[:, kc, :], 'Pbuf', first=False, nfill=5)

    def phase_o(t0):
        for j in range(8):
            so, do = w_get('w_out', j)
            for cc in range(2):
                c = 2 * j + cc
                bn, tok = proj_chunk(so, cc, lambda kc: merged[:, kc, :], do + B['merged'].rdeps())
                B['merged'].read(tok)
                if cc == 1:
                    w_release(so, tok)
                tok2 = act(o_out[:, c, :], banks[bn][:], AF.Copy, [tok] + B['o_out'].wdeps() + B['uext'].wdeps() + B['Pbuf'].wdeps())
                B['o_out'].wrote(tok2)
                q = nxt('sqb', 2)
                tok3 = act(sqb[q][:], banks[bn][:], AF.Square, [tok] + B[f'sqb{q}'].wdeps())
                bbuf[bn].read(tok3)
                B[f'sqb{q}'].wrote(tok3)
                tok4 = pg.op('pe', (lambda e, c=c, q=q: e.matmul(banks['sa'][:], ones_bf[:], sqb[q][:], start=(c == 0), stop=(c == 15))),
                             [tok3] + (bbuf['sa'].wdeps() if c == 0 else []))
                B[f'sqb{q}'].read(tok4)
        bbuf['sa'].wrote(tok4)
        tok = act(rstdx[:], banks['sa'][:], AF.Sqrt, bbuf['sa'].rdeps() + B['rstdx'].wdeps(), bias=EPS, scale=1.0 / D)
        bbuf['sa'].read(tok)
        tok = recip(rstdx[:], rstdx[:], [tok])
        B['rstdx'].wrote(tok)
        kcs = list(range(16))
        st, issue = x_loader(xT, t0)
        for _ in range(2):
            issue(kcs)
        for c in range(16):
            s = st['slots'][c]
            t = nxt('tmp', 3)
            tok = stt(t1[t][:], o_out[:, c, :], vecs[:, VGP, c:c + 1], rstdx[:], ALU.mult, ALU.mult,
                      B['o_out'].rdeps() + B['rstdx'].rdeps() + B[f'tmp{t}'].wdeps())
            B['o_out'].read(tok)
            B[f'tmp{t}'].wrote(tok)
            a = nxt('tmp', 3)
            an = f'tmp{a}'
            tok2 = tt('pool', acc[a][:], t1[t][:], xs[s][:], ALU.add, [tok] + B[f'xs{s}'].rdeps() + B[an].wdeps())
            B[f'tmp{t}'].read(tok2)
            B[f'xs{s}'].read(tok2)
            B[an].wrote(tok2)
            dst = outT[c * P:(c + 1) * P, t0:t0 + NT]
            tok3 = pg.dma('pool', f'st{a}', (lambda e, a=a, dst=dst: e.dma_start(out=dst, in_=acc[a][:])), [tok2])
            B[an].read(tok3)
            issue(kcs)
        B['rstdx'].read(tok)

    def mem_kv():
        rms_tile(memT, 0, VGMEM, Pbuf, B['Pbuf'], width=256)
        mrhs = lambda kc: Pbuf[:, kc, 0:256]
        for j in range(8):
            s_, d_ = w_get('mk', j)
            for cc in range(2):
                c = 2 * j + cc
                bn = take_mm()
                tok = mm(banks[bn][:, 0:256], [(WBUF(s_)[:, kc, cc * 128:(cc + 1) * 128], mrhs(kc)) for kc in range(16)],
                         d_ + B['Pbuf'].rdeps() + bbuf[bn].wdeps())
                bbuf[bn].wrote(tok)
                tok2 = act(kmT[:, c, :], banks[bn][:, 0:256], AF.Copy, [tok])
                bbuf[bn].read(tok2)
                B['kmT'].wrote(tok2)
            w_release(s_, tok)
        for j in range(8):
            s_, d_ = w_get('mv', j)
            for mc in range(2):
                bn = take_mm()
                tok = mm(banks[bn][:, 0:256], [(Pbuf[:, kc, mc * 128:(mc + 1) * 128], WBUF(s_)[:, kc, :]) for kc in range(16)],
                         d_ + B['Pbuf'].rdeps() + bbuf[bn].wdeps())
                bbuf[bn].wrote(tok)
                tok2 = act(vm[:, mc, j * 256:(j + 1) * 256], banks[bn][:, 0:256], AF.Copy, [tok])
                bbuf[bn].read(tok2)
                B['vm'].wrote(tok2)
            w_release(s_, tok)
            B['Pbuf'].read(tok)

    cast_some(4)
    for t in range(n_pre):
        last = (t == n_pre - 1)
        rms_tile(xTp, t * NT, VG, hT, B['hT'])
        gates_tile(True)
        cast_some(4 if last else 2)
        if last:
            phase_a(do_conv=False, mask_halo=True)
        phase_b(state_only=True, last_pre=last)
    cast_some(len(GROUPS))
    mem_kv()
    for t in range(n_main):
        rms_tile(xT, t * NT, VG, hT, B['hT'])
        gates_tile(False)
        phase_a(do_conv=True, mask_halo=False)
        phase_b(state_only=False, last_pre=False)
        phase_b_fin()
        phase_c()
        fill(10 ** 6)
        phase_a_fin()
        phase_o(t * NT)

    if dry:
        print("sbuf bytes remaining", nc.sbuf_bytes_remaining)
        es.close()
        return None, wrec

    if debug:
        alld = [(e_, pg.cnt[e_]) for e_ in ('pe', 'act', 'dve', 'pool') if pg.cnt[e_] > 0]
        dbg = {'Cm': (Cm, [P, 16 * 512], F32), 'nst': (nst, [P, 16], F32), 'decbc': (decbc, [P, 16], F32),
               'wkf': (wkf, [P, 16], F32), 'g_sm': (g_sm, [4, 32], F32), 'g_A': (g_li, [4, 512], F32),
               'g_L': (g_L, [4, 512], F32), 'g_F': (g_G, [4, 512], F32), 'g_wk': (g_e, [4, 512], F32),
               'R2': (R2, [P, 10240], BF16), 'hT': (hT, [P, 16 * 512], BF16), 'kmT': (kmT, [P, 16 * 256], BF16),
               'vm': (vm, [P, 2 * 2048], BF16)}
        for nm, (t_, shp, dt_) in dbg.items():
            d_ = nc.dram_tensor("dbg_" + nm, shp, dt_, kind="ExternalOutput").ap()
            src_ = t_[:]
            if len(src_.shape) == 3:
                src_ = src_.rearrange("p a b -> p (a b)")
            pg.dma('sp', 'cst', (lambda e, d_=d_, src_=src_: e.dma_start(out=d_, in_=src_)), alld)
        pg.op('pool', lambda e: e.nop(), [('cst', pg.dcnt['cst'])], sig=False)

    fin = [(f'st{a}', pg.dcnt.get(f'st{a}', 0)) for a in range(3) if pg.dcnt.get(f'st{a}', 0) > 0]
    pg.op('pool', lambda e: e.nop(), fin, sig=False)

    with nc.Block() as block:
        def run(ename, eng):
            waited = {}
            own = 0
            serial = ename in ('act', 'dve', 'pool')
            for (fn, deps, sig) in pg.q[ename]:
                deps = list(deps)
                if serial and own > 0 and not (len(sig) > 1 and sig[1] is True):
                    deps.append((ename, own))
                for (key, val) in deps:
                    if key == ename and not serial and sig[0] != 'dma':
                        continue
                    if waited.get(key, 0) >= val:
                        continue
                    eng.wait_ge(sems[key], val)
                    waited[key] = val
                ins = fn(eng)
                if sig[0] == 'eng':
                    ins.then_inc(sems[ename], 1)
                    own += 1
                elif sig[0] == 'dma':
                    ins.then_inc(sems[sig[1]], 16)

        @block.tensor
        def _(e):
            run('pe', e)

        @block.scalar
        def _(e):
            run('act', e)

        @block.vector
        def _(e):
            run('dve', e)

        @block.gpsimd
        def _(e):
            run('pool', e)

        @block.sync
        def _(e):
            run('sp', e)
    es.close()
    return nc, None


def make_program(n_pre, n_main, debug=False):
    _, wseq = build_program(n_pre, n_main, None)
    nc, _ = build_program(n_pre, n_main, wseq, debug=debug)
    return nc


def host_consts():
    cst = np.zeros((P, 384), np.float32)
    cst[:, 0:128] = np.eye(P, dtype=np.float32)
    jj, ii = np.meshgrid(np.arange(P), np.arange(P), indexing="ij")
    cst[:, 128:256] = (jj <= ii).astype(np.float32)
    cst[:, 256:384] = 1.0
    csm = np.zeros((4, 516), np.float32)
    for h in range(4):
        csm[h, h * 128:(h + 1) * 128] = 1.0
        csm[h, 512 + h] = 1.0
    return cst, csm


def chan(v):
    return np.ascontiguousarray(np.asarray(v, np.float32).reshape(16, P).T)


def make_in_maps(inputs, n_pre_tok, n_main_tok, cores):
    f = lambda a: np.ascontiguousarray(np.asarray(a, np.float32))
    x = inputs["x"]
    cst, csm = host_consts()
    vecs = np.stack([chan(inputs["g_pre"]), chan(inputs["b_dw"]), chan(inputs["g_ln"]), chan(inputs["b_ln"]),
                     chan(inputs["g_ml_head"]), chan(inputs["g_post"]), chan(inputs["g_mem"])], axis=1)
    wdw = np.ascontiguousarray(np.asarray(inputs["w_dw"], np.float32).T.reshape(16, P, 31).transpose(1, 0, 2))
    wqk = np.ascontiguousarray(np.asarray(inputs["w_qk_conv"], np.float32).T.reshape(32, P, 4).transpose(1, 0, 2))
    b_if = np.asarray(inputs["b_if"], np.float32)
    bif = np.ascontiguousarray(np.stack([b_if[0:4], b_if[4:8]], axis=1))
    w_in = f(inputs["w_in"])
    wif = np.ascontiguousarray(w_in[:, OFF['if']:OFF['if'] + 8].reshape(16, P, 8).transpose(1, 0, 2))
    shared = {"w_in": w_in, "w_mem_kv": f(inputs["w_mem_kv"]), "w_conv_out": f(inputs["w_conv_out"]),
              "w_ml_out": f(inputs["w_ml_out"]), "w_xa_out": f(inputs["w_xa_out"]), "w_out": f(inputs["w_out"]),
              "cst": cst, "csm": csm, "vecs": vecs.reshape(P, -1), "wdw": wdw.reshape(P, -1), "wqk": wqk.reshape(P, -1),
              "bif": bif, "wif": wif.reshape(P, -1)}
    maps = []
    for (b, s0, second) in cores:
        m = dict(shared)
        m["xT"] = np.ascontiguousarray(x[b, s0:s0 + n_main_tok].T)
        if second:
            m["xTp"] = np.ascontiguousarray(x[b, s0 - n_pre_tok:s0].T)
            mval = 1.0
        else:
            m["xTp"] = m["xT"][:, :n_pre_tok] if n_pre_tok <= n_main_tok else np.ascontiguousarray(x[b, 0:n_pre_tok].T)
            mval = 0.0
        m["memT"] = np.ascontiguousarray(np.asarray(inputs["mem"][b], np.float32).T)
        pmv = np.zeros((P, 2), np.float32)
        pmv[:, 0] = mval
        pmv[:, 1] = (mval - 1.0) * BIG
        m["pm"] = pmv
        maps.append(m)
    return maps


_NC_CACHE = {}


def kernel(**inputs):
    x = np.asarray(inputs["x"])
    Bb, S, _ = x.shape
    half = S // 2
    n_tiles = half // NT
    key = (n_tiles, n_tiles)
    if key not in _NC_CACHE:
        _NC_CACHE[key] = make_program(n_tiles, n_tiles)
    nc = _NC_CACHE[key]
    cores = []
    for b in range(Bb):
        cores.append((b, 0, False))
        cores.append((b, half, True))
    in_maps = make_in_maps(inputs, half, half, cores)
    res = run_bass_kernel_spmd(nc, in_maps, core_ids=list(range(len(cores))))
    out = np.empty((Bb, S, D), np.float32)
    for i, (b, s0, _) in enumerate(cores):
        out[b, s0:s0 + half] = res.results[i]["outT"].T
    return out
```

```python
import contextlib
import numpy as np
import concourse.bass as bass
import concourse.mybir as mybir
from concourse.alu_op_type import AluOpType as ALU
from concourse.bass_utils import run_bass_kernel_spmd

F32 = mybir.dt.float32
BF16 = mybir.dt.bfloat16
AF = mybir.ActivationFunctionType
P = 128
D = 2048
KC = 16
NT = 512
EPS = 1e-6
BIG = 1.0e4
KSCALE_LN = float(np.log(512.0 ** -0.5))
XSCALE = float(512.0 ** -0.5)

OFF = {'glu_a': 0, 'glu_b': 2048, 'z_conv': 4096, 'q': 6144, 'k': 8192, 'v': 10240, 'o': 12288,
       'z_ml': 14336, 'if': 16384, 'q_xa': 16392, 'z_xa': 18440, 'g_c': 20488, 'g_m': 22536, 'g_x': 24584}
GROUPS = ['k', 'v', 'mk', 'mv', 'glu_b', 'glu_a', 'q', 'o', 'z_conv', 'conv_out', 'g_c', 'z_ml', 'ml_out',
          'g_m', 'q_xa', 'z_xa', 'xa_out', 'g_x', 'w_out']
GSRC = {'mk': ('w_mem_kv', 0), 'mv': ('w_mem_kv', 2048), 'conv_out': ('w_conv_out', 0),
        'ml_out': ('w_ml_out', 0), 'xa_out': ('w_xa_out', 0), 'w_out': ('w_out', 0)}
for _g in GROUPS:
    if _g not in GSRC:
        GSRC[_g] = ('w_in', OFF[_g])
GIDX = {g: i for i, g in enumerate(GROUPS)}
NSLOT = 3
VG, VB, VGL, VBL, VGM, VGP, VGMEM = 0, 1, 2, 3, 4, 5, 6


class Buf:
    def __init__(self):
        self.w = {}
        self.r = {}

    def rdeps(self):
        return list(self.w.items())

    def wdeps(self):
        d = dict(self.r)
        for k, v in self.w.items():
            d[k] = max(d.get(k, 0), v)
        return list(d.items())

    def read(self, tok):
        if tok is not None:
            self.r[tok[0]] = max(self.r.get(tok[0], 0), tok[1])

    def wrote(self, tok):
        if tok is not None:
            self.w[tok[0]] = max(self.w.get(tok[0], 0), tok[1])


class Prog:
    ENG = ('pe', 'act', 'dve', 'pool', 'sp')

    def __init__(self, dry):
        self.dry = dry
        self.q = {e: [] for e in self.ENG}
        self.cnt = {e: 0 for e in self.ENG}
        self.dcnt = {}

    def op(self, e, fn, deps=(), sig=True, nosync=False):
        tok = None
        if sig:
            self.cnt[e] += 1
            tok = (e, self.cnt[e])
        if not self.dry:
            self.q[e].append((fn, tuple(d for d in deps if d is not None), ('eng', nosync) if sig else ('none', nosync)))
        return tok

    def dma(self, qe, sem, fn, deps=()):
        self.dcnt[sem] = self.dcnt.get(sem, 0) + 16
        tok = (sem, self.dcnt[sem])
        if not self.dry:
            self.q[qe].append((fn, tuple(d for d in deps if d is not None), ('dma', sem)))
        return tok


def build_program(n_pre, n_main, wseq=None, debug=False):
    dry = wseq is None
    T_pre, T_main = n_pre * NT, n_main * NT
    pg = Prog(dry)
    nc = bass.Bass("TRN2", target_bir_lowering=False)
    es = contextlib.ExitStack()

    def din(name, shape, dt=F32):
        return nc.dram_tensor(name, list(shape), dt, kind="ExternalInput").ap()

    xT = din("xT", [D, T_main])
    xTp = din("xTp", [D, T_pre])
    memT = din("memT", [D, 256])
    wsrc = {'w_in': din("w_in", [D, 26632]), 'w_mem_kv': din("w_mem_kv", [D, 4096]),
            'w_conv_out': din("w_conv_out", [D, D]), 'w_ml_out': din("w_ml_out", [D, D]),
            'w_xa_out': din("w_xa_out", [D, D]), 'w_out': din("w_out", [D, D])}
    d_cst = din("cst", [P, 384])
    d_csm = din("csm", [4, 516])
    d_vecs = din("vecs", [P, 7 * 16])
    d_wdw = din("wdw", [P, 16 * 31])
    d_wqk = din("wqk", [P, 32 * 4])
    d_bif = din("bif", [4, 2])
    d_pm = din("pm", [P, 2])
    d_wif = din("wif", [P, 16 * 8])
    outT = nc.dram_tensor("outT", [D, T_main], F32, kind="ExternalOutput").ap()
    WB = nc.dram_tensor("wbscr", [len(GROUPS) * 8, P, KC * 256], BF16, kind="Internal").ap()

    def sb(name, shape, dt):
        return es.enter_context(nc.sbuf_tensor("sb_" + name, list(shape), dt))

    def psb(name):
        return es.enter_context(nc.psum_tensor("ps_" + name, [P, 512], F32))

    cst = sb("cst", [P, 384], F32)
    csm = sb("csm", [4, 516], F32)
    ident_bf = sb("identb", [P, P], BF16)
    ones_bf = sb("onesb", [P, P], BF16)
    maskf = cst[:, 128:256]
    ones4 = sb("ones4", [4, 512], F32)
    vecs = sb("vecs", [P, 7, 16], F32)
    wdw = sb("wdw", [P, 16, 31], F32)
    wqk = sb("wqk", [P, 32, 4], F32)
    bif = sb("bif", [4, 2], F32)
    negbf = sb("negbf", [4, 1], F32)
    pm = sb("pm", [P, 2], F32)
    wif_f = sb("wif_f", [P, 16, 8], F32)
    wif = sb("wif", [P, 16, 8], BF16)
    hT = sb("hT", [P, 16, 512], BF16)
    wbuf = [sb(f"wbuf{i}", [P, 16, 256], BF16) for i in range(NSLOT)]
    R1 = sb("R1", [P, 16864], BF16)
    uext = R1[:, 0:8672].rearrange("p (c t) -> p c t", t=542)
    Pbuf = R1[:, 8672:16864].rearrange("p (c t) -> p c t", t=512)
    o_out = R1[:, 0:16384].bitcast(F32).rearrange("p (c t) -> p c t", t=512)
    R2 = sb("R2", [P, 10240], BF16)
    qT = R2[:, 0:2048].rearrange("p (c t) -> p c t", t=512)
    kT = R2[:, 2048:4096].rearrange("p (c t) -> p c t", t=512)
    ktok = R2[:, 4096:6144].rearrange("p (c t) -> p c t", t=512)
    vtok = R2[:, 6144:8192].rearrange("p (c t) -> p c t", t=512)
    oT = R2[:, 8192:10240].rearrange("p (c t) -> p c t", t=512)
    merged = R2[:, 0:8192].rearrange("p (c t) -> p c t", t=512)
    qx = oT
    _r2f = R2[:, 0:8192].bitcast(F32)
    xr = [_r2f[:, i * 512:(i + 1) * 512] for i in range(8)]
    Cm = sb("Cm", [P, 16, 512], F32)
    Cd = sb("Cd", [P, 4, 512], BF16)
    nst = sb("nst", [P, 16], F32)
    nd = sb("nd", [P, 4], F32)
    nrep = sb("nrep", [P, 4, 128], BF16)
    kmT = sb("kmT", [P, 16, 256], BF16)
    vm = sb("vm", [P, 2, 2048], BF16)
    uhalo = sb("uhalo", [P, 16, 30], BF16)
    qkhalo = sb("qkhalo", [P, 32, 4], BF16)
    xs = [sb(f"xs{i}", [P, 512], F32) for i in range(2)]
    sqb = [sb(f"sqb{i}", [P, 512], BF16) for i in range(2)]
    rstdx = sb("mf", [P, 512], F32)
    Fbc = rstdx
    rdx = rstdx
    sg = [sb(f"sg{i}", [P, 512], F32) for i in range(2)]
    tmpf = [sb(f"tmp{i}", [P, 512], F32) for i in range(3)]
    cacc = [sb(f"cacc{i}", [P, 512], F32) for i in range(1)]
    acc = tmpf
    t1 = tmpf
    qpre = [sb(f"qpre{i}", [P, 516], BF16) for i in range(2)]
    lnr = sb("lnr", [P, 512], F32)
    lnb = sb("lnb", [P, 512], F32)
    zs = [sb(f"zs{i}", [P, 512], BF16) for i in range(2)]
    yb = [sb(f"yb{i}", [P, 512], BF16) for i in range(2)]
    Sm = [sb(f"Sm{i}", [P, 128], BF16) for i in range(2)]
    dmt = sb("dmt", [P, 128], F32)
    rden = sb("rden", [P, 128], F32)
    prod = sb("prod", [P, 4, 128], F32)
    ho = sb("ho", [P, 4, 128], F32)
    hsq = sb("hsq", [P, 4, 128], BF16)
    rsh = sb("rsh", [P, 128], F32)
    vw = [sb(f"vw{i}", [P, 512], BF16) for i in range(2)]
    wkf = sb("wkf", [P, 16], F32)
    wkb = sb("wkb", [P, 16], BF16)
    decbc = sb("decbc", [P, 16], F32)
    g_li = sb("g_li", [4, 512], F32)
    g_e = sb("g_e", [4, 512], F32)
    g_L = sb("g_L", [4, 512], F32)
    g_G = sb("g_G", [4, 512], F32)
    g_wk = g_e
    g_F = g_G
    g_sm = sb("g_sm", [4, 32], F32)
    pT = sb("pT", [P, 2, 512], BF16)
    memx = sb("memx", [P, 16, 256], F32) if False else None

    banks = {n: psb(n) for n in ['m0', 'm1', 'm2', 'sa', 'sb', 'sm', 'hh', 'dc']}
    bbuf = {n: Buf() for n in banks}
    for _r in ('sm_s', 'sm_d', 'sm_n', 'sm_m'):
        bbuf[_r] = bbuf['sm']

    semnames = ['pe', 'act', 'dve', 'pool', 'cst', 'xs0', 'xs1', 'st0', 'st1', 'st2'] + [f'xr{i}' for i in range(8)] + \
               [f'w{i}' for i in range(NSLOT)] + [f'cg{i}' for i in range(len(GROUPS))]
    sems = {n: es.enter_context(nc.semaphore(n)) for n in semnames}

    def mm(out_ap, pairs, deps=(), sig=True):
        n = len(pairs)
        tok = None
        for i, (l, r) in enumerate(pairs):
            tok = pg.op('pe', (lambda e, l=l, r=r, i=i: e.matmul(out_ap, l, r, start=(i == 0), stop=(i == n - 1))),
                        deps if i == 0 else (), sig=(sig and i == n - 1))
        return tok

    def act(out, in_, func, deps=(), bias=None, scale=None):
        kw = {}
        if bias is not None:
            kw['bias'] = bias
        if scale is not None:
            kw['scale'] = scale
        return pg.op('act', lambda e: e.activation(out=out, in_=in_, func=func, **kw), deps)

    def tt(eng, out, a, b, op, deps=()):
        return pg.op(eng, lambda e: e.tensor_tensor(out=out, in0=a, in1=b, op=op), deps)

    def ts(eng, out, a, s1, s2, op0, op1, deps=()):
        if op1 is None:
            return pg.op(eng, lambda e: e.tensor_scalar(out=out, in0=a, scalar1=s1, scalar2=None, op0=op0), deps)
        return pg.op(eng, lambda e: e.tensor_scalar(out=out, in0=a, scalar1=s1, scalar2=s2, op0=op0, op1=op1), deps)

    def stt(out, in0, scalar, in1, op0, op1, deps=(), nosync=False):
        return pg.op('dve', lambda e: e.scalar_tensor_tensor(out=out, in0=in0, scalar=scalar, in1=in1, op0=op0, op1=op1), deps,
                     nosync=nosync)

    def cp(eng, out, in_, deps=()):
        return pg.op(eng, lambda e: e.tensor_copy(out=out, in_=in_), deps)

    def recip(out, in_, deps=()):
        return pg.op('dve', lambda e: e.reciprocal(out=out, in_=in_), deps)

    def scan(out, d0, d1, init, op0, op1, deps=()):
        return pg.op('dve', lambda e: e.tensor_tensor_scan(out=out, data0=d0, data1=d1, initial=init, op0=op0, op1=op1), deps)

    def memset(eng, ap, val, deps=()):
        return pg.op(eng, lambda e: e.memset(ap, val), deps)

    mm_names = ['m0', 'm1', 'm2']
    mm_i = [0]

    def take_mm():
        n = mm_names[mm_i[0] % 3]
        mm_i[0] += 1
        return n

    wrec = []
    wstate = {'next_load': 0, 'next_use': 0}
    slotbuf = [Buf() for _ in range(NSLOT)]
    released = {}
    castbuf = {g: Buf() for g in GROUPS}

    def WBUF(i):
        return wbuf[i % NSLOT]

    def w_try_issue():
        if dry:
            return
        while wstate['next_load'] < len(wseq):
            i = wstate['next_load']
            if i >= NSLOT and not released.get(i - NSLOT, False):
                return
            s = i % NSLOT
            g, j = wseq[i]
            src = WB[GIDX[g] * 8 + j].rearrange("p (k n) -> p k n", n=256)
            deps = slotbuf[s].wdeps() + castbuf[g].rdeps()
            assert castbuf[g].w, ("cast not issued before load", g)
            tok = pg.dma('sp', f'w{s}', (lambda e, s=s, src=src: e.dma_start(out=wbuf[s][:], in_=src)), deps)
            slotbuf[s].wrote(tok)
            wstate['next_load'] = i + 1

    def w_get(g, j):
        i = wstate['next_use']
        wstate['next_use'] = i + 1
        if dry:
            wrec.append((g, j))
            return i, []
        assert wseq[i] == (g, j), (i, wseq[i], g, j)
        w_try_issue()
        assert wstate['next_load'] > i, ("weight block not loadable (slot starvation)", i, g, j)
        return i, slotbuf[i % NSLOT].rdeps()

    def w_release(i, tok):
        if dry:
            return
        slotbuf[i % NSLOT].read(tok)
        released[i] = True
        w_try_issue()

    B_cst = Buf()
    loads = [(cst, d_cst), (csm, d_csm), (vecs, d_vecs.rearrange("p (a b) -> p a b", b=16)),
             (wdw, d_wdw.rearrange("p (a b) -> p a b", b=31)), (wqk, d_wqk.rearrange("p (a b) -> p a b", b=4)),
             (bif, d_bif), (pm, d_pm), (wif_f, d_wif.rearrange("p (a b) -> p a b", b=8))]
    for dst, src in loads:
        tok = pg.dma('sp', 'cst', (lambda e, dst=dst, src=src: e.dma_start(out=dst[:], in_=src)))
    B_cst.wrote(tok)
    cdep = B_cst.rdeps()
    tok = cp('dve', ident_bf[:], cst[:, 0:128], cdep)
    tok = cp('dve', ones_bf[:], cst[:, 256:384], cdep)
    tok = cp('dve', wif[:], wif_f[:], cdep)
    tok = ts('dve', negbf[:], bif[:, 1:2], -1.0, None, ALU.mult, None, cdep)
    tok = memset('dve', ones4[:], 1.0)
    tok = memset('dve', g_sm[:], 0.0)
    tok = memset('dve', uhalo[:], 0.0)
    tok = memset('dve', qkhalo[:], 0.0)
    tok = memset('dve', nst[:], 0.0)
    tok = memset('dve', Cm[:], 0.0)
    B_setup = Buf()
    B_setup.wrote(tok)
    sdep = B_setup.rdeps()
    sel = lambda h: csm[:, h * 128:(h + 1) * 128]
    i4 = csm[:, 512:516]

    def cast_group(g):
        sname, c0 = GSRC[g]
        src_t = wsrc[sname]
        for j in range(8):
            src = src_t[:, c0 + 256 * j: c0 + 256 * (j + 1)].rearrange("(k p) n -> p k n", p=P)
            dst = WB[GIDX[g] * 8 + j].rearrange("p (k n) -> p k n", n=256)
            tok = pg.dma('pool', f'cg{GIDX[g]}', (lambda e, dst=dst, src=src: e.dma_start(out=dst, in_=src)))
        castbuf[g].wrote(tok)

    cast_next = [0]

    def cast_some(n):
        for _ in range(n):
            if cast_next[0] < len(GROUPS):
                cast_group(GROUPS[cast_next[0]])
                cast_next[0] += 1

    B = {n: Buf() for n in ['hT', 'uext', 'Pbuf', 'qT', 'kT', 'ktok', 'vtok', 'oT', 'merged', 'Cm', 'Cd', 'nst', 'nd',
                            'nrep', 'kmT', 'vm', 'uhalo', 'qkhalo', 'rstdx', 'lnm', 'lnr', 'lnb', 'Fbc', 'dmt', 'rden',
                            'prod', 'ho', 'hsq', 'rsh', 'wkf', 'wkb', 'decbc', 'g_li', 'g_e', 'g_L', 'g_G', 'g_wk',
                            'g_F', 'g_sm', 'qx', 'pT', 'rdx', 'o_out']}
    for n, k in [('xs', 2), ('sqb', 2), ('sg', 2), ('qpre', 2), ('zs', 2), ('yb', 2),
                 ('Sm', 2), ('vw', 2)]:
        for i in range(k):
            B[f'{n}{i}'] = Buf()
    B['Fbc'] = B['rstdx']
    B['rdx'] = B['rstdx']
    B['g_wk'] = B['g_e']
    B['g_F'] = B['g_G']
    B['qx'] = B['oT']
    for i in range(3):
        B[f'tmp{i}'] = Buf()
    for i in range(8):
        B[f'xr{i}'] = Buf()

    def r2dep():
        d = B['merged'].wdeps()
        for i in range(8):
            d = d + B[f'xr{i}'].wdeps()
        return d

    def r2users():
        return B['qT'].wdeps() + B['kT'].wdeps() + B['ktok'].wdeps() + B['vtok'].wdeps() + B['merged'].wdeps()
    for i in range(2):
        B[f'cacc{i}'] = Buf()
    rot = {}

    def nxt(name, k):
        i = rot.get(name, 0)
        rot[name] = i + 1
        return i % k

    XS = {'xs': xs, 'xr': xr}

    def x_loader(src_dram, t0, width=NT, pool='xs'):
        state = {'issued': 0}
        nsl = len(XS[pool])

        def issue(kc_list):
            i = state['issued']
            if i >= len(kc_list):
                return
            s = nxt(pool, nsl)
            kc = kc_list[i]
            src = src_dram[kc * P:(kc + 1) * P, t0:t0 + width]
            bname = f'{pool}{s}'
            dst_ = XS[pool][s][:, 0:width]
            xdeps = B[bname].wdeps() + (r2users() if pool == 'xr' else [])
            tok = pg.dma('sp', bname, (lambda e, dst_=dst_, src=src: e.dma_start(out=dst_, in_=src)), xdeps)
            B[bname].wrote(tok)
            state.setdefault('slots', []).append(s)
            state['issued'] = i + 1
        return state, issue

    def rms_tile(src_dram, t0, gvec, dst, dstB, width=NT):
        kcs = list(range(16)) + list(range(16))
        st, issue = x_loader(src_dram, t0, width, pool='xr')
        for _ in range(8):
            issue(kcs)
        for i in range(16):
            s = st['slots'][i]
            q = nxt('sqb', 2)
            tok = act(sqb[q][:, 0:width], xr[s][:, 0:width], AF.Square, B[f'xr{s}'].rdeps() + B[f'sqb{q}'].wdeps())
            B[f'xr{s}'].read(tok)
            B[f'sqb{q}'].wrote(tok)
            deps = B[f'sqb{q}'].rdeps() + (bbuf['sa'].wdeps() if i == 0 else [])
            tok = pg.op('pe', (lambda e, q=q, i=i: e.matmul(banks['sa'][:, 0:width], ones_bf[:], sqb[q][:, 0:width],
                                                             start=(i == 0), stop=(i == 15))), deps)
            B[f'sqb{q}'].read(tok)
            issue(kcs)
        bbuf['sa'].wrote(tok)
        tok = act(rstdx[:, 0:width], banks['sa'][:, 0:width], AF.Sqrt, bbuf['sa'].rdeps() + B['rstdx'].wdeps(),
                  bias=EPS, scale=1.0 / D)
        bbuf['sa'].read(tok)
        tok = recip(rstdx[:, 0:width], rstdx[:, 0:width], [tok])
        B['rstdx'].wrote(tok)
        for i in range(16):
            s = st['slots'][16 + i]
            deps = B[f'xr{s}'].rdeps() + B['rstdx'].rdeps() + (dstB.wdeps() if i == 0 else [])
            tok = stt(dst[:, i, 0:width], xr[s][:, 0:width], vecs[:, gvec, i:i + 1], rstdx[:, 0:width], ALU.mult, ALU.mult, deps)
            B[f'xr{s}'].read(tok)
            issue(kcs)
        B['rstdx'].read(tok)
        dstB.wrote(tok)

    def gates_tile(prefix):
        gi = banks['hh'][0:4, 0:512]
        gf = banks['sb'][0:4, 0:512]
        tok = mm(gi, [(wif[:, kc, 0:4], hT[:, kc, :]) for kc in range(16)], B['hT'].rdeps() + bbuf['hh'].wdeps() + sdep)
        bbuf['hh'].wrote(tok)
        tok = mm(gf, [(wif[:, kc, 4:8], hT[:, kc, :]) for kc in range(16)], bbuf['sb'].wdeps())
        bbuf['sb'].wrote(tok)
        B['hT'].read(tok)
        tok = act(g_li[:], gi, AF.Identity, bbuf['hh'].rdeps() + B['g_li'].wdeps(), bias=bif[:, 0:1])
        bbuf['hh'].read(tok)
        B['g_li'].wrote(tok)
        tok = act(g_e[:], gf, AF.Exp, bbuf['sb'].rdeps() + B['g_e'].wdeps(), bias=negbf[:, 0:1], scale=-1.0)
        bbuf['sb'].read(tok)
        tok = act(g_e[:], g_e[:], AF.Ln, [tok], bias=1.0)
        B['g_e'].wrote(tok)
        if prefix:
            tok = ts('dve', g_li[:], g_li[:], pm[0:4, 0:1], pm[0:4, 1:2], ALU.mult, ALU.add, B['g_li'].rdeps())
            B['g_li'].wrote(tok)
            tok = ts('dve', g_e[:], g_e[:], pm[0:4, 0:1], None, ALU.mult, None, B['g_e'].rdeps())
            B['g_e'].wrote(tok)
        tok = cp('dve', g_sm[:, 12:13], g_sm[:, 1:2], B['g_sm'].wdeps())
        tok = scan(g_L[:], ones4[:], g_e[:], g_sm[:, 0:1], ALU.mult, ALU.add, B['g_e'].rdeps() + B['g_L'].wdeps())
        B['g_e'].read(tok)
        tok = cp('dve', g_sm[:, 0:1], g_L[:, 511:512])
        tok = tt('dve', g_li[:], g_li[:], g_L[:], ALU.add, B['g_li'].wdeps())
        tok = scan(g_G[:], g_li[:], g_li[:], g_sm[:, 1:2], ALU.max, ALU.max, B['g_G'].wdeps())
        tok = cp('dve', g_sm[:, 1:2], g_G[:, 511:512])
        gl = g_G[:].rearrange("p (c t) -> p c t", t=128)[:, :, 127]
        if prefix:
            for c_ in range(4):
                tok = ts('dve', g_sm[:, 4 + c_:5 + c_], g_G[:, 511:512], -1.0, None, ALU.mult, None)
        else:
            tok = ts('dve', g_sm[:, 4:8], gl, -1.0, None, ALU.mult, None)
        tok = ts('dve', g_sm[:, 8:12], g_sm[:, 4:8], KSCALE_LN, None, ALU.add, None)
        tok = cp('dve', g_sm[:, 13:16], gl[:, 0:3])
        tok = tt('dve', g_sm[:, 16:20], g_sm[:, 12:16], g_sm[:, 4:8], ALU.add)
        B['g_sm'].wrote(tok)
        B['g_li'].wrote(tok)
        B['g_L'].wrote(tok)
        B['g_G'].wrote(tok)
        dv = [tok]
        tok = act(g_sm[:, 20:24], g_sm[:, 16:20], AF.Exp, dv)
        for c in range(4):
            cs = slice(c * 128, (c + 1) * 128)
            tok = act(g_wk[:, cs], g_li[:, cs], AF.Exp, dv + (B['g_wk'].wdeps() if c == 0 else []), bias=g_sm[:, 8 + c:9 + c])
            tok = act(g_F[:, cs], g_L[:, cs], AF.Exp, dv + (B['g_F'].wdeps() if c == 0 else []), bias=g_sm[:, 4 + c:5 + c])
        B['g_wk'].wrote(tok)
        B['g_F'].wrote(tok)
        B['g_sm'].read(tok)
        B['g_sm'].wrote(tok)
        B['g_li'].read(tok)
        B['g_L'].read(tok)

    def gates_pe():
        smb = banks['sm']
        deps = B['g_wk'].rdeps() + bbuf['sm_m'].wdeps()
        for c in range(4):
            tok = mm(smb[:, 384 + 4 * c:388 + 4 * c], [(g_wk[:, c * 128:(c + 1) * 128], i4)], deps if c == 0 else ())
        for h in range(4):
            tok = mm(smb[:, 400 + 4 * h:404 + 4 * h], [(sel(h), g_sm[:, 20:24])], B['g_sm'].rdeps() if h == 0 else ())
        bbuf['sm_m'].wrote(tok)
        B['g_wk'].read(tok)
        B['g_sm'].read(tok)
        d = bbuf['sm_m'].rdeps()
        tok = cp('dve', wkf[:], smb[:, 384:400], d + B['wkf'].wdeps())
        B['wkf'].wrote(tok)
        tok = cp('dve', wkb[:], smb[:, 384:400], d + B['wkb'].wdeps())
        B['wkb'].wrote(tok)
        tok = cp('dve', decbc[:], smb[:, 400:416], d + B['decbc'].wdeps())
        B['decbc'].wrote(tok)
        bbuf['sm_m'].read(tok)

    def proj_chunk(slot, cc, rhs_fn, first_deps):
        bn = take_mm()
        tok = mm(banks[bn][:], [(WBUF(slot)[:, kc, cc * 128:(cc + 1) * 128], rhs_fn(kc)) for kc in range(16)],
                 first_deps + bbuf[bn].wdeps())
        bbuf[bn].wrote(tok)
        return bn, tok

    hT_rhs = lambda kc: hT[:, kc, :]

    def phase_a(do_conv, mask_halo):
        for c in range(16):
            tok = cp('pool', uext[:, c, 0:30], uhalo[:, c, :], (B['uext'].wdeps() + B['uhalo'].rdeps() + B['o_out'].wdeps()) if c == 0 else ())
        B['uext'].wrote(tok)
        B['uhalo'].read(tok)
        for j in range(8):
            sb_, db = w_get('glu_b', j)
            sgs = []
            for cc in range(2):
                bn, tok = proj_chunk(sb_, cc, hT_rhs, db + B['hT'].rdeps())
                s = nxt('sg', 2)
                tok2 = act(sg[s][:], banks[bn][:], AF.Sigmoid, [tok] + B[f'sg{s}'].wdeps())
                bbuf[bn].read(tok2)
                B[f'sg{s}'].wrote(tok2)
                sgs.append((s, tok2))
            w_release(sb_, tok)
            sa_, da = w_get('glu_a', j)
            for cc in range(2):
                c = 2 * j + cc
                s, tok2 = sgs[cc]
                bn2, tok = proj_chunk(sa_, cc, hT_rhs, da)
                B['hT'].read(tok)
                tok3 = tt('dve', uext[:, c, 30:542], banks[bn2][:], sg[s][:], ALU.mult, [tok, tok2] + B['uext'].wdeps())
                bbuf[bn2].read(tok3)
                B[f'sg{s}'].read(tok3)
                B['uext'].wrote(tok3)
            w_release(sa_, tok)
        for c in range(16):
            if mask_halo:
                tok = ts('pool', uhalo[:, c, :], uext[:, c, 512:542], pm[:, 0:1], None, ALU.mult, None,
                         B['uext'].rdeps() + B['uhalo'].wdeps())
            else:
                tok = cp('pool', uhalo[:, c, :], uext[:, c, 512:542], B['uext'].rdeps() + B['uhalo'].wdeps())
        B['uhalo'].wrote(tok)
        B['uext'].read(tok)
        if not do_conv:
            return
        filler['ops'] = list(conv_ops())
        filler['pos'] = 0

    filler = {'ops': [], 'pos': 0}

    def conv_ops():
        for c in range(16):
            st_ = {}

            def first(c=c, st_=st_):
                a = nxt("cacc", 1)
                st_['a'] = a
                an = f'cacc{a}'
                ts('dve', cacc[a][:], uext[:, c, 0:512], wdw[:, c, 0:1], None, ALU.mult, None,
                   B['uext'].rdeps() + B[an].wdeps())
            yield first
            for j in range(1, 31):
                def tap(c=c, j=j, st_=st_):
                    a = st_['a']
                    tok = stt(cacc[a][:], uext[:, c, j:j + 512], wdw[:, c, j:j + 1], cacc[a][:], ALU.mult, ALU.add, nosync=True)
                    if j == 30:
                        an = f'cacc{a}'
                        B[an].wrote(tok)
                        tok2 = act(uext[:, c, 30:542], cacc[a][:], AF.Identity,
                                   B[an].rdeps() + B['uhalo'].rdeps() + B['uext'].wdeps(), bias=vecs[:, VB, c:c + 1])
                        B[an].read(tok2)
                        B['uext'].wrote(tok2)
                yield tap

    def fill(n):
        while n > 0 and filler['pos'] < len(filler['ops']):
            filler['ops'][filler['pos']]()
            filler['pos'] += 1
            n -= 1

    def phase_a_fin():
        ub = lambda c: uext[:, c, 30:542]
        for c in range(16):
            q = nxt('sqb', 2)
            tok = act(sqb[q][:], ub(c), AF.Square, B['uext'].rdeps() + B[f'sqb{q}'].wdeps())
            B[f'sqb{q}'].wrote(tok)
            tok1 = pg.op('pe', (lambda e, c=c: e.matmul(banks['sa'][:], ones_bf[:], ub(c), start=(c == 0), stop=(c == 15))),
                         B['uext'].rdeps() + (bbuf['sa'].wdeps() if c == 0 else []))
            tok2 = pg.op('pe', (lambda e, c=c, q=q: e.matmul(banks['sb'][:], ones_bf[:], sqb[q][:], start=(c == 0), stop=(c == 15))),
                         [tok] + (bbuf['sb'].wdeps() if c == 0 else []))
            B[f'sqb{q}'].read(tok2)
        bbuf['sa'].wrote(tok1)
        bbuf['sb'].wrote(tok2)
        B['uext'].read(tok1)
        m_ = nxt('tmp', 3)
        lnm = tmpf[m_]
        d0 = B[f'tmp{m_}'].wdeps() + B['lnr'].wdeps() + B['lnb'].wdeps()
        tok = ts('dve', lnm[:], banks['sa'][:], 1.0 / D, None, ALU.mult, None, bbuf['sa'].rdeps() + d0)
        bbuf['sa'].read(tok)
        tok = tt('dve', lnb[:], lnm[:], lnm[:], ALU.mult)
        tok = stt(lnr[:], banks['sb'][:], 1.0 / D, lnb[:], ALU.mult, ALU.subtract, bbuf['sb'].rdeps())
        bbuf['sb'].read(tok)
        tok = act(lnr[:], lnr[:], AF.Sqrt, [tok], bias=EPS)
        tok = recip(lnr[:], lnr[:], [tok])
        tok = stt(lnb[:], lnm[:], -1.0, lnr[:], ALU.mult, ALU.mult)
        B[f'tmp{m_}'].wrote(tok)
        B[f'tmp{m_}'].read(tok)
        B['lnr'].wrote(tok)
        B['lnb'].wrote(tok)
        lnd = [tok]
        for j in range(8):
            sz, dz = w_get('z_conv', j)
            for cc in range(2):
                c = 2 * j + cc
                bn, tok = proj_chunk(sz, cc, hT_rhs, dz + B['hT'].rdeps())
                B['hT'].read(tok)
                if cc == 1:
                    w_release(sz, tok)
                z = nxt('zs', 2)
                tokz = act(zs[z][:], banks[bn][:], AF.Silu, [tok] + B[f'zs{z}'].wdeps())
                bbuf[bn].read(tokz)
                B[f'zs{z}'].wrote(tokz)
                t = nxt('tmp', 3)
                tok = tt('dve', t1[t][:], ub(c), lnr[:], ALU.mult, lnd + B['uext'].rdeps() + B[f'tmp{t}'].wdeps())
                tok = tt('dve', t1[t][:], t1[t][:], lnb[:], ALU.add)
                B[f'tmp{t}'].wrote(tok)
                y = nxt('yb', 2)
                tok = act(yb[y][:], t1[t][:], AF.Silu, [tok] + B[f'yb{y}'].wdeps(), bias=vecs[:, VBL, c:c + 1],
                          scale=vecs[:, VGL, c:c + 1])
                B[f'tmp{t}'].read(tok)
                B[f'yb{y}'].wrote(tok)
                tok = tt('pool', ub(c), yb[y][:], zs[z][:], ALU.mult, [tok, tokz] + B['uext'].wdeps())
                B[f'yb{y}'].read(tok)
                B[f'zs{z}'].read(tok)
                B['uext'].wrote(tok)
        B['lnr'].read(tok)
        B['lnb'].read(tok)
        out_and_gate('conv_out', 'g_c', lambda kc: uext[:, kc, 30:542], 'uext', first=False)

    def out_and_gate(wname, gname, rhs_fn, srcbuf, first, nfill=0):
        for j in range(8):
            sg_, dg = w_get(gname, j)
            sgs = []
            for cc in range(2):
                bn, tok = proj_chunk(sg_, cc, hT_rhs, dg + B['hT'].rdeps())
                B['hT'].read(tok)
                s = nxt('sg', 2)
                tok2 = act(sg[s][:], banks[bn][:], AF.Sigmoid, [tok] + B[f'sg{s}'].wdeps())
                bbuf[bn].read(tok2)
                B[f'sg{s}'].wrote(tok2)
                sgs.append((s, tok2))
            w_release(sg_, tok)
            so, do = w_get(wname, j)
            for cc in range(2):
                c = 2 * j + cc
                s, tok2 = sgs[cc]
                bn2, tok = proj_chunk(so, cc, rhs_fn, do + B[srcbuf].rdeps())
                B[srcbuf].read(tok)
                if cc == 1:
                    w_release(so, tok)
                if first:
                    tok3 = tt('dve', merged[:, c, :], banks[bn2][:], sg[s][:], ALU.mult, [tok, tok2] + B['merged'].wdeps()
                              + B['qT'].wdeps() + B['kT'].wdeps() + B['ktok'].wdeps() + B['vtok'].wdeps() + r2dep())
                    bbuf[bn2].read(tok3)
                    B[f'sg{s}'].read(tok3)
                    B['merged'].wrote(tok3)
                else:
                    t = nxt('tmp', 3)
                    tok3 = tt('dve', t1[t][:], banks[bn2][:], sg[s][:], ALU.mult, [tok, tok2] + B[f'tmp{t}'].wdeps())
                    bbuf[bn2].read(tok3)
                    B[f'sg{s}'].read(tok3)
                    B[f'tmp{t}'].wrote(tok3)
                    tok4 = tt('pool', merged[:, c, :], merged[:, c, :], t1[t][:], ALU.add, [tok3] + B['merged'].wdeps())
                    B[f'tmp{t}'].read(tok4)
                    B['merged'].wrote(tok4)
                fill(nfill)

    def qk_chunk(slot, cc, ch, dst, d_idx, dslot, mask_halo, only_halo):
        bn, tok = proj_chunk(slot, cc, hT_rhs, dslot + B['hT'].rdeps())
        B['hT'].read(tok)
        pq = nxt('qpre', 2)
        pn = f'qpre{pq}'
        tok1 = act(qpre[pq][:, 4:516], banks[bn][:], AF.Copy, [tok] + B[pn].wdeps())
        bbuf[bn].read(tok1)
        tokh = cp('pool', qpre[pq][:, 0:4], qkhalo[:, ch, :], B[pn].wdeps() + B['qkhalo'].rdeps())
        if mask_halo:
            tok2 = ts('pool', qkhalo[:, ch, :], qpre[pq][:, 512:516], pm[:, 0:1], None, ALU.mult, None, [tok1])
        else:
            tok2 = cp('pool', qkhalo[:, ch, :], qpre[pq][:, 512:516], [tok1])
        B['qkhalo'].wrote(tok2)
        B[pn].wrote(tok2)
        B[pn].wrote(tok1)
        if only_halo:
            B[pn].read(tok2)
            return tok
        a = nxt('tmp', 3)
        an = f'tmp{a}'
        tok3 = ts('dve', acc[a][:], qpre[pq][:, 1:513], wqk[:, ch, 0:1], None, ALU.mult, None, B[pn].rdeps() + B[an].wdeps())
        for j in range(1, 4):
            tok3 = stt(acc[a][:], qpre[pq][:, 1 + j:513 + j], wqk[:, ch, j:j + 1], acc[a][:], ALU.mult, ALU.add, nosync=True)
        B[pn].read(tok3)
        B[an].wrote(tok3)
        tok4 = act(dst[:, d_idx, :], acc[a][:], AF.Silu, [tok3])
        B[an].read(tok4)
        fill(2)
        return tok, tok4

    def phase_b(state_only, last_pre):
        gates_pe()
        for h in range(4):
            if (not state_only) or last_pre:
                d0 = B['qT'].wdeps() + r2dep()
                for jj in range(2):
                    sq_, dq = w_get('q', 2 * h + jj)
                    for cc in range(2):
                        r = qk_chunk(sq_, cc, 4 * h + 2 * jj + cc, qT, 2 * jj + cc, dq + d0, last_pre, state_only)
                        if state_only:
                            tokp = r
                        else:
                            tokp, tokw = r
                            B['qT'].wrote(tokw)
                    w_release(sq_, tokp)
            d0 = B['kT'].wdeps() + r2dep()
            for jj in range(2):
                sk_, dk = w_get('k', 2 * h + jj)
                for cc in range(2):
                    tokp, tokw = qk_chunk(sk_, cc, 16 + 4 * h + 2 * jj + cc, kT, 2 * jj + cc, dk + d0, last_pre, False)
                    B['kT'].wrote(tokw)
                w_release(sk_, tokp)
            sv0, dv0 = w_get('v', 2 * h)
            sv1, dv1 = w_get('v', 2 * h + 1)
            for tc in range(4):
                bn = take_mm()
                tsl = slice(tc * 128, (tc + 1) * 128)
                tok = mm(banks[bn][:, 0:256], [(hT[:, kc, tsl], WBUF(sv0)[:, kc, :]) for kc in range(16)],
                         dv0 + dv1 + B['hT'].rdeps() + bbuf[bn].wdeps(), sig=False)
                tok = mm(banks[bn][:, 256:512], [(hT[:, kc, tsl], WBUF(sv1)[:, kc, :]) for kc in range(16)])
                bbuf[bn].wrote(tok)
                B['hT'].read(tok)
                tok2 = act(vtok[:, tc, :], banks[bn][:], AF.Copy, [tok] + B['vtok'].wdeps() + r2dep())
                bbuf[bn].read(tok2)
                B['vtok'].wrote(tok2)
                fill(4)
            w_release(sv0, tok)
            w_release(sv1, tok)
            if not state_only:
                d0 = B['oT'].wdeps()
                for jj in range(2):
                    so_, do_ = w_get('o', 2 * h + jj)
                    for cc in range(2):
                        bn, tok = proj_chunk(so_, cc, hT_rhs, do_ + B['hT'].rdeps())
                        B['hT'].read(tok)
                        tok2 = act(oT[:, 2 * jj + cc, :], banks[bn][:], AF.Sigmoid, [tok] + d0)
                        bbuf[bn].read(tok2)
                        B['oT'].wrote(tok2)
                        fill(2)
                    w_release(so_, tok)
            for tc in range(4):
                trb = banks['dc'][:].bitcast(BF16)
                deps = B['kT'].rdeps() + bbuf['dc'].wdeps()
                for d in range(4):
                    tok = pg.op('pe', (lambda e, d=d, tc=tc, trb=trb: e.transpose(trb[:, d * 128:(d + 1) * 128],
                                                                                 kT[:, d, tc * 128:(tc + 1) * 128], ident_bf[:])),
                                deps if d == 0 else (), sig=(d == 3))
                bbuf['dc'].wrote(tok)
                B['kT'].read(tok)
                tok2 = cp('dve', ktok[:, tc, :], trb[:, 0:512], [tok] + B['ktok'].wdeps() + r2dep())
                bbuf['dc'].read(tok2)
                B['ktok'].wrote(tok2)
            if not state_only:
                tok = mm(banks['sb'][:], [(sel(h), g_F[:])], B['g_F'].rdeps() + bbuf['sb'].wdeps())
                bbuf['sb'].wrote(tok)
                B['g_F'].read(tok)
                tok2 = cp('dve', Fbc[:], banks['sb'][:], [tok] + B['Fbc'].wdeps())
                bbuf['sb'].read(tok2)
                B['Fbc'].wrote(tok2)
            smb = banks['sm']
            hb = banks['hh']
            pend = [None]

            def tail(cprev):
                csp = slice(cprev * 128, (cprev + 1) * 128)
                tokn = mm(banks['m2'][:, 0:128], [(ones_bf[:], hsq[:, ec, :]) for ec in range(4)],
                          B['hsq'].rdeps() + bbuf['m2'].wdeps())
                bbuf['m2'].wrote(tokn)
                B['hsq'].read(tokn)
                tok10 = act(rsh[:], banks['m2'][:, 0:128], AF.Sqrt, [tokn] + B['rsh'].wdeps(), bias=EPS, scale=1.0 / 512.0)
                bbuf['m2'].read(tok10)
                tok10 = recip(rsh[:], rsh[:], [tok10])
                B['rsh'].wrote(tok10)
                for ec in range(4):
                    tok11 = stt(Pbuf[:, 4 * h + ec, csp], ho[:, ec, :], vecs[:, VGM, 4 * h + ec:4 * h + ec + 1], rsh[:],
                                ALU.mult, ALU.mult, (B['Pbuf'].wdeps() + B['o_out'].wdeps() + B['ho'].rdeps()) if (ec == 0) else ())
                B['ho'].read(tok11)
                B['rsh'].read(tok11)
                B['Pbuf'].wrote(tok11)

            if state_only:
                b4 = ['dc', 'sa', 'hh', 'sm']
                dcol = h * 4
                for tc in range(4):
                    col = tc * 4 + h
                    v_ = nxt('vw', 2)
                    tokv = act(vw[v_][:], vtok[:, tc, :], AF.Copy, B['vtok'].rdeps() + B['wkf'].rdeps() + B[f'vw{v_}'].wdeps(),
                               scale=wkf[:, col:col + 1])
                    B['vtok'].read(tokv)
                    B['wkf'].read(tokv)
                    B[f'vw{v_}'].wrote(tokv)
                    for d in range(4):
                        bn = b4[d]
                        tok = pg.op('pe', (lambda e, bn=bn, tc=tc, d=d, v_=v_: e.matmul(banks[bn][:], ktok[:, tc, d * 128:(d + 1) * 128],
                                                                                        vw[v_][:], start=(tc == 0), stop=(tc == 3))),
                                    [tokv] + B['ktok'].rdeps() + (bbuf[bn].wdeps() if tc == 0 else []))
                        if tc == 3:
                            bbuf[bn].wrote(tok)
                    B[f'vw{v_}'].read(tok)
                for d in range(4):
                    bn = b4[d]
                    tokc = stt(Cm[:, 4 * h + d, :], Cm[:, 4 * h + d, :], decbc[:, dcol:dcol + 1], banks[bn][:], ALU.mult, ALU.add,
                               bbuf[bn].rdeps() + B['Cm'].wdeps() + B['decbc'].rdeps())
                    bbuf[bn].read(tokc)
                    B['Cm'].wrote(tokc)
                for d in range(4):
                    for tc in range(4):
                        col = tc * 4 + h
                        tok = pg.op('pe', (lambda e, tc=tc, d=d, col=col: e.matmul(banks['sb'][:, d:d + 1], ktok[:, tc, d * 128:(d + 1) * 128],
                                                                                   wkb[:, col:col + 1], start=(tc == 0), stop=(tc == 3))),
                                    (B['wkb'].rdeps() + bbuf['sb'].wdeps()) if (d == 0 and tc == 0) else ())
                bbuf['sb'].wrote(tok)
                B['ktok'].read(tok)
                B['wkb'].read(tok)
                tokn2 = stt(nst[:, 4 * h:4 * h + 4], nst[:, 4 * h:4 * h + 4], decbc[:, dcol:dcol + 1], banks['sb'][:, 0:4],
                            ALU.mult, ALU.add, [tok] + B['nst'].wdeps())
                bbuf['sb'].read(tokn2)
                B['decbc'].read(tokn2)
                B['nst'].wrote(tokn2)
                continue
            for c in range(4):
                cs = slice(c * 128, (c + 1) * 128)
                col = c * 4 + h
                dcol = h * 4 + c
                v_ = nxt('vw', 2)
                tokv = act(vw[v_][:], vtok[:, c, :], AF.Copy, B['vtok'].rdeps() + B['wkf'].rdeps() + B[f'vw{v_}'].wdeps(),
                           scale=wkf[:, col:col + 1])
                B['vtok'].read(tokv)
                B['wkf'].read(tokv)
                B[f'vw{v_}'].wrote(tokv)
                if not state_only:
                    tok3 = act(Cd[:], Cm[:, 4 * h:4 * h + 4, :], AF.Copy, B['Cm'].rdeps() + B['Cd'].wdeps() + B['decbc'].rdeps(),
                               scale=decbc[:, dcol:dcol + 1])
                    B['Cm'].read(tok3)
                    B['Cd'].wrote(tok3)
                    tok4 = ts('dve', nd[:], nst[:, 4 * h:4 * h + 4], decbc[:, dcol:dcol + 1], None, ALU.mult, None,
                              B['nst'].rdeps() + B['nd'].wdeps() + B['decbc'].rdeps())
                    B['nst'].read(tok4)
                    B['nd'].wrote(tok4)
                    for d in range(4):
                        tok5 = ts('pool', nrep[:, d, :], ones_bf[:], nd[:, d:d + 1], None, ALU.mult, None,
                                  [tok4] + B['nrep'].wdeps() + cdep)
                    B['nd'].read(tok5)
                    B['nrep'].wrote(tok5)
                    tok = mm(banks['m0'][:, 0:128], [(kT[:, d, cs], qT[:, d, cs]) for d in range(4)],
                             B['kT'].rdeps() + B['qT'].rdeps() + bbuf['m0'].wdeps())
                    bbuf['m0'].wrote(tok)
                    B['kT'].read(tok)
                    s = nxt('Sm', 2)
                    tok2 = stt(Sm[s][:], banks['m0'][:, 0:128], wkf[:, col:col + 1], maskf, ALU.mult, ALU.mult,
                               [tok] + B['wkf'].rdeps() + B[f'Sm{s}'].wdeps() + cdep)
                    bbuf['m0'].read(tok2)
                    B['wkf'].read(tok2)
                    B[f'Sm{s}'].wrote(tok2)
                for d in range(4):
                    bn = 'dc' if d % 2 == 0 else 'sa'
                    tok = mm(banks[bn][:], [(ktok[:, c, d * 128:(d + 1) * 128], vw[v_][:])],
                             [tokv] + B['ktok'].rdeps() + bbuf[bn].wdeps())
                    bbuf[bn].wrote(tok)
                    tokc = stt(Cm[:, 4 * h + d, :], Cm[:, 4 * h + d, :], decbc[:, dcol:dcol + 1], banks[bn][:], ALU.mult, ALU.add,
                               [tok] + B['Cm'].wdeps() + B['decbc'].rdeps())
                    bbuf[bn].read(tokc)
                    B['Cm'].wrote(tokc)
                B[f'vw{v_}'].read(tok)
                for d in range(4):
                    tok = mm(banks['sb'][:, d:d + 1], [(ktok[:, c, d * 128:(d + 1) * 128], wkb[:, col:col + 1])],
                             (B['wkb'].rdeps() + bbuf['sb'].wdeps()) if d == 0 else (), sig=(d == 3))
                bbuf['sb'].wrote(tok)
                B['ktok'].read(tok)
                B['wkb'].read(tok)
                tokn2 = stt(nst[:, 4 * h:4 * h + 4], nst[:, 4 * h:4 * h + 4], decbc[:, dcol:dcol + 1], banks['sb'][:, 0:4],
                            ALU.mult, ALU.add, [tok] + B['nst'].wdeps())
                bbuf['sb'].read(tokn2)
                B['decbc'].read(tokn2)
                B['nst'].wrote(tokn2)
                if state_only:
                    continue
                for ec in range(4):
                    es_ = slice(ec * 128, (ec + 1) * 128)
                    pairs = [(vtok[:, c, es_], Sm[s][:])] + [(Cd[:, d, es_], qT[:, d, cs]) for d in range(4)]
                    tokh = mm(hb[:, es_], pairs, ([tok2, tok3] + B['vtok'].rdeps() + bbuf['hh'].wdeps()) if ec == 0 else (),
                              sig=(ec == 3))
                bbuf['hh'].wrote(tokh)
                B['Cd'].read(tokh)
                B['vtok'].read(tokh)
                pairs = [(ones_bf[:], Sm[s][:])] + [(nrep[:, d, :], qT[:, d, cs]) for d in range(4)]
                tokd = mm(banks['m1'][:, 0:128], pairs, [tok5] + bbuf['m1'].wdeps())
                bbuf['m1'].wrote(tokd)
                B['nrep'].read(tokd)
                B[f'Sm{s}'].read(tokd)
                B['qT'].read(tokd)
                if pend[0] is not None:
                    tail(pend[0])
                tok6 = act(dmt[:], banks['m1'][:, 0:128], AF.Abs, [tokd] + B['rden'].wdeps())
                bbuf['m1'].read(tok6)
                tok6 = tt('dve', dmt[:], dmt[:], Fbc[:, cs], ALU.max, [tok6] + B['Fbc'].rdeps())
                B['Fbc'].read(tok6)
                tok6 = recip(rden[:], dmt[:], [tok6])
                B['rden'].wrote(tok6)
                for ec in range(4):
                    tok7 = tt('pool', prod[:, ec, :], oT[:, ec, cs], rden[:], ALU.mult,
                              [tok6] + B['oT'].rdeps() + B['prod'].wdeps())
                B['rden'].read(tok7)
                B['oT'].read(tok7)
                B['prod'].wrote(tok7)
                tok8 = tt('dve', ho[:].rearrange("p a b -> p (a b)"), hb[:], prod[:].rearrange("p a b -> p (a b)"), ALU.mult,
                          [tokh, tok7] + B['ho'].wdeps())
                bbuf['hh'].read(tok8)
                B['prod'].read(tok8)
                B['ho'].wrote(tok8)
                tok9 = act(hsq[:].rearrange("p a b -> p (a b)"), ho[:].rearrange("p a b -> p (a b)"), AF.Square,
                           [tok8] + B['hsq'].wdeps())
                B['hsq'].wrote(tok9)
                B['ho'].read(tok9)
                pend[0] = c
            if pend[0] is not None:
                tail(pend[0])

    def phase_b_fin():
        for j in range(8):
            sz, dz = w_get('z_ml', j)
            for cc in range(2):
                c = 2 * j + cc
                bn, tok = proj_chunk(sz, cc, hT_rhs, dz + B['hT'].rdeps())
                B['hT'].read(tok)
                if cc == 1:
                    w_release(sz, tok)
                z = nxt('zs', 2)
                tokz = act(zs[z][:], banks[bn][:], AF.Silu, [tok] + B[f'zs{z}'].wdeps())
                bbuf[bn].read(tokz)
                B[f'zs{z}'].wrote(tokz)
                tok = tt('pool', Pbuf[:, c, :], Pbuf[:, c, :], zs[z][:], ALU.mult, [tokz] + B['Pbuf'].wdeps())
                B[f'zs{z}'].read(tok)
                B['Pbuf'].wrote(tok)
                fill(5)
        out_and_gate('ml_out', 'g_m', lambda kc: Pbuf[:, kc, :], 'Pbuf', first=True, nfill=5)

    def phase_c():
        for h in range(4):
            d0 = B['qx'].wdeps()
            for jj in range(2):
                sq_, dq = w_get('q_xa', 2 * h + jj)
                for cc in range(2):
                    bn, tok = proj_chunk(sq_, cc, hT_rhs, dq + B['hT'].rdeps())
                    B['hT'].read(tok)
                    tok2 = act(qx[:, 2 * jj + cc, :], banks[bn][:], AF.Copy, [tok] + d0)
                    bbuf[bn].read(tok2)
                    B['qx'].wrote(tok2)
                w_release(sq_, tok)
            d0 = B['pT'].wdeps()
            for mc in range(2):
                bn = take_mm()
                tok = mm(banks[bn][:], [(kmT[:, 4 * h + d, mc * 128:(mc + 1) * 128], qx[:, d, :]) for d in range(4)],
                         B['qx'].rdeps() + B['kmT'].rdeps() + bbuf[bn].wdeps())
                bbuf[bn].wrote(tok)
                tok2 = act(pT[:, mc, :], banks[bn][:], AF.Exp, [tok] + d0, scale=XSCALE)
                bbuf[bn].read(tok2)
                B['pT'].wrote(tok2)
            B['qx'].read(tok)
            tok = mm(banks['sa'][:], [(ones_bf[:], pT[:, mc, :]) for mc in range(2)], B['pT'].rdeps() + bbuf['sa'].wdeps())
            bbuf['sa'].wrote(tok)
            tok2 = recip(rdx[:], banks['sa'][:], [tok] + B['rdx'].wdeps())
            bbuf['sa'].read(tok2)
            B['rdx'].wrote(tok2)
            for jj in range(2):
                sz, dz = w_get('z_xa', 2 * h + jj)
                for cc in range(2):
                    ec = 2 * jj + cc
                    c = 4 * h + ec
                    bn = take_mm()
                    tok = mm(banks[bn][:], [(vm[:, mc, c * 128:(c + 1) * 128], pT[:, mc, :]) for mc in range(2)],
                             B['pT'].rdeps() + B['vm'].rdeps() + bbuf[bn].wdeps())
                    bbuf[bn].wrote(tok)
                    B['pT'].read(tok)
                    bn2, tokz = proj_chunk(sz, cc, hT_rhs, dz + B['hT'].rdeps())
                    B['hT'].read(tokz)
                    z = nxt('zs', 2)
                    tokz2 = act(zs[z][:], banks[bn2][:], AF.Silu, [tokz] + B[f'zs{z}'].wdeps())
                    bbuf[bn2].read(tokz2)
                    B[f'zs{z}'].wrote(tokz2)
                    t = nxt('tmp', 3)
                    tok3 = tt('dve', t1[t][:], banks[bn][:], rdx[:], ALU.mult, [tok] + B['rdx'].rdeps() + B[f'tmp{t}'].wdeps())
                    bbuf[bn].read(tok3)
                    B['rdx'].read(tok3)
                    B[f'tmp{t}'].wrote(tok3)
                    tok4 = tt('pool', Pbuf[:, c, :], t1[t][:], zs[z][:], ALU.mult, [tok3, tokz2] + B['Pbuf'].wdeps())
                    B[f'tmp{t}'].read(tok4)
                    B[f'zs{z}'].read(tok4)
                    B['Pbuf'].wrote(tok4)
                    fill(5)
                w_release(sz, tokz)
        out_and_gate('xa_out', 'g_x', lambda kc: Pbuf[:, kc, :], 'Pbuf', first=False, nfill=5)

    def phase_o(t0):
        for j in range(8):
            so, do = w_get('w_out', j)
            for cc in range(2):
                c = 2 * j + cc
                bn, tok = proj_chunk(so, cc, lambda kc: merged[:, kc, :], do + B['merged'].rdeps())
                B['merged'].read(tok)
                if cc == 1:
                    w_release(so, tok)
                tok2 = act(o_out[:, c, :], banks[bn][:], AF.Copy, [tok] + B['o_out'].wdeps() + B['uext'].wdeps() + B['Pbuf'].wdeps())
                B['o_out'].wrote(tok2)
                q = nxt('sqb', 2)
                tok3 = act(sqb[q][:], banks[bn][:], AF.Square, [tok] + B[f'sqb{q}'].wdeps())
                bbuf[bn].read(tok3)
                B[f'sqb{q}'].wrote(tok3)
                tok4 = pg.op('pe', (lambda e, c=c, q=q: e.matmul(banks['sa'][:], ones_bf[:], sqb[q][:], start=(c == 0), stop=(c == 15))),
                             [tok3] + (bbuf['sa'].wdeps() if c == 0 else []))
                B[f'sqb{q}'].read(tok4)
        bbuf['sa'].wrote(tok4)
        tok = act(rstdx[:], banks['sa'][:], AF.Sqrt, bbuf['sa'].rdeps() + B['rstdx'].wdeps(), bias=EPS, scale=1.0 / D)
        bbuf['sa'].read(tok)
        tok = recip(rstdx[:], rstdx[:], [tok])
        B['rstdx'].wrote(tok)
        kcs = list(range(16))
        st, issue = x_loader(xT, t0)
        for _ in range(2):
            issue(kcs)
        for c in range(16):
            s = st['slots'][c]
            t = nxt('tmp', 3)
            tok = stt(t1[t][:], o_out[:, c, :], vecs[:, VGP, c:c + 1], rstdx[:], ALU.mult, ALU.mult,
                      B['o_out'].rdeps() + B['rstdx'].rdeps() + B[f'tmp{t}'].wdeps())
            B['o_out'].read(tok)
            B[f'tmp{t}'].wrote(tok)
            a = nxt('tmp', 3)
            an = f'tmp{a}'
            tok2 = tt('pool', acc[a][:], t1[t][:], xs[s][:], ALU.add, [tok] + B[f'xs{s}'].rdeps() + B[an].wdeps())
            B[f'tmp{t}'].read(tok2)
            B[f'xs{s}'].read(tok2)
            B[an].wrote(tok2)
            dst = outT[c * P:(c + 1) * P, t0:t0 + NT]
            tok3 = pg.dma('pool', f'st{a}', (lambda e, a=a, dst=dst: e.dma_start(out=dst, in_=acc[a][:])), [tok2])
            B[an].read(tok3)
            issue(kcs)
        B['rstdx'].read(tok)

    def mem_kv():
        rms_tile(memT, 0, VGMEM, Pbuf, B['Pbuf'], width=256)
        mrhs = lambda kc: Pbuf[:, kc, 0:256]
        for j in range(8):
            s_, d_ = w_get('mk', j)
            for cc in range(2):
                c = 2 * j + cc
                bn = take_mm()
                tok = mm(banks[bn][:, 0:256], [(WBUF(s_)[:, kc, cc * 128:(cc + 1) * 128], mrhs(kc)) for kc in range(16)],
                         d_ + B['Pbuf'].rdeps() + bbuf[bn].wdeps())
                bbuf[bn].wrote(tok)
                tok2 = act(kmT[:, c, :], banks[bn][:, 0:256], AF.Copy, [tok])
                bbuf[bn].read(tok2)
                B['kmT'].wrote(tok2)
            w_release(s_, tok)
        for j in range(8):
            s_, d_ = w_get('mv', j)
            for mc in range(2):
                bn = take_mm()
                tok = mm(banks[bn][:, 0:256], [(Pbuf[:, kc, mc * 128:(mc + 1) * 128], WBUF(s_)[:, kc, :]) for kc in range(16)],
                         d_ + B['Pbuf'].rdeps() + bbuf[bn].wdeps())
                bbuf[bn].wrote(tok)
                tok2 = act(vm[:, mc, j * 256:(j + 1) * 256], banks[bn][:, 0:256], AF.Copy, [tok])
                bbuf[bn].read(tok2)
                B['vm'].wrote(tok2)
            w_release(s_, tok)
            B['Pbuf'].read(tok)

    cast_some(4)
    for t in range(n_pre):
        last = (t == n_pre - 1)
        rms_tile(xTp, t * NT, VG, hT, B['hT'])
        gates_tile(True)
        cast_some(4 if last else 2)
        if last:
            phase_a(do_conv=False, mask_halo=True)
        phase_b(state_only=True, last_pre=last)
    cast_some(len(GROUPS))
    mem_kv()
    for t in range(n_main):
        rms_tile(xT, t * NT, VG, hT, B['hT'])
        gates_tile(False)
        phase_a(do_conv=True, mask_halo=False)
        phase_b(state_only=False, last_pre=False)
        phase_b_fin()
        phase_c()
        fill(10 ** 6)
        phase_a_fin()
        phase_o(t * NT)

    if dry:
        print("sbuf bytes remaining", nc.sbuf_bytes_remaining)
        es.close()
        return None, wrec

    if debug:
        alld = [(e_, pg.cnt[e_]) for e_ in ('pe', 'act', 'dve', 'pool') if pg.cnt[e_] > 0]
        dbg = {'Cm': (Cm, [P, 16 * 512], F32), 'nst': (nst, [P, 16], F32), 'decbc': (decbc, [P, 16], F32),
               'wkf': (wkf, [P, 16], F32), 'g_sm': (g_sm, [4, 32], F32), 'g_A': (g_li, [4, 512], F32),
               'g_L': (g_L, [4, 512], F32), 'g_F': (g_G, [4, 512], F32), 'g_wk': (g_e, [4, 512], F32),
               'R2': (R2, [P, 10240], BF16), 'hT': (hT, [P, 16 * 512], BF16), 'kmT': (kmT, [P, 16 * 256], BF16),
               'vm': (vm, [P, 2 * 2048], BF16)}
        for nm, (t_, shp, dt_) in dbg.items():
            d_ = nc.dram_tensor("dbg_" + nm, shp, dt_, kind="ExternalOutput").ap()
            src_ = t_[:]
            if len(src_.shape) == 3:
                src_ = src_.rearrange("p a b -> p (a b)")
            pg.dma('sp', 'cst', (lambda e, d_=d_, src_=src_: e.dma_start(out=d_, in_=src_)), alld)
        pg.op('pool', lambda e: e.nop(), [('cst', pg.dcnt['cst'])], sig=False)

    fin = [(f'st{a}', pg.dcnt.get(f'st{a}', 0)) for a in range(3) if pg.dcnt.get(f'st{a}', 0) > 0]
    pg.op('pool', lambda e: e.nop(), fin, sig=False)

    with nc.Block() as block:
        def run(ename, eng):
            waited = {}
            own = 0
            serial = ename in ('act', 'dve', 'pool')
            for (fn, deps, sig) in pg.q[ename]:
                deps = list(deps)
                if serial and own > 0 and not (len(sig) > 1 and sig[1] is True):
                    deps.append((ename, own))
                for (key, val) in deps:
                    if key == ename and not serial and sig[0] != 'dma':
                        continue
                    if waited.get(key, 0) >= val:
                        continue
                    eng.wait_ge(sems[key], val)
                    waited[key] = val
                ins = fn(eng)
                if sig[0] == 'eng':
                    ins.then_inc(sems[ename], 1)
                    own += 1
                elif sig[0] == 'dma':
                    ins.then_inc(sems[sig[1]], 16)

        @block.tensor
        def _(e):
            run('pe', e)

        @block.scalar
        def _(e):
            run('act', e)

        @block.vector
        def _(e):
            run('dve', e)

        @block.gpsimd
        def _(e):
            run('pool', e)

        @block.sync
        def _(e):
            run('sp', e)
    es.close()
    return nc, None


def make_program(n_pre, n_main, debug=False):
    _, wseq = build_program(n_pre, n_main, None)
    nc, _ = build_program(n_pre, n_main, wseq, debug=debug)
    return nc


def host_consts():
    cst = np.zeros((P, 384), np.float32)
    cst[:, 0:128] = np.eye(P, dtype=np.float32)
    jj, ii = np.meshgrid(np.arange(P), np.arange(P), indexing="ij")
    cst[:, 128:256] = (jj <= ii).astype(np.float32)
    cst[:, 256:384] = 1.0
    csm = np.zeros((4, 516), np.float32)
    for h in range(4):
        csm[h, h * 128:(h + 1) * 128] = 1.0
        csm[h, 512 + h] = 1.0
    return cst, csm


def chan(v):
    return np.ascontiguousarray(np.asarray(v, np.float32).reshape(16, P).T)


def make_in_maps(inputs, n_pre_tok, n_main_tok, cores):
    f = lambda a: np.ascontiguousarray(np.asarray(a, np.float32))
    x = inputs["x"]
    cst, csm = host_consts()
    vecs = np.stack([chan(inputs["g_pre"]), chan(inputs["b_dw"]), chan(inputs["g_ln"]), chan(inputs["b_ln"]),
                     chan(inputs["g_ml_head"]), chan(inputs["g_post"]), chan(inputs["g_mem"])], axis=1)
    wdw = np.ascontiguousarray(np.asarray(inputs["w_dw"], np.float32).T.reshape(16, P, 31).transpose(1, 0, 2))
    wqk = np.ascontiguousarray(np.asarray(inputs["w_qk_conv"], np.float32).T.reshape(32, P, 4).transpose(1, 0, 2))
    b_if = np.asarray(inputs["b_if"], np.float32)
    bif = np.ascontiguousarray(np.stack([b_if[0:4], b_if[4:8]], axis=1))
    w_in = f(inputs["w_in"])
    wif = np.ascontiguousarray(w_in[:, OFF['if']:OFF['if'] + 8].reshape(16, P, 8).transpose(1, 0, 2))
    shared = {"w_in": w_in, "w_mem_kv": f(inputs["w_mem_kv"]), "w_conv_out": f(inputs["w_conv_out"]),
              "w_ml_out": f(inputs["w_ml_out"]), "w_xa_out": f(inputs["w_xa_out"]), "w_out": f(inputs["w_out"]),
              "cst": cst, "csm": csm, "vecs": vecs.reshape(P, -1), "wdw": wdw.reshape(P, -1), "wqk": wqk.reshape(P, -1),
              "bif": bif, "wif": wif.reshape(P, -1)}
    maps = []
    for (b, s0, second) in cores:
        m = dict(shared)
        m["xT"] = np.ascontiguousarray(x[b, s0:s0 + n_main_tok].T)
        if second:
            m["xTp"] = np.ascontiguousarray(x[b, s0 - n_pre_tok:s0].T)
            mval = 1.0
        else:
            m["xTp"] = m["xT"][:, :n_pre_tok] if n_pre_tok <= n_main_tok else np.ascontiguousarray(x[b, 0:n_pre_tok].T)
            mval = 0.0
        m["memT"] = np.ascontiguousarray(np.asarray(inputs["mem"][b], np.float32).T)
        pmv = np.zeros((P, 2), np.float32)
        pmv[:, 0] = mval
        pmv[:, 1] = (mval - 1.0) * BIG
        m["pm"] = pmv
        maps.append(m)
    return maps


_NC_CACHE = {}


def kernel(**inputs):
    x = np.asarray(inputs["x"])
    Bb, S, _ = x.shape
    half = S // 2
    n_tiles = half // NT
    key = (n_tiles, n_tiles)
    if key not in _NC_CACHE:
        _NC_CACHE[key] = make_program(n_tiles, n_tiles)
    nc = _NC_CACHE[key]
    cores = []
    for b in range(Bb):
        cores.append((b, 0, False))
        cores.append((b, half, True))
    in_maps = make_in_maps(inputs, half, half, cores)
    res = run_bass_kernel_spmd(nc, in_maps, core_ids=list(range(len(cores))))
    out = np.empty((Bb, S, D), np.float32)
    for i, (b, s0, _) in enumerate(cores):
        out[b, s0:s0 + half] = res.results[i]["outT"].T
    return out
```
